# Optimizing a Trainium2 kernel written in Bass

```python
import math
import jax, jax.numpy as jnp
from jax import lax
import numpy as np

D_MODEL = 1024
BATCH = 2
SEQ = 16384
DEPTH = 1
DEC_BATCH = 32
DEC_SEQ = 64
PAST_LEN = 1024

CHUNK = 64
Q_BLOCK = 128
N_HEADS = 8
QK_NOPE = 64
QK_ROPE = 32
QK_HEAD = QK_NOPE + QK_ROPE
V_HEAD = 64
Q_LORA = 256
KV_LORA = 128
ROPE_THETA = 10000.0
ATTN_SCALE = QK_HEAD ** -0.5
POOL_WIDTH = 512
POOL_GROUPS = 4
POOL_GROUP_DIM = POOL_WIDTH // POOL_GROUPS
POOL_WINDOWS = (2, 4, 8, 16)
POOL_STATE = max(POOL_WINDOWS) - 1
POOL_OUT_GROUP = D_MODEL // POOL_GROUPS
D_FF = ((8 * D_MODEL + 767) // 768) * 256
IN_TOTAL = Q_LORA + KV_LORA + QK_ROPE + POOL_WIDTH + 2 * D_MODEL
DEEPNORM_ALPHA = (2 * DEPTH) ** 0.25
DEEPNORM_BETA = (8 * DEPTH) ** -0.25
RMS_EPS = 1e-6
LN_EPS = 1e-5
NEG_INF = -1e30

kernel_name = "hybrid_mla_pool_streaming_step"


def _rms_norm(x, g):
    xf = x.astype(jnp.float32)
    y = xf * lax.rsqrt(jnp.mean(xf * xf, axis=-1, keepdims=True) + RMS_EPS)
    return (y * g.astype(jnp.float32)).astype(x.dtype)


def _layer_norm(x, g, b):
    xf = x.astype(jnp.float32)
    mu = jnp.mean(xf, axis=-1, keepdims=True)
    var = jnp.mean(jnp.square(xf - mu), axis=-1, keepdims=True)
    y = (xf - mu) * lax.rsqrt(var + LN_EPS)
    return (y * g.astype(jnp.float32) + b.astype(jnp.float32)).astype(x.dtype)


def _rope(x, pos):
    half = QK_ROPE // 2
    inv = 1.0 / (ROPE_THETA ** (jnp.arange(half, dtype=jnp.float32) / half))
    ang = pos.astype(jnp.float32)[:, None] * inv[None, :]
    shape = (pos.shape[0],) + (1,) * (x.ndim - 3) + (half,)
    cos = jnp.cos(ang).reshape(shape)
    sin = jnp.sin(ang).reshape(shape)
    xf = x.astype(jnp.float32)
    x1, x2 = xf[..., :half], xf[..., half:]
    return jnp.concatenate([x1 * cos - x2 * sin, x2 * cos + x1 * sin], axis=-1).astype(x.dtype)


def _mixer_inputs(x, pos, w_in, b_gate, q_norm_g, w_uq, kv_norm_g, w_uk):
    b, t, _ = x.shape
    z = x @ w_in
    o1 = Q_LORA
    o2 = o1 + KV_LORA
    o3 = o2 + QK_ROPE
    o4 = o3 + POOL_WIDTH
    z_q, z_kv, z_kr, u, z_g = z[..., :o1], z[..., o1:o2], z[..., o2:o3], z[..., o3:o4], z[..., o4:]
    q = (_rms_norm(z_q, q_norm_g) @ w_uq).reshape(b, t, N_HEADS, QK_HEAD)
    q_lat = jnp.einsum('bthn,rhn->bthr', q[..., :QK_NOPE], w_uk)
    q_rope = _rope(q[..., QK_NOPE:], pos)
    c_kv = _rms_norm(z_kv, kv_norm_g)
    k_rope = _rope(z_kr, pos)
    gates = jax.nn.sigmoid((z_g + b_gate).astype(jnp.float32)).astype(x.dtype)
    return q_lat, q_rope, c_kv, k_rope, u, gates[..., :D_MODEL], gates[..., D_MODEL:]


def _mla_attend(q_lat, q_rope, c_kv, k_rope, q_pos, k_pos):
    s = (jnp.einsum('bqhr,bkr->bhqk', q_lat, c_kv)
         + jnp.einsum('bqhe,bke->bhqk', q_rope, k_rope)).astype(jnp.float32) * ATTN_SCALE
    mask = (k_pos // CHUNK)[None, :] <= (q_pos // CHUNK)[:, None]
    s = jnp.where(mask[None, None], s, NEG_INF)
    p = jax.nn.softmax(s, axis=-1).astype(c_kv.dtype)
    return jnp.einsum('bhqk,bkr->bqhr', p, c_kv)


def _prompt_attention(q_lat, q_rope, c_kv, k_rope, pos):
    b, t, h, r = q_lat.shape
    nb = t // Q_BLOCK
    ql = q_lat.reshape(b, nb, Q_BLOCK, h, r).transpose(1, 0, 2, 3, 4)
    qr = q_rope.reshape(b, nb, Q_BLOCK, h, QK_ROPE).transpose(1, 0, 2, 3, 4)
    qp = pos.reshape(nb, Q_BLOCK)

    def one_block(args):
        ql_b, qr_b, qp_b = args
        return _mla_attend(ql_b, qr_b, c_kv, k_rope, qp_b, pos)

    out = lax.map(one_block, (ql, qr, qp))
    return out.transpose(1, 0, 2, 3, 4).reshape(b, t, h, r)


def _pool_branch(u, prefix, pos, w_pool, pool_scale):
    b, t, _ = u.shape
    ext = jnp.concatenate([prefix, u], axis=1).astype(jnp.float32)
    cs = jnp.concatenate([jnp.zeros((b, 1, POOL_WIDTH), jnp.float32), jnp.cumsum(ext, axis=1)], axis=1)
    end = cs[:, POOL_STATE + 1:POOL_STATE + 1 + t]
    outs = []
    for g, w in enumerate(POOL_WINDOWS):
        sl = slice(g * POOL_GROUP_DIM, (g + 1) * POOL_GROUP_DIM)
        start = cs[:, POOL_STATE + 1 - w:POOL_STATE + 1 - w + t, sl]
        cnt = jnp.minimum(pos + 1, w).astype(jnp.float32)[None, :, None]
        d = ((end[..., sl] - start) / cnt - ext[:, POOL_STATE:, sl]).astype(u.dtype)
        outs.append(d @ w_pool[g])
    return jnp.concatenate(outs, axis=-1) * pool_scale


def _finish_layer(x, att_lat, p_branch, g_attn, g_pool, w_uv, w_attn_up, w_o,
                  ln1_g, ln1_b, w_gate, w_up, w_down, ln2_g, ln2_b):
    b, t, _ = x.shape
    o = jnp.einsum('bthr,rhv->bthv', att_lat, w_uv).reshape(b, t, N_HEADS * V_HEAD)
    a_branch = o @ w_attn_up
    m = g_attn * a_branch + g_pool * p_branch
    h = _layer_norm(DEEPNORM_ALPHA * x + m @ w_o, ln1_g, ln1_b)
    f = (jax.nn.silu(h @ w_gate) * (h @ w_up)) @ w_down
    return _layer_norm(DEEPNORM_ALPHA * h + f, ln2_g, ln2_b)


def setup_inputs(seed: int = 0) -> dict:
    key = jax.random.key(seed)
    ks = jax.random.split(key, 24)
    f32 = jnp.float32
    nrm = lambda k, s, sc: jax.random.normal(k, s, f32) * sc
    L = DEPTH
    return {
        "x_prompt": nrm(ks[0], (BATCH, SEQ, D_MODEL), 1.0),
        "x_sample": nrm(ks[1], (DEC_BATCH, DEC_SEQ, D_MODEL), 1.0),
        "cache_kv_latent": nrm(ks[2], (L, DEC_BATCH, PAST_LEN, KV_LORA), 1.0),
        "cache_k_rope": nrm(ks[3], (L, DEC_BATCH, PAST_LEN, QK_ROPE), 1.0),
        "state_pool": nrm(ks[4], (L, DEC_BATCH, POOL_STATE, POOL_WIDTH), 1.0),
        "w_in": nrm(ks[5], (L, D_MODEL, IN_TOTAL), D_MODEL ** -0.5),
        "b_gate": nrm(ks[6], (L, 2 * D_MODEL), 0.02),
        "q_norm_g": 1.0 + nrm(ks[7], (L, Q_LORA), 0.02),
        "w_uq": nrm(ks[8], (L, Q_LORA, N_HEADS * QK_HEAD), Q_LORA ** -0.5),
        "kv_norm_g": 1.0 + nrm(ks[9], (L, KV_LORA), 0.02),
        "w_uk": nrm(ks[10], (L, KV_LORA, N_HEADS, QK_NOPE), KV_LORA ** -0.5),
        "w_uv": nrm(ks[11], (L, KV_LORA, N_HEADS, V_HEAD), KV_LORA ** -0.5 * DEEPNORM_BETA),
        "w_attn_up": nrm(ks[12], (L, N_HEADS * V_HEAD, D_MODEL), (N_HEADS * V_HEAD) ** -0.5 * DEEPNORM_BETA),
        "w_pool": nrm(ks[13], (L, POOL_GROUPS, POOL_GROUP_DIM, POOL_OUT_GROUP), POOL_GROUP_DIM ** -0.5 * DEEPNORM_BETA),
        "pool_scale": 1.0 + nrm(ks[14], (L, D_MODEL), 0.02),
        "w_o": nrm(ks[15], (L, D_MODEL, D_MODEL), D_MODEL ** -0.5 * DEEPNORM_BETA),
        "ln1_g": 1.0 + nrm(ks[16], (L, D_MODEL), 0.02),
        "ln1_b": nrm(ks[17], (L, D_MODEL), 0.02),
        "w_gate": nrm(ks[18], (L, D_MODEL, D_FF), D_MODEL ** -0.5),
        "w_up": nrm(ks[19], (L, D_MODEL, D_FF), D_MODEL ** -0.5 * DEEPNORM_BETA),
        "w_down": nrm(ks[20], (L, D_FF, D_MODEL), D_FF ** -0.5 * DEEPNORM_BETA),
        "ln2_g": 1.0 + nrm(ks[21], (L, D_MODEL), 0.02),
        "ln2_b": nrm(ks[22], (L, D_MODEL), 0.02),
    }


def reference(x_prompt, x_sample, cache_kv_latent, cache_k_rope, state_pool, w_in, b_gate,
              q_norm_g, w_uq, kv_norm_g, w_uk, w_uv, w_attn_up, w_pool, pool_scale, w_o,
              ln1_g, ln1_b, w_gate, w_up, w_down, ln2_g, ln2_b):
    n_p = x_prompt.shape[1]
    n_s = x_sample.shape[1]
    past = cache_kv_latent.shape[2]
    pos_p = jnp.arange(n_p, dtype=jnp.int32)
    pos_s = past + jnp.arange(n_s, dtype=jnp.int32)
    k_pos_s = jnp.arange(past + n_s, dtype=jnp.int32)
    h_p, h_s = x_prompt, x_sample
    c_p_list, kr_p_list, pool_p_list = [], [], []
    c_s_list, kr_s_list, pool_s_list = [], [], []
    for l in range(DEPTH):
        ql, qr, c, kr, u, ga, gp = _mixer_inputs(h_p, pos_p, w_in[l], b_gate[l], q_norm_g[l],
                                                 w_uq[l], kv_norm_g[l], w_uk[l])
        att = _prompt_attention(ql, qr, c, kr, pos_p)
        prefix = jnp.zeros((h_p.shape[0], POOL_STATE, POOL_WIDTH), u.dtype)
        pb = _pool_branch(u, prefix, pos_p, w_pool[l], pool_scale[l])
        c_p_list.append(c)
        kr_p_list.append(kr)
        pool_p_list.append(u[:, -POOL_STATE:])
        h_p = _finish_layer(h_p, att, pb, ga, gp, w_uv[l], w_attn_up[l], w_o[l], ln1_g[l], ln1_b[l],
                            w_gate[l], w_up[l], w_down[l], ln2_g[l], ln2_b[l])
        ql, qr, c, kr, u, ga, gp = _mixer_inputs(h_s, pos_s, w_in[l], b_gate[l], q_norm_g[l],
                                                 w_uq[l], kv_norm_g[l], w_uk[l])
        c_all = jnp.concatenate([cache_kv_latent[l], c], axis=1)
        kr_all = jnp.concatenate([cache_k_rope[l], kr], axis=1)
        att = _mla_attend(ql, qr, c_all, kr_all, pos_s, k_pos_s)
        pb = _pool_branch(u, state_pool[l], pos_s, w_pool[l], pool_scale[l])
        u_all = jnp.concatenate([state_pool[l], u], axis=1)
        c_s_list.append(c)
        kr_s_list.append(kr)
        pool_s_list.append(u_all[:, -POOL_STATE:])
        h_s = _finish_layer(h_s, att, pb, ga, gp, w_uv[l], w_attn_up[l], w_o[l], ln1_g[l], ln1_b[l],
                            w_gate[l], w_up[l], w_down[l], ln2_g[l], ln2_b[l])
    return (h_p, h_s, jnp.stack(c_p_list), jnp.stack(kr_p_list), jnp.stack(pool_p_list),
            jnp.stack(c_s_list), jnp.stack(kr_s_list), jnp.stack(pool_s_list))
```

```python
import math
import os
from contextlib import ExitStack
QSTOP = int(os.environ.get("QSTOP", "99"))
DSTOP = int(os.environ.get("DSTOP", "99"))

import numpy as np
import ml_dtypes

import concourse.bass as bass
import concourse.mybir as mybir
from concourse.bass_utils import run_bass_kernel_spmd

F32 = mybir.dt.float32
BF16 = mybir.dt.bfloat16
AF = mybir.ActivationFunctionType
ALU = mybir.AluOpType

D_MODEL = 1024
KC = 8
N_HEADS = 8
QK_NOPE = 64
QK_ROPE = 32
Q_LORA = 256
KV_LORA = 128
POOL_WIDTH = 512
D_FF = 2816
NFC = D_FF // 128
IN_TOTAL = 2976
ATTN_SCALE = 96 ** -0.5
ALPHA = 2 ** 0.25
RMS_EPS = 1e-6
LN_EPS = 1e-5
NEG_BIG = -30000.0
NCORES = 8
OW = 130


class _Op:
    __slots__ = ("stream", "fn", "deps", "dma", "signal", "pos", "dslot", "dval", "waits", "sigval")

    def __init__(self, stream, fn, deps, dma):
        self.stream = stream
        self.fn = fn
        self.deps = deps
        self.dma = dma
        self.signal = False
        self.pos = -1
        self.dslot = -1
        self.dval = 0
        self.waits = None
        self.sigval = 0


class Prog:
    STREAMS = ("pe", "act", "dve", "pool", "sp")
    R = 8

    def __init__(self, nc, es):
        self.nc = nc
        self.eng = {"pe": nc.tensor, "act": nc.scalar, "dve": nc.vector, "pool": nc.gpsimd, "sp": nc.sync}
        self.ops = []
        self.lastw = {}
        self.readers = {}
        self.sem = {s: es.enter_context(nc.semaphore("sem_" + s)) for s in self.STREAMS}
        self.dsem = {q: [es.enter_context(nc.semaphore("dq_%s_%d" % (q, i))) for i in range(self.R)]
                     for q in ("sp", "pool")}
        self.last_in_stream = {}
        self.dma_since_barrier = []

    def add(self, stream, fn, r=(), w=(), dma=False):
        idx = len(self.ops)
        deps = {}
        for k in r:
            p = self.lastw.get(k)
            if p is not None:
                deps[p] = True
        for k in w:
            p = self.lastw.get(k)
            if p is not None:
                deps[p] = True
            for q in self.readers.get(k, ()):
                if q not in deps:
                    deps[q] = False
        for k in r:
            self.readers.setdefault(k, []).append(idx)
        for k in w:
            self.lastw[k] = idx
            self.readers[k] = []
        self.ops.append(_Op(stream, fn, deps, dma))
        self.last_in_stream[stream] = idx
        if dma:
            self.dma_since_barrier.append(idx)
        return idx

    def barrier(self):
        deps = {i: True for i in self.last_in_stream.values()}
        for i in self.dma_since_barrier:
            deps[i] = True
        for s in self.STREAMS:
            self.ops.append(_Op(s, None, dict(deps), False))
        self.lastw = {}
        self.readers = {}
        self.dma_since_barrier = []
        self.last_in_stream = {}

    def finalize(self):
        ops = self.ops
        pos = {s: 0 for s in self.STREAMS}
        dcount = {"sp": 0, "pool": 0}
        dlist = {"sp": [], "pool": []}
        waited_pos = {s: {} for s in self.STREAMS}
        waited_dma = {s: {} for s in self.STREAMS}
        for idx, op in enumerate(ops):
            E = op.stream
            deps = op.deps
            if op.dma:
                j = dcount[E]
                if j >= self.R:
                    deps[dlist[E][j - self.R]] = True
                op.dslot = j % self.R
                op.dval = 16 * (j // self.R + 1)
                dcount[E] = j + 1
                dlist[E].append(idx)
            need = []
            for p in sorted(deps):
                P = ops[p]
                if P.dma:
                    key = (P.stream, P.dslot)
                    if waited_dma[E].get(key, 0) >= P.dval:
                        continue
                    waited_dma[E][key] = P.dval
                    need.append(("d", P.stream, P.dslot, P.dval))
                else:
                    if P.fn is None:
                        continue
                    Ep = P.stream
                    if Ep == E and E == "pe":
                        continue
                    if waited_pos[E].get(Ep, -1) >= P.pos:
                        continue
                    waited_pos[E][Ep] = P.pos
                    P.signal = True
                    need.append(("c", p))
            op.waits = need
            op.pos = pos[E]
            pos[E] += 1
        cnt = {s: 0 for s in self.STREAMS}
        for op in ops:
            if (not op.dma) and op.signal:
                cnt[op.stream] += 1
                op.sigval = cnt[op.stream]
        nwait = 0
        for op in ops:
            eng = self.eng[op.stream]
            for wt in op.waits:
                if wt[0] == "c":
                    P = ops[wt[1]]
                    eng.wait_ge(self.sem[P.stream], P.sigval)
                else:
                    eng.wait_ge(self.dsem[wt[1]][wt[2]], wt[3])
                nwait += 1
            if op.fn is None:
                continue
            ins = op.fn()
            if op.dma:
                ins.then_inc(self.dsem[op.stream][op.dslot], 16)
            elif op.signal:
                ins.then_inc(self.sem[op.stream], 1)
        self.stats = dict(cnt=dict(cnt), dma={q: len(v) for q, v in dlist.items()}, nops=len(ops), nwait=nwait)
        if os.environ.get("PROG_STATS"):
            print("PROG_STATS", self.stats, flush=True)
        return len(ops), nwait


class Arena:
    def __init__(self, nc, es, nbytes):
        self.t = es.enter_context(nc.sbuf_tensor("arena", [128, nbytes // 4], F32))
        self.size = nbytes
        self.off = 0

    def alloc(self, shape, dtype):
        n = 1
        for s in shape:
            n *= s
        nb = n * (2 if dtype == BF16 else 4)
        nb_al = (nb + 63) // 64 * 64
        assert self.off + nb_al <= self.size, ("arena overflow", self.off, nb_al, self.size)
        a = self.t[:, self.off // 4:(self.off + nb_al) // 4]
        self.off += nb_al
        if dtype == BF16:
            a = a.bitcast(BF16)
        a = a[:, 0:n]
        if len(shape) > 1:
            names = ["d%d" % i for i in range(len(shape))]
            pat = "p (" + " ".join(names) + ") -> p " + " ".join(names)
            a = a.rearrange(pat, **{nm: s for nm, s in zip(names[:-1], shape[:-1])})
        return a


def build_program(NG, STOP=99):
    SEQ = NG * 1024
    NKB = SEQ // 128
    NSTEP = NKB // 4
    nc = bass.Bass("TRN2", target_bir_lowering=False)

    def din(name, shape, dt=F32):
        return nc.dram_tensor(name, list(shape), dt, kind="ExternalInput").ap()

    def dout(name, shape, dt=F32):
        return nc.dram_tensor(name, list(shape), dt, kind="ExternalOutput").ap()

    def dint(name, shape, dt=BF16):
        return nc.dram_tensor(name, list(shape), dt, kind="Internal").ap()

    x_all = din("x_all", [2, SEQ, D_MODEL])
    x_own = din("x_own", [2, NG, 128, D_MODEL])
    x_halo = din("x_halo", [2, NG, 32, D_MODEL])
    x_s = din("x_s", [4, 64, D_MODEL])
    ckv_cache = din("ckv_cache", [4, 1024, 128])
    kr_cache = din("kr_cache", [4, 1024, 32])
    pool_state = din("pool_state", [4, 15, 512])
    w_in = din("w_in", [D_MODEL, IN_TOTAL])
    w_uq = din("w_uq", [Q_LORA, 768])
    w_ukT = din("w_ukT", [128, 8, 128])
    w_uvp = din("w_uvp", [128, 8, 128])
    w_attn_up = din("w_attn_up", [512, D_MODEL])
    w_pool = din("w_pool", [4, 128, 256])
    w_o = din("w_o", [D_MODEL, D_MODEL])
    w_gate = din("w_gate", [D_MODEL, D_FF])
    w_up = din("w_up", [D_MODEL, D_FF])
    w_down = din("w_down", [D_FF, D_MODEL])
    gq = din("gq", [1, Q_LORA])
    gkv = din("gkv", [1, KV_LORA])
    b_gateT = din("b_gateT", [128, 16])
    pool_scaleT = din("pool_scaleT", [128, 8])
    lnv = din("lnv", [4, D_MODEL])
    tabK = din("tabK", [2, 128, NKB, 32])
    tabO = din("tabO", [2, 128, NG, 32])
    tabS = din("tabS", [2, 64, 32])
    ident_d = din("ident", [128, 128], BF16)
    maskK_d = din("maskK", [128, 1024], BF16)
    maskQ_d = din("maskQ", [128, 512], BF16)
    rowmask_d = din("rowmask", [128, 4])
    band_d = din("band", [2, 128, 4, 128], BF16)
    bandH_d = din("bandH", [128, 4, 4, 128], BF16)

    y_own = dout("y_own", [2, NG, 128, D_MODEL])
    y_s = dout("y_s", [4, 64, D_MODEL])
    ckv_own = dout("ckv_own", [2, NG, 128, 128])
    kr_own = dout("kr_own", [2, NG, 128, 32])
    pool_p = dout("pool_p", [2, 64, 512])
    ckv_s = dout("ckv_s", [4, 64, 128])
    kr_s = dout("kr_s", [4, 64, 32])
    pool_s = dout("pool_s", [4, 32, 512])

    wb_in = dint("wb_in", [D_MODEL, IN_TOTAL])
    wb_attn_up = dint("wb_attn_up", [512, D_MODEL])
    wb_pool = dint("wb_pool", [4, 128, 256])
    wb_o = dint("wb_o", [D_MODEL, D_MODEL])
    wb_gate = dint("wb_gate", [D_MODEL, D_FF])
    wb_up = dint("wb_up", [D_MODEL, D_FF])
    wb_down = dint("wb_down", [D_FF, D_MODEL])

    es = ExitStack()
    with es:
        P = Prog(nc, es)
        ar = Arena(nc, es, 207 * 1024)
        ps = es.enter_context(nc.psum_tensor("ps", [128, 8, 512], F32))
        psf = ps[:, :, :]
        psb = psf.bitcast(BF16)

        def BK(j):
            return ("ps", j)

        ident = ar.alloc([128], BF16)
        w_in_a = ar.alloc([KC, 416], BF16)
        w_uq_sb = ar.alloc([2, 768], BF16)
        w_ukT_sb = ar.alloc([8, 128], BF16)
        gq_bc = ar.alloc([Q_LORA], F32)
        gkv_bc = ar.alloc([KV_LORA], F32)
        tabO_sb = ar.alloc([2, NG, 32], F32)
        tabS_sb = ar.alloc([2, 32], F32)
        maskK = ar.alloc([1024], BF16)
        maskQ = ar.alloc([512], BF16)
        mhalf = ar.alloc([8], F32)
        rowmask = ar.alloc([4], F32)
        small = ar.alloc([64], F32)
        junk = ar.alloc([1024], F32)
        b_gate_sb = ar.alloc([16], F32)
        pool_scale_sb = ar.alloc([8], F32)

        def dma(q, out, in_, r=(), w=()):
            eng = P.eng[q]
            return P.add(q, lambda: eng.dma_start(out=out, in_=in_), r=r, w=w, dma=True)

        RES = "res"
        dma("sp", ident, ident_d, w=[RES])
        dma("pool", w_in_a, w_in[:, 0:416].rearrange("(k p) n -> p k n", p=128), w=[RES])
        dma("pool", w_uq_sb, w_uq.rearrange("(k p) n -> p k n", p=128), w=[RES])
        dma("pool", w_ukT_sb, w_ukT, w=[RES])
        dma("sp", gq_bc, gq.to_broadcast([128, Q_LORA]), w=[RES])
        dma("sp", gkv_bc, gkv.to_broadcast([128, KV_LORA]), w=[RES])
        dma("sp", tabO_sb, tabO.rearrange("c p s j -> p c s j"), w=[RES])
        dma("sp", tabS_sb[0:64], tabS.rearrange("c p j -> p c j"), w=[RES])
        dma("sp", maskK, maskK_d, w=[RES])
        dma("sp", maskQ, maskQ_d, w=[RES])
        dma("sp", rowmask, rowmask_d, w=[RES])
        dma("sp", b_gate_sb, b_gateT, w=[RES])
        dma("sp", pool_scale_sb, pool_scaleT, w=[RES])
        P.add("dve", lambda: nc.vector.memset(mhalf, -0.5), w=[RES])

        def cast_w(dst, src, split):
            if split > 1:
                dst = dst.rearrange("r (a c) -> (r a) c", a=split)
                src = src.rearrange("r (a c) -> (r a) c", a=split)
            dma("pool", dst, src, w=["wscratch"])

        cast_list = [
            lambda: cast_w(wb_in, w_in, 2),
            lambda: cast_w(wb_attn_up, w_attn_up, 1),
            lambda: cast_w(wb_pool.rearrange("g a b -> (g a) b"), w_pool.rearrange("g a b -> (g a) b"), 1),
            lambda: cast_w(wb_o, w_o, 1),
            lambda: cast_w(wb_gate, w_gate, 2),
            lambda: cast_w(wb_up, w_up, 2),
            lambda: cast_w(wb_down, w_down, 1),
        ]

        stash_mark = ar.off

        def transposes(src_fn, n, nt, bank, dst, dst_key, src_keys, evac="dve"):
            def f():
                ins = None
                for i in range(n):
                    ins = nc.tensor.transpose(psb[:, bank, i * 128:(i + 1) * 128], src_fn(i), ident)
                return ins
            P.add("pe", f, r=list(src_keys) + [RES], w=[BK(bank)])
            src = psb[:, bank, 0:n * 128].rearrange("p (a t) -> p a t", a=n)[:, :, 0:nt]
            if evac == "act":
                P.add("act", lambda: nc.scalar.copy(out=dst, in_=src), r=[BK(bank)], w=[dst_key])
            else:
                P.add("dve", lambda: nc.vector.tensor_copy(out=dst, in_=src), r=[BK(bank)], w=[dst_key])

        def rstd_from_ss(ss, n_el, eps, ncol, nt, key):
            P.add("dve", lambda: nc.vector.tensor_scalar(out=ss, in0=ss, scalar1=1.0 / n_el, scalar2=eps,
                                                         op0=ALU.mult, op1=ALU.add), r=[key], w=[key])
            P.add("pool", lambda: nc.gpsimd.tensor_tensor(out=ss, in0=ss, in1=mhalf[0:nt, 0:ncol], op=ALU.pow),
                  r=[key, RES], w=[key])

        def rope_ops(zr, cos2, sin2, out, nt, r, w, tmpA, tmpB):
            P.add("dve", lambda: nc.vector.tensor_tensor(out=tmpA, in0=zr, in1=cos2, op=ALU.mult), r=r, w=["ropeA"])
            P.add("dve", lambda: nc.vector.tensor_tensor(out=tmpB[:, :, 0:16], in0=zr[:, :, 16:32], in1=sin2[:, :, 0:16],
                                                         op=ALU.mult), r=r, w=["ropeB0"])
            P.add("dve", lambda: nc.vector.tensor_tensor(out=tmpB[:, :, 16:32], in0=zr[:, :, 0:16], in1=sin2[:, :, 16:32],
                                                         op=ALU.mult), r=r, w=["ropeB1"])
            P.add("dve", lambda: nc.vector.tensor_tensor(out=out, in0=tmpA, in1=tmpB, op=ALU.add),
                  r=["ropeA", "ropeB0", "ropeB1"], w=w)

        att_ctr = {"t": 0}
        NSB = 3
        NPB = 4

        def attention(nq, groups, key_blocks, Pbuf, att_tm, rl, stash_dst, stash_key, qkeys, inter=None, inter_n=0):
            tiles = [(kb, g) for kb in key_blocks for g in groups]
            LAG = 2
            per_tile = 0
            if inter is not None:
                per_tile = min(3, max(1, -(-inter_n // max(1, len(tiles) - 4))))
            first_kb = key_blocks[0]
            last_kb = key_blocks[-1]
            started = set()
            info = []
            for i in range(len(tiles) + LAG):
                if i < len(tiles):
                    kb, g = tiles[i]
                    t = att_ctr["t"]
                    att_ctr["t"] += 1
                    sb = 3 + (t % NSB)
                    pb_i = t % NPB
                    nk = kb["nk"]
                    N = g["N"]
                    info.append((sb, pb_i))

                    def fS(kb=kb, g=g, sb=sb, nk=nk, N=N):
                        nc.tensor.matmul(psf[0:nk, sb, 0:N], lhsT=kb["ckvT"], rhs=g["lat"], start=True, stop=False)
                        last = kb["mask"] is None
                        ins = nc.tensor.matmul(psf[0:nk, sb, 0:N], lhsT=kb["krT"], rhs=g["rope"](kb["pb"]),
                                               start=False, stop=last)
                        if not last:
                            ins = nc.tensor.matmul(psf[0:nk, sb, 0:N], lhsT=kb["mask"], rhs=maskQ[:, 0:N],
                                                   start=False, stop=True)
                        return ins
                    P.add("pe", fS, r=list(kb["keys"]) + list(qkeys) + [RES], w=[BK(sb)])
                    P.add("act", lambda sb=sb, pb_i=pb_i, nk=nk, N=N: nc.scalar.activation(
                        out=Pbuf[0:nk, pb_i, 0:N], in_=psf[0:nk, sb, 0:N], func=AF.Exp, scale=ATTN_SCALE),
                        r=[BK(sb)], w=[("P", pb_i)])
                if i >= LAG:
                    kb, g = tiles[i - LAG]
                    sb, pb_i = info[i - LAG]
                    nk = kb["nk"]

                    def fPV(kb=kb, g=g, pb_i=pb_i, nk=nk):
                        ins = None
                        for (h, col) in g["heads"]:
                            bk = h // 3
                            st = (kb is first_kb) and (bk not in started)
                            if st:
                                started.add(bk)
                            c0 = (h % 3) * OW
                            ins = nc.tensor.matmul(psf[0:nq, bk, c0:c0 + 129], lhsT=Pbuf[:, pb_i, col:col + nq],
                                                   rhs=kb["V"], start=st, stop=(kb is last_kb),
                                                   skip_group_check=True)
                        return ins
                    P.add("pe", fPV, r=[("P", pb_i)] + list(kb["keys"]), w=[BK(0), BK(1), BK(2)])
                if inter is not None and i >= 2:
                    for _ in range(per_tile):
                        next(inter, None)
            for bk, (h0, nh) in enumerate([(0, 3), (3, 3), (6, 2)]):
                ov = psf[0:nq, bk, 0:nh * OW].rearrange("p (h c) -> p h c", c=OW)
                P.add("dve", lambda ov=ov, h0=h0, nh=nh: nc.vector.reciprocal(
                    out=rl[0:nq, h0:h0 + nh].unsqueeze(2), in_=ov[:, :, 128:129]), r=[BK(bk)], w=[("rl", bk)])
                P.add("dve", lambda ov=ov, h0=h0, nh=nh: nc.vector.tensor_tensor(
                    out=att_tm[0:nq, h0:h0 + nh, :], in0=ov[:, :, 0:128],
                    in1=rl[0:nq, h0:h0 + nh].unsqueeze(2).to_broadcast([nq, nh, 128]), op=ALU.mult),
                    r=[BK(bk), ("rl", bk)], w=[("att_tm", bk)])
            transposes(lambda h: att_tm[:, h, :], 8, nq, 7, stash_dst, stash_key,
                       [("att_tm", 0), ("att_tm", 1), ("att_tm", 2)])

        def q_stage(x_src, nt, cos2, sin2, ckv_dst, kr_dst, W, qb, sample_kv=None):
            xb = W["xb"]; xT = W["xT"]; QlatT = W["QlatT"][qb]; QropeT = W["QropeT"][qb]
            tag = ("q", qb)
            dma("pool", xb[0:nt], x_src, w=["q_xb"])
            transposes(lambda k: xb[:, k * 128:(k + 1) * 128], 8, nt, 7, xT[:, :, 0:nt], "q_xT", ["q_xb"])
            yield
            def fz():
                ins = None
                for k in range(KC):
                    ins = nc.tensor.matmul(psf[0:nt, 6, 0:416], lhsT=xT[:, k, 0:nt], rhs=w_in_a[:, k, :],
                                           start=(k == 0), stop=(k == KC - 1))
                return ins
            P.add("pe", fz, r=["q_xT", RES], w=[BK(6)])
            yield
            if QSTOP < 1:
                return
            ss = small[0:nt, 0:2]
            P.add("act", lambda: nc.scalar.activation(out=junk[0:nt, 0:256], in_=psf[0:nt, 6, 0:256], func=AF.Square,
                                                      accum_out=small[0:nt, 0:1]), r=[BK(6)], w=["q_ss0", "junk"])
            yield
            P.add("act", lambda: nc.scalar.activation(out=junk[0:nt, 0:128], in_=psf[0:nt, 6, 256:384], func=AF.Square,
                                                      accum_out=small[0:nt, 1:2]), r=[BK(6)], w=["q_ss1", "junk"])
            yield
            P.add("dve", lambda: nc.vector.tensor_scalar(out=small[0:nt, 0:1], in0=small[0:nt, 0:1], scalar1=1.0 / 256,
                                                         scalar2=RMS_EPS, op0=ALU.mult, op1=ALU.add),
                  r=["q_ss0"], w=["q_ss0"])
            yield
            P.add("dve", lambda: nc.vector.tensor_scalar(out=small[0:nt, 1:2], in0=small[0:nt, 1:2], scalar1=1.0 / 128,
                                                         scalar2=RMS_EPS, op0=ALU.mult, op1=ALU.add),
                  r=["q_ss1"], w=["q_ss1"])
            yield
            P.add("pool", lambda: nc.gpsimd.tensor_tensor(out=ss, in0=ss, in1=mhalf[0:nt, 0:2], op=ALU.pow),
                  r=["q_ss0", "q_ss1", RES], w=["q_ss0", "q_ss1"])
            yield
            if QSTOP < 2:
                return
            zqn = W["zqn"]
            P.add("dve", lambda: nc.vector.scalar_tensor_tensor(out=zqn[0:nt], in0=psf[0:nt, 6, 0:256], scalar=small[0:nt, 0:1],
                                                                in1=gq_bc[0:nt], op0=ALU.mult, op1=ALU.mult),
                  r=[BK(6), "q_ss0", RES], w=["q_zqn"])
            yield
            ckv32 = W["ckv32"]
            P.add("dve", lambda: nc.vector.scalar_tensor_tensor(out=ckv32[0:nt], in0=psf[0:nt, 6, 256:384], scalar=small[0:nt, 1:2],
                                                                in1=gkv_bc[0:nt], op0=ALU.mult, op1=ALU.mult),
                  r=[BK(6), "q_ss1", RES], w=["q_ckv32"])
            yield
            dma("sp", ckv_dst, ckv32[0:nt], r=["q_ckv32"])
            yield
            if QSTOP < 3:
                return
            kr32 = W["kr32"]
            rope_ops(psf[0:nt, 6, 384:416].unsqueeze(1), cos2.unsqueeze(1), sin2.unsqueeze(1), kr32[0:nt].unsqueeze(1), nt,
                     [BK(6), RES], ["q_kr32"], W["rA"][0:nt, 0:1, :], W["rB"][0:nt, 0:1, :])
            yield
            dma("sp", kr_dst, kr32[0:nt], r=["q_kr32"])
            yield
            if sample_kv is not None:
                sample_kv(ckv32, kr32)
            if QSTOP < 4:
                return
            zqnT = W["zqnT"]
            transposes(lambda k: zqn[:, k * 128:(k + 1) * 128], 2, nt, 7, zqnT[:, :, 0:nt], "q_zqnT", ["q_zqn"])
            yield
            qn = W["qn"]; qr = W["qr"]
            for half in range(2):
                def fq(half=half):
                    ins = None
                    for k in range(2):
                        ins = nc.tensor.matmul(psf[0:nt, 6, 0:384], lhsT=zqnT[:, k, 0:nt],
                                               rhs=w_uq_sb[:, k, half * 384:(half + 1) * 384], start=(k == 0), stop=(k == 1))
                    return ins
                P.add("pe", fq, r=["q_zqnT", RES], w=[BK(6)])
                yield
                qv = psf[0:nt, 6, 0:384].rearrange("p (h c) -> p h c", c=96)
                P.add("dve", lambda qv=qv, half=half: nc.vector.tensor_copy(out=qn[0:nt, half * 4:(half + 1) * 4, :], in_=qv[:, :, 0:64]),
                      r=[BK(6)], w=[("q_qn", half)])
                yield
                rope_ops(qv[:, :, 64:96], cos2.unsqueeze(1).to_broadcast([nt, 4, 32]), sin2.unsqueeze(1).to_broadcast([nt, 4, 32]),
                         W["qr32"][0:nt, half * 4:(half + 1) * 4, :], nt, [BK(6), RES], [("q_qr32", half)],
                         W["rA"][0:nt, 0:4, :], W["rB"][0:nt, 0:4, :])
            if QSTOP < 5:
                return
            P.add("dve", lambda: nc.vector.tensor_copy(out=qr[0:nt], in_=W["qr32"][0:nt].unsqueeze(2).to_broadcast([nt, 8, 4, 32])),
                  r=[("q_qr32", 0), ("q_qr32", 1)], w=["q_qr"])
            yield
            if QSTOP < 6:
                return
            qnT = W["qnT"]
            transposes(lambda j: qn[:, 2 * j:2 * j + 2, :].rearrange("p a b -> p (a b)"), 4, nt, 7, qnT[:, :, 0:nt], "q_qnT",
                       [("q_qn", 0), ("q_qn", 1)])
            yield
            if QSTOP < 7:
                return
            for half in range(2):
                def fl(half=half):
                    ins = None
                    for hh in range(4):
                        h = half * 4 + hh
                        pb = (h % 2) * 64
                        ins = nc.tensor.matmul(psf[:, 6, hh * nt:(hh + 1) * nt], lhsT=w_ukT_sb[:, h, :],
                                               rhs=qnT[:, h // 2, 0:nt], start=True, stop=True)
                    return ins
                P.add("pe", fl, r=["q_qnT", RES], w=[BK(6)])
                yield
                P.add("dve", lambda half=half: nc.vector.tensor_copy(out=QlatT[:, half * 4 * nt:(half + 1) * 4 * nt],
                                                                     in_=psf[:, 6, 0:4 * nt]),
                      r=[BK(6)], w=[(tag, "QlatT")])
            if QSTOP < 8:
                return
            def ftq():
                ins = None
                for h in range(8):
                    ins = nc.tensor.transpose(psb[:, 7, h * 128:(h + 1) * 128], qr[:, h, :, :].rearrange("p a b -> p (a b)"), ident)
                return ins
            P.add("pe", ftq, r=["q_qr", RES], w=[BK(7)])
            yield
            for m in range(4):
                P.add("dve", lambda m=m: nc.vector.tensor_scalar(
                    out=QropeT[:, m, 0:8 * nt].rearrange("p (h t) -> p h t", h=8),
                    in0=psb[:, 7, :].rearrange("p (h t) -> p h t", h=8)[:, :, 0:nt],
                    scalar1=rowmask[:, m:m + 1], scalar2=None, op0=ALU.mult),
                      r=[BK(7), RES], w=[(tag, "QropeT")])

        wctr = {"n": 0}

        def d_head(NB, nt, x_blk_src, xh_src, W, bank):
            xT = W["xT"]; xb = W["xb"]
            for i in range(NB):
                dma("pool", xb[0:nt, i % 2, :], x_blk_src(i), w=[("d_xb", i % 2)])
                transposes(lambda k, i=i: xb[:, i % 2, k * 128:(k + 1) * 128], 8, nt, bank, xT[:, :, i * nt:(i + 1) * nt],
                           ("d_xT", i), [("d_xb", i % 2)], evac="act")
            if xh_src is not None:
                xhb = W["xhb"]; xhT = W["xhT"]
                dma("pool", xhb, xh_src, w=["d_xhb"])
                transposes(lambda k: xhb[:, k * 128:(k + 1) * 128], 8, 128, bank, xhT, "d_xhT", ["d_xhb"], evac="act")

        def d_stage(NB, nt, x_blk_src, xh_src, attT, band_sel, y_dst, pool_dst, W, halo_direct=None, head_done=False,
                    next_head=None):
            T = NB * nt
            if DSTOP < -1:
                return
            xT = W["xT"]; xb = W["xb"]
            if not head_done:
                d_head(NB, nt, x_blk_src, xh_src, W, 7)
            xTk = [("d_xT", i) for i in range(NB)]
            if DSTOP < 0:
                return
            uh = W["uh"]
            wu = W["wA"][0]
            dma("sp", wu, wb_in[:, 416:928].rearrange("(k p) n -> p k n", p=128), w=[("d_wA", 0)])
            if xh_src is not None:
                xhT = W["xhT"]
                def fuh():
                    ins = None
                    for k in range(KC):
                        ins = nc.tensor.matmul(psf[:, 0, :], lhsT=xhT[:, k, :], rhs=wu[:, k, :], start=(k == 0), stop=(k == KC - 1))
                    return ins
                P.add("pe", fuh, r=["d_xhT", ("d_wA", 0)], w=[BK(0)])
                P.add("act", lambda: nc.scalar.copy(out=uh, in_=psf[:, 0, :]), r=[BK(0)], w=["d_uh"])
            else:
                halo_direct(uh)
            u_tm = W["u_tm"]; dT = W["dT"]; ulast = W["ulast"]
            for i in range(NB):
                bk = 1 + (i % 2)
                def fu(i=i, bk=bk):
                    ins = None
                    for k in range(KC):
                        ins = nc.tensor.matmul(psf[0:nt, bk, :], lhsT=xT[:, k, i * nt:(i + 1) * nt], rhs=wu[:, k, :],
                                               start=(k == 0), stop=(k == KC - 1))
                    return ins
                P.add("pe", fu, r=[("d_xT", i), ("d_wA", 0)], w=[BK(bk)])
                P.add("act", lambda i=i, bk=bk: nc.scalar.copy(out=u_tm[0:nt, i, :], in_=psf[0:nt, bk, :]), r=[BK(bk)],
                      w=[("d_u", i)])
                for (pi, dst) in pool_dst:
                    if pi == i and not os.environ.get("NOPOOLDST"):
                        lo = nt // 2
                        P.add("act", lambda bk=bk, lo=lo: nc.scalar.copy(out=ulast[0:nt, :], in_=psf[0:nt, bk, :]),
                              r=[BK(bk)], w=["d_ulast"])
                        if not os.environ.get("NOPOOLDMA"):
                            dma("sp", dst, ulast[lo:nt, :], r=["d_ulast"], w=["d_ulast_out"])
            if DSTOP < 1:
                return
            bandt = W["band"]; bandH = W["bandH"]
            for i in range(NB):
                bk = 3 + (i % 2)
                def fd(i=i, bk=bk):
                    ins = None
                    for g in range(4):
                        nc.tensor.matmul(psf[:, bk, g * nt:(g + 1) * nt], lhsT=u_tm[:, i, g * 128:(g + 1) * 128],
                                         rhs=bandt[:, band_sel(i), g, 0:nt], start=(g == 0), stop=False,
                                         skip_group_check=True)
                        ins = nc.tensor.matmul(psf[:, bk, g * nt:(g + 1) * nt], lhsT=uh[:, g * 128:(g + 1) * 128],
                                               rhs=bandH[:, i, g, 0:nt], start=False, stop=True,
                                               skip_group_check=True)
                    return ins
                P.add("pe", fd, r=[("d_u", i), "d_uh", "d_band"], w=[BK(bk)])
                P.add("dve", lambda i=i, bk=bk: nc.vector.tensor_copy(
                    out=dT[:, :, i * nt:(i + 1) * nt], in_=psf[:, bk, 0:4 * nt].rearrange("p (g t) -> p g t", g=4)),
                    r=[BK(bk)], w=[("d_dT", i)])
            dTk = [("d_dT", i) for i in range(NB)]
            if DSTOP < 2:
                return
            wuv = W["wuv"]; oT = W["oT"]
            for j in range(4):
                bk = 5 + (j % 2)
                def fo(j=j, bk=bk):
                    nc.tensor.matmul(psf[:, bk, 0:T], lhsT=wuv[:, 2 * j, :], rhs=attT[:, 2 * j, :], start=True, stop=False)
                    return nc.tensor.matmul(psf[:, bk, 0:T], lhsT=wuv[:, 2 * j + 1, :], rhs=attT[:, 2 * j + 1, :], start=False, stop=True)
                P.add("pe", fo, r=["d_attT", "d_wsm"], w=[BK(bk)])
                P.add("dve", lambda j=j, bk=bk: nc.vector.tensor_copy(out=oT[:, j, 0:T], in_=psf[:, bk, 0:T]), r=[BK(bk)],
                      w=[("d_oT", j)])
            oTk = [("d_oT", j) for j in range(4)]
            if DSTOP < 3:
                return
            wau = W["wau"]; wpl = W["wpl"]; mT = W["mT"]
            gsb = W["gsb"]; t1 = W["t1"]; t2 = W["t2"]
            for cc in range(8):
                par = cc % 2
                if cc % 4 == 0:
                    ia, ip = (1, 2) if cc == 0 else (0, 1)
                    wga = W["wA"][ia]
                    wgp = W["wA"][ip]
                    c0 = 928 + cc * 128
                    kga = ("d_wA", ia); kgp = ("d_wA", ip)
                    dma("sp", wga, wb_in[:, c0:c0 + 512].rearrange("(k p) n -> p k n", p=128), w=[kga])
                    dma("sp", wgp, wb_in[:, c0 + 1024:c0 + 1536].rearrange("(k p) n -> p k n", p=128), w=[kgp])
                co = (cc % 4) * 128
                for which, wt, wk, bk in ((0, wga, kga, 0 + par), (1, wgp, kgp, 2 + par)):
                    def fg(wt=wt, bk=bk, co=co):
                        ins = None
                        for k in range(KC):
                            ins = nc.tensor.matmul(psf[:, bk, 0:T], lhsT=wt[:, k, co:co + 128], rhs=xT[:, k, 0:T],
                                                   start=(k == 0), stop=(k == KC - 1))
                        return ins
                    P.add("pe", fg, r=xTk + [wk], w=[BK(bk)])
                    P.add("act", lambda which=which, bk=bk, cc=cc, par=par: nc.scalar.activation(
                        out=gsb[:, which, par, 0:T], in_=psf[:, bk, 0:T], func=AF.Sigmoid,
                        bias=b_gate_sb[:, which * 8 + cc:which * 8 + cc + 1], scale=1.0),
                        r=[BK(bk), RES], w=[("d_g", which, par)])
                bkA = 4 + par
                def fA(cc=cc, bkA=bkA):
                    ins = None
                    for j in range(4):
                        ins = nc.tensor.matmul(psf[:, bkA, 0:T], lhsT=wau[:, j, cc * 128:(cc + 1) * 128], rhs=oT[:, j, 0:T],
                                               start=(j == 0), stop=(j == 3))
                    return ins
                P.add("pe", fA, r=oTk + ["d_wsm"], w=[BK(bkA)])
                bkB = 6 + par
                g = cc // 2
                P.add("pe", lambda cc=cc, bkB=bkB, g=g: nc.tensor.matmul(
                    psf[:, bkB, 0:T], lhsT=wpl[:, g, (cc % 2) * 128:(cc % 2) * 128 + 128], rhs=dT[:, g, 0:T], start=True, stop=True),
                    r=dTk + ["d_wsm"], w=[BK(bkB)])
                P.add("dve", lambda bkA=bkA, par=par: nc.vector.tensor_tensor(out=t1[:, par, 0:T], in0=psf[:, bkA, 0:T],
                                                                              in1=gsb[:, 0, par, 0:T], op=ALU.mult),
                      r=[BK(bkA), ("d_g", 0, par)], w=[("d_t1", par)])
                P.add("dve", lambda bkB=bkB, par=par, cc=cc: nc.vector.scalar_tensor_tensor(
                    out=t2[:, par, 0:T], in0=psf[:, bkB, 0:T], scalar=pool_scale_sb[:, cc:cc + 1], in1=gsb[:, 1, par, 0:T],
                    op0=ALU.mult, op1=ALU.mult), r=[BK(bkB), ("d_g", 1, par), RES], w=[("d_t2", par)])
                P.add("dve", lambda par=par, cc=cc: nc.vector.tensor_tensor(out=mT[:, cc, 0:T], in0=t1[:, par, 0:T],
                                                                            in1=t2[:, par, 0:T], op=ALU.add),
                      r=[("d_t1", par), ("d_t2", par)], w=[("d_mT", cc)])
            mTk = [("d_mT", cc) for cc in range(8)]
            if DSTOP < 4:
                return
            wo = [W["wA"][2], W["wA"][0]]
            wok = [("d_wA", 2), ("d_wA", 0)]
            for half in range(2):
                dma("sp", wo[half], wb_o[:, half * 512:(half + 1) * 512].rearrange("(k p) n -> p k n", p=128), w=[wok[half]])
            h32 = W["h32"]; hb = W["hb"]; hT = W["hT"]; x32 = W["x32"]; lnc = W["lnc"]
            stats = W["stats"]; mv = W["mv"]

            def ln_a(i, b0, resid, resid_key, v, out_key):
                st = stats[0:nt, i]; m = mv[0:nt, i]
                P.add("dve", lambda: nc.vector.scalar_tensor_tensor(
                    out=v, in0=resid, scalar=float(ALPHA), in1=psf[0:nt, b0:b0 + 2, :].rearrange("p a b -> p (a b)"),
                    op0=ALU.mult, op1=ALU.add), r=[BK(b0), BK(b0 + 1), resid_key], w=[out_key])
                for c in range(2):
                    P.add("dve", lambda c=c: nc.vector.bn_stats(out=st[:, c, :], in_=v[:, c * 512:(c + 1) * 512]),
                          r=[out_key], w=[("d_stats", i, c)])
                P.add("dve", lambda: nc.vector.bn_aggr(out=m[:, 0:2], in_=st.rearrange("p a b -> p (a b)")),
                      r=[("d_stats", i, 0), ("d_stats", i, 1)], w=[("d_mv", i)])
                P.add("dve", lambda: nc.vector.tensor_scalar(out=m[:, 2:3], in0=m[:, 1:2], scalar1=LN_EPS, scalar2=None,
                                                             op0=ALU.add), r=[("d_mv", i)], w=[("d_mv2", i)])
                P.add("pool", lambda: nc.gpsimd.tensor_tensor(out=m[:, 3:4], in0=m[:, 2:3], in1=mhalf[0:nt, 0:1], op=ALU.pow),
                      r=[("d_mv2", i), RES], w=[("d_rstd", i)])

            def ln_b(i, gi, v, out_key):
                m = mv[0:nt, i]
                P.add("dve", lambda: nc.vector.tensor_scalar(out=m[:, 4:5], in0=m[:, 0:1], scalar1=m[:, 3:4], scalar2=-1.0,
                                                             op0=ALU.mult, op1=ALU.mult), r=[("d_mv", i), ("d_rstd", i)], w=[("d_nmr", i)])
                P.add("act", lambda: nc.scalar.activation(out=v, in_=v, func=AF.Identity, bias=m[:, 4:5], scale=m[:, 3:4]),
                      r=[out_key, ("d_nmr", i), ("d_rstd", i)], w=[out_key])
                P.add("dve", lambda: nc.vector.tensor_tensor(out=v, in0=v, in1=lnc[0:nt, gi, :], op=ALU.mult),
                      r=[out_key, "d_lnc"], w=[out_key])
                P.add("dve", lambda: nc.vector.tensor_tensor(out=v, in0=v, in1=lnc[0:nt, gi + 1, :], op=ALU.add),
                      r=[out_key, "d_lnc"], w=[out_key])

            def emit_fh(i):
                b0 = 2 * (i % 3)
                def fh(i=i, b0=b0):
                    ins = None
                    for half in range(2):
                        for cc in range(8):
                            ins = nc.tensor.matmul(psf[0:nt, b0 + half, :], lhsT=mT[:, cc, i * nt:(i + 1) * nt],
                                                   rhs=wo[half][:, cc, :], start=(cc == 0), stop=(cc == 7))
                    return ins
                P.add("pe", fh, r=mTk + wok, w=[BK(b0), BK(b0 + 1)])

            def emit_ln1a(i):
                b0 = 2 * (i % 3)
                dma("sp", x32[0:nt, i % 2, :], x_blk_src(i), w=[("d_x32", i % 2)])
                ln_a(i, b0, x32[0:nt, i % 2, :], ("d_x32", i % 2), h32[0:nt, i, :], ("d_h32", i))

            def emit_ln1b(i):
                ln_b(i, 0, h32[0:nt, i, :], ("d_h32", i))
                P.add("act", lambda i=i: nc.scalar.copy(out=hb[0:nt, i % 2, :], in_=h32[0:nt, i, :]), r=[("d_h32", i)],
                      w=[("d_hb", i % 2)])

            def emit_tr(i):
                transposes(lambda k, i=i: hb[:, i % 2, k * 128:(k + 1) * 128], 8, nt, 7, hT[:, :, i * nt:(i + 1) * nt],
                           ("d_hT", i), [("d_hb", i % 2)], evac="act")

            emit_fh(0)
            if NB > 1:
                emit_fh(1)
            for i in range(NB + 2):
                if i < NB:
                    emit_ln1a(i)
                if i + 2 < NB:
                    emit_fh(i + 2)
                if 1 <= i <= NB:
                    emit_ln1b(i - 1)
                if 2 <= i:
                    emit_tr(i - 2)
            hTk = [("d_hT", i) for i in range(NB)]
            P.barrier()
            if DSTOP < 5:
                return
            aT = W["aT"]; sl = W["sl"]
            for ft in range(D_FF // 256):
                wi = ft % 2
                wg = W["wG"][wi]; wu_ = W["wU"][wi]
                dma("sp", wg, wb_gate[:, ft * 256:(ft + 1) * 256].rearrange("(k p) n -> p k n", p=128), w=[("d_wG", wi)])
                dma("sp", wu_, wb_up[:, ft * 256:(ft + 1) * 256].rearrange("(k p) n -> p k n", p=128), w=[("d_wU", wi)])
                for sub in range(2):
                    fc = ft * 2 + sub
                    par = fc % 2
                    for which, wt, wk, bk in ((0, wg, ("d_wG", wi), 0 + par), (1, wu_, ("d_wU", wi), 2 + par)):
                        def fgu(wt=wt, bk=bk, sub=sub):
                            ins = None
                            for k in range(KC):
                                ins = nc.tensor.matmul(psf[:, bk, 0:T], lhsT=wt[:, k, sub * 128:(sub + 1) * 128], rhs=hT[:, k, 0:T],
                                                       start=(k == 0), stop=(k == KC - 1))
                            return ins
                        P.add("pe", fgu, r=hTk + [wk], w=[BK(bk)])
                    P.add("act", lambda par=par: nc.scalar.activation(out=sl[:, par, 0:T], in_=psf[:, 0 + par, 0:T], func=AF.Silu),
                          r=[BK(0 + par)], w=[("d_sl", par)])
                    P.add("dve", lambda par=par, fc=fc: nc.vector.tensor_tensor(out=aT[:, fc, 0:T], in0=psf[:, 2 + par, 0:T],
                                                                                in1=sl[:, par, 0:T], op=ALU.mult),
                          r=[BK(2 + par), ("d_sl", par)], w=[("d_aT", fc)])
            if DSTOP < 6:
                return
            for fc in range(NFC):
                wi = fc % 3
                wd = W["wD"][wi]
                dma("sp", wd, wb_down[fc * 128:(fc + 1) * 128, :], w=[("d_wD", wi)])
                def fdn(fc=fc, wd=wd):
                    ins = None
                    for i in range(NB):
                        for half in range(2):
                            ins = nc.tensor.matmul(psf[0:nt, 2 * i + half, :], lhsT=aT[:, fc, i * nt:(i + 1) * nt],
                                                   rhs=wd[:, half * 512:(half + 1) * 512], start=(fc == 0), stop=(fc == NFC - 1))
                    return ins
                P.add("pe", fdn, r=[("d_aT", fc), ("d_wD", wi)], w=[BK(j) for j in range(2 * NB)])
            if DSTOP < 7:
                return
            for i in range(NB):
                ln_a(i, 2 * i, h32[0:nt, i, :], ("d_h32", i), h32[0:nt, i, :], ("d_h32", i))
                if i == 0 and next_head is not None:
                    next_head(0)
            for i in range(NB):
                ln_b(i, 2, h32[0:nt, i, :], ("d_h32", i))
                dma("sp", y_dst(i), h32[0:nt, i, :], r=[("d_h32", i)], w=[("d_yout", i)])
            P.barrier()

        def d_alloc(NB, nt):
            T = NB * nt
            W = {}
            W["band"] = ar.alloc([2, 4, 128], BF16)
            W["bandH"] = ar.alloc([4, 4, 128], BF16)
            W["wuv"] = ar.alloc([8, 128], BF16)
            W["wau"] = ar.alloc([4, 1024], BF16)
            W["wpl"] = ar.alloc([4, 256], BF16)
            W["lnc"] = ar.alloc([4, 1024], F32)
            W["h32"] = ar.alloc([NB, 1024], F32)
            W["hT"] = ar.alloc([KC, T], BF16)
            W["x32"] = ar.alloc([2, 1024], F32)
            W["stats"] = ar.alloc([NB, 2, 6], F32)
            W["mv"] = ar.alloc([NB, 8], F32)
            W["xb"] = ar.alloc([2, 1024], BF16)
            W["xT"] = ar.alloc([KC, T], BF16)
            W["xhb"] = ar.alloc([1024], BF16)
            W["xhT"] = ar.alloc([KC, 128], BF16)
            mark = ar.off
            W["uh"] = ar.alloc([512], BF16)
            W["u_tm"] = ar.alloc([NB, 512], BF16)
            W["ulast"] = ar.alloc([512], F32)
            W["dT"] = ar.alloc([4, T], BF16)
            W["oT"] = ar.alloc([4, T], BF16)
            W["mT"] = ar.alloc([8, T], BF16)
            W["gsb"] = ar.alloc([2, 2, T], BF16)
            W["t1"] = ar.alloc([2, T], F32)
            W["t2"] = ar.alloc([2, T], F32)
            W["wA"] = [ar.alloc([KC, 512], BF16) for _ in range(3)]
            W["hb"] = ar.alloc([2, 1024], BF16)
            end1 = ar.off
            ar.off = mark
            W["aT"] = ar.alloc([NFC, T], BF16)
            W["sl"] = ar.alloc([2, T], F32)
            W["wG"] = [ar.alloc([KC, 256], BF16) for _ in range(2)]
            W["wU"] = [ar.alloc([KC, 256], BF16) for _ in range(2)]
            W["wD"] = [ar.alloc([1024], BF16) for _ in range(3)]
            ar.off = max(ar.off, end1)
            P.add("dve", lambda: nc.vector.memset(W["u_tm"], 0.0), w=[("d_u", i) for i in range(NB)])
            P.add("dve", lambda: nc.vector.memset(W["xb"], 0.0), w=[("d_xb", 0), ("d_xb", 1)])
            P.add("dve", lambda: nc.vector.memset(W["hb"], 0.0), w=[("d_hb", 0), ("d_hb", 1)])
            if DSTOP >= -2:
                dma("sp", W["band"], band_d.rearrange("f p g t -> p f g t"), w=["d_band"])
                dma("sp", W["bandH"], bandH_d, w=["d_band"])
            if DSTOP >= -3:
                dma("pool", W["wuv"], w_uvp, w=["d_wsm"])
            if DSTOP >= -4:
                dma("sp", W["wau"], wb_attn_up.rearrange("(j p) n -> p j n", p=128), r=["wscratch"], w=["d_wsm"])
                dma("sp", W["wpl"], wb_pool.rearrange("g p n -> p g n"), r=["wscratch"], w=["d_wsm"])
            if DSTOP >= -5:
                dma("sp", W["lnc"], lnv.rearrange("(o a) n -> o a n", o=1).to_broadcast([128, 4, 1024]), w=["d_lnc"])
            return W

        def qa_alloc():
            W = {}
            W["xb"] = ar.alloc([1024], BF16)
            W["xT"] = ar.alloc([KC, 128], BF16)
            W["zqn"] = ar.alloc([256], BF16)
            W["zqnT"] = ar.alloc([2, 128], BF16)
            W["ckv32"] = ar.alloc([128], F32)
            W["kr32"] = ar.alloc([32], F32)
            W["rA"] = ar.alloc([4, 32], F32)
            W["rB"] = ar.alloc([4, 32], F32)
            W["qn"] = ar.alloc([8, 64], BF16)
            W["qr32"] = ar.alloc([8, 32], F32)
            W["qr"] = ar.alloc([8, 4, 32], BF16)
            W["qnT"] = ar.alloc([4, 128], BF16)
            W["QlatT"] = [ar.alloc([1024], BF16) for _ in range(2)]
            W["QropeT"] = [ar.alloc([4, 1024], BF16) for _ in range(2)]
            W["Pbuf"] = ar.alloc([NPB, 512], BF16)
            W["att_tm"] = ar.alloc([8, 128], BF16)
            W["rl"] = ar.alloc([8], F32)
            P.add("dve", lambda: nc.vector.memset(W["Pbuf"], 0.0), w=[("P", i) for i in range(NPB)])
            P.add("dve", lambda: nc.vector.memset(W["att_tm"], 0.0), w=[("att_tm", i) for i in range(3)])
            P.add("dve", lambda: nc.vector.memset(W["qr"], 0.0), w=["q_qr"])
            P.add("dve", lambda: nc.vector.memset(W["qn"], 0.0), w=[("q_qn", 0), ("q_qn", 1)])
            P.add("dve", lambda: nc.vector.memset(W["zqn"], 0.0), w=["q_zqn"])
            return W

        attT_stash = ar.alloc([8, NG * 128], BF16)
        phase_mark = ar.off
        P.barrier()

        def prompt_batch(b):
            ar.off = phase_mark
            ckvT = ar.alloc([SEQ], BF16)
            Vx = ar.alloc([NKB, OW], BF16)
            krT = ar.alloc([NKB // 4 * 128], BF16)
            kv_mark = ar.off
            NXB = 3
            xkb = [ar.alloc([4, 1024], BF16) for _ in range(NXB)]
            xkT = [ar.alloc([KC, 128], BF16) for _ in range(2)]
            tabk = [ar.alloc([2, 4, 32], F32) for _ in range(2)]
            sskb = [ar.alloc([4], F32) for _ in range(2)]
            rAk = ar.alloc([4, 32], F32)
            rBk = ar.alloc([4, 32], F32)
            krtm = ar.alloc([4, 32], BF16)
            P.add("dve", lambda: nc.vector.memset(Vx[:, :, 128:130], 1.0), w=["Vones"])

            def k_load(t):
                xk = xkb[t % NXB]
                dma("pool", xk, x_all[b, t * 512:(t + 1) * 512, :].rearrange("(j p) d -> p j d", p=128), w=[("k_x", t % NXB)])

            def k_head(t):
                bi = t % 2
                xk = xkb[t % NXB]
                ssk = sskb[bi]
                dma("sp", tabk[bi], tabK[:, :, 4 * t:4 * t + 4, :].rearrange("c p j e -> p c j e"), w=[("k_tab", bi)])
                zb0 = 2 * bi
                for j in range(4):
                    xi = (4 * t + j) % 2
                    transposes(lambda k, xk=xk, j=j: xk[:, j, k * 128:(k + 1) * 128], 8, 128, 4 + xi, xkT[xi], ("k_xT", xi),
                               [("k_x", t % NXB)], evac="act")
                    zbk = zb0 + j // 2
                    zc = (j % 2) * 256
                    def fzk(xi=xi, zbk=zbk, zc=zc):
                        ins = None
                        for k in range(KC):
                            ins = nc.tensor.matmul(psf[:, zbk, zc:zc + 160], lhsT=xkT[xi][:, k, :], rhs=w_in_a[:, k, 256:416],
                                                   start=(k == 0), stop=(k == KC - 1), skip_group_check=True)
                        return ins
                    P.add("pe", fzk, r=[("k_xT", xi), RES], w=[("k_z", bi, j)])
                    jc = (bi * 4 + j) * 128
                    P.add("act", lambda zbk=zbk, zc=zc, j=j, jc=jc, ssk=ssk: nc.scalar.activation(
                        out=junk[:, jc:jc + 128], in_=psf[:, zbk, zc:zc + 128], func=AF.Square, accum_out=ssk[:, j:j + 1]),
                        r=[("k_z", bi, j)], w=[("k_ss", bi, j), ("junk", bi, j)])

            def k_tail(t):
                bi = t % 2
                ssk = sskb[bi]
                zb0 = 2 * bi
                ssk_keys = [("k_ss", bi, j) for j in range(4)]
                P.add("dve", lambda: nc.vector.tensor_scalar(out=ssk, in0=ssk, scalar1=1.0 / 128, scalar2=RMS_EPS,
                                                             op0=ALU.mult, op1=ALU.add), r=ssk_keys, w=ssk_keys)
                P.add("pool", lambda: nc.gpsimd.tensor_tensor(out=ssk, in0=ssk, in1=mhalf[:, 0:4], op=ALU.pow),
                      r=ssk_keys + [RES], w=ssk_keys)
                for j in range(4):
                    kb = 4 * t + j
                    zbk = zb0 + j // 2
                    zc = (j % 2) * 256
                    P.add("dve", lambda zbk=zbk, zc=zc, kb=kb, j=j: nc.vector.scalar_tensor_tensor(
                        out=Vx[:, kb, 0:128], in0=psf[:, zbk, zc:zc + 128], scalar=ssk[:, j:j + 1], in1=gkv_bc,
                        op0=ALU.mult, op1=ALU.mult), r=[("k_z", bi, j), ("k_ss", bi, j), RES], w=[("V", kb)])
                zr = psf[:, zb0:zb0 + 2, :].rearrange("p a (c e) -> p (a c) e", c=2)[:, :, 128:160]
                rope_ops(zr, tabk[bi][:, 0], tabk[bi][:, 1], krtm, 128, [("k_z", bi, j) for j in range(4)] + [("k_tab", bi)],
                         ["k_krtm"], rAk, rBk)
                def ftr():
                    ins = None
                    for j in range(4):
                        ins = nc.tensor.transpose(psb[:, 6, j * 128:(j + 1) * 128], Vx[:, 4 * t + j, 0:128], ident)
                    return ins
                P.add("pe", ftr, r=[("V", 4 * t + j) for j in range(4)] + [RES], w=[BK(6)])
                P.add("dve", lambda: nc.vector.tensor_copy(out=ckvT[:, t * 512:(t + 1) * 512], in_=psb[:, 6, 0:512]),
                      r=[BK(6)], w=[("ckvT", t)])
                P.add("pe", lambda: nc.tensor.transpose(psb[:, 7, 0:128], krtm.rearrange("p a b -> p (a b)"), ident),
                      r=["k_krtm", RES], w=[BK(7)])
                P.add("dve", lambda: nc.vector.tensor_copy(out=krT[:, t * 128:(t + 1) * 128], in_=psb[:, 7, 0:128]),
                      r=[BK(7)], w=[("krT", t)])

            k_load(0)
            if NSTEP > 1:
                k_load(1)
            for t in range(NSTEP + 1):
                if t + 2 < NSTEP:
                    k_load(t + 2)
                if t < NSTEP:
                    k_head(t)
                if t >= 1:
                    k_tail(t - 1)
            P.barrier()
            casts = list(cast_list) if b == 0 else []
            if STOP < 2:
                for c in casts:
                    c()
                return
            ar.off = kv_mark
            W = qa_alloc()
            def mk_q(s):
                return q_stage(x_own[b, s], 128, tabO_sb[:, 0, s, :], tabO_sb[:, 1, s, :], ckv_own[b, s], kr_own[b, s], W, s % 2)
            order = list(range(NG - 1, -1, -1))
            for _ in mk_q(order[0]):
                pass
            for oi, s in enumerate(order):
                qb = s % 2
                groups = []
                for g in range(2):
                    groups.append(dict(lat=W["QlatT"][qb][:, g * 512:(g + 1) * 512],
                                       rope=(lambda pb, g=g, qb=qb: W["QropeT"][qb][:, pb // 32, g * 512:(g + 1) * 512]),
                                       heads=[(g * 4 + hl, hl * 128) for hl in range(4)], N=512))
                kbs = []
                for kb in range(8 * s + 8):
                    pb = 32 * (kb % 4)
                    c0 = (kb // 4) * 128
                    ing = kb >= 8 * s
                    kbs.append(dict(ckvT=ckvT[:, kb * 128:(kb + 1) * 128], krT=krT[:, c0:c0 + 128], pb=pb,
                                    V=Vx[:, kb, 0:129], nk=128,
                                    mask=(maskK[:, (kb - 8 * s) * 128:(kb - 8 * s + 1) * 128] if ing else None), keys=[]))
                nxt = mk_q(order[oi + 1]) if (oi + 1 < NG and not os.environ.get("NOINTER")) else None
                if QSTOP >= 9:
                    attention(128, groups, kbs, W["Pbuf"], W["att_tm"], W["rl"],
                              attT_stash[:, :, s * 128:(s + 1) * 128], ("stash", s), [(("q", qb), "QlatT"), (("q", qb), "QropeT")],
                              inter=nxt, inter_n=24)
                if casts:
                    casts.pop(0)()
                if nxt is not None:
                    for _ in nxt:
                        pass
                elif oi + 1 < NG:
                    for _ in mk_q(order[oi + 1]):
                        pass
            for c in casts:
                c()
            P.barrier()
            if STOP < 3:
                return
            ar.off = phase_mark
            NB = min(4, NG)
            Wd = d_alloc(NB, 128)
            ngr = NG // NB
            def xsrc(gi):
                return lambda i: x_own[b, gi * NB + i]
            def xhsrc(gi):
                return x_halo[b, gi * NB:(gi + 1) * NB].rearrange("s r d -> (s r) d")
            for gi in range(ngr):
                pool_dst = [(NB - 1, pool_p[b])] if gi == ngr - 1 else []
                nh = None
                if gi + 1 < ngr and not os.environ.get("NOPREHEAD"):
                    nh = (lambda bank, gi=gi: d_head(NB, 128, xsrc(gi + 1), xhsrc(gi + 1), Wd, bank))
                d_stage(NB, 128, xsrc(gi), xhsrc(gi),
                        attT_stash[:, :, gi * NB * 128:(gi + 1) * NB * 128],
                        (lambda i, gi=gi: 0 if (gi == 0 and i == 0) else 1),
                        lambda i, gi=gi: y_own[b, gi * NB + i], pool_dst, Wd,
                        head_done=(gi > 0 and not os.environ.get("NOPREHEAD")), next_head=nh)
            P.barrier()

        for b in range(2):
            if STOP >= 1:
                prompt_batch(b)

        def sample_phase():
            ar.off = phase_mark
            ckvT = ar.alloc([1088], BF16)
            Vx = ar.alloc([9, OW], BF16)
            krT = ar.alloc([3 * 128], BF16)
            krc = ar.alloc([8, 32], BF16)
            ckvn = ar.alloc([128], BF16)
            krn = ar.alloc([128], BF16)
            W = qa_alloc()
            P.add("dve", lambda: nc.vector.memset(Vx[:, :, 128:130], 1.0), w=["Vones"])
            P.add("dve", lambda: nc.vector.memset(krT[:, 256:384], 0.0), w=["s_krTn"])
            P.add("dve", lambda: nc.vector.memset(krn, 0.0), w=["s_krn"])
            P.add("dve", lambda: nc.vector.memset(Vx[64:128, 8, :], 0.0), w=["s_Vn", "Vones"])
            for e in range(4):
                dma("pool", Vx[:, 0:8, 0:128], ckv_cache[e].rearrange("(k p) r -> p k r", p=128), w=["s_V"])
                dma("pool", krc, kr_cache[e].rearrange("(k p) r -> p k r", p=128), w=["s_krc"])
                transposes(lambda k: Vx[:, k, 0:128], 8, 128, 7, ckvT[:, 0:1024].rearrange("p (k t) -> p k t", k=8), "s_ckvT", ["s_V"])
                transposes(lambda k: krc[:, 4 * k:4 * k + 4, :].rearrange("p a b -> p (a b)"), 2, 128, 7,
                           krT[:, 0:256].rearrange("p (k t) -> p k t", k=2), "s_krT", ["s_krc"])

                def sample_kv(ckv32, kr32):
                    P.add("dve", lambda: nc.vector.tensor_copy(out=Vx[0:64, 8, 0:128], in_=ckv32[0:64]), r=["q_ckv32"], w=["s_Vn"])
                    P.add("dve", lambda: nc.vector.tensor_copy(out=krn[0:64, 0:32], in_=kr32[0:64]), r=["q_kr32"], w=["s_krn"])
                    transposes(lambda k: Vx[:, 8, 0:128], 1, 64, 7, ckvT[:, 1024:1088].unsqueeze(1), "s_ckvTn", ["s_Vn"])
                    P.add("pe", lambda: nc.tensor.transpose(psb[:, 7, 0:128], krn, ident), r=["s_krn", RES],
                          w=[BK(7)])
                    P.add("dve", lambda: nc.vector.tensor_copy(out=krT[0:32, 256:320], in_=psb[0:32, 7, 0:64]), r=[BK(7)], w=["s_krTn"])

                for _ in q_stage(x_s[e], 64, tabS_sb[0:64, 0, :], tabS_sb[0:64, 1, :], ckv_s[e], kr_s[e], W, 0, sample_kv=sample_kv):
                    pass
                groups = [dict(lat=W["QlatT"][0][:, 0:512], rope=(lambda pb: W["QropeT"][0][:, pb // 32, 0:512]),
                               heads=[(h, h * 64) for h in range(8)], N=512)]
                kbs = []
                for kb in range(8):
                    pb = 32 * (kb % 4)
                    c0 = (kb // 4) * 128
                    kbs.append(dict(ckvT=ckvT[:, kb * 128:(kb + 1) * 128], krT=krT[:, c0:c0 + 128], pb=pb,
                                    V=Vx[:, kb, 0:129], nk=128, mask=None, keys=["s_ckvT", "s_krT", "s_V"]))
                kbs.append(dict(ckvT=ckvT[:, 1024:1088], krT=krT[:, 256:320], pb=0, V=Vx[:, 8, 0:129], nk=64, mask=None,
                                keys=["s_ckvTn", "s_krTn", "s_Vn"]))
                attention(64, groups, kbs, W["Pbuf"], W["att_tm"], W["rl"],
                          attT_stash[:, :, e * 64:(e + 1) * 64], ("stash", e), [(("q", 0), "QlatT"), (("q", 0), "QropeT")])
            P.barrier()
            ar.off = phase_mark
            Wd = d_alloc(4, 64)

            def halo_direct(uh):
                P.add("dve", lambda: nc.vector.memset(uh, 0.0), w=["d_uh0"])
                for e in range(4):
                    dma("pool", uh[32 * e + 17:32 * e + 32, :], pool_state[e], r=["d_uh0"], w=["d_uh"])

            d_stage(4, 64, lambda i: x_s[i], None, attT_stash[:, :, 0:256], lambda i: 1, lambda i: y_s[i],
                    [(i, pool_s[i]) for i in range(4)], Wd, halo_direct=halo_direct)
            P.barrier()
        if STOP >= 4:
            sample_phase()
        nops, nwait = P.finalize()
    return nc, (nops, nwait)


_CACHE = {}


def _rope_tables(pos):
    half = QK_ROPE // 2
    inv = (np.float32(1.0) / (np.float32(10000.0) ** (np.arange(half, dtype=np.float32) / np.float32(half)))).astype(np.float32)
    ang = (pos.astype(np.float32)[:, None] * inv[None, :]).astype(np.float32)
    c = np.cos(ang).astype(np.float32)
    s = np.sin(ang).astype(np.float32)
    cos2 = np.concatenate([c, c], axis=-1)
    sin2 = np.concatenate([-s, s], axis=-1)
    return cos2, sin2


def _bands():
    wins = (2, 4, 8, 16)
    band = np.zeros((2, 128, 4, 128), np.float32)
    bandH = np.zeros((32, 4, 128), np.float32)
    tp = np.arange(128)[:, None]
    t = np.arange(128)[None, :]
    r = np.arange(32)[:, None] - 32
    for g, w in enumerate(wins):
        inwin = ((tp <= t) & (tp >= t - w + 1)).astype(np.float32)
        eye = (tp == t).astype(np.float32)
        band[1, :, g, :] = inwin / w - eye
        cnt = np.minimum(t + 1, w).astype(np.float32)
        band[0, :, g, :] = inwin / cnt - eye
        bandH[:, g, :] = ((r >= t - w + 1).astype(np.float32)) / w
    bandHm = np.zeros((128, 4, 4, 128), np.float32)
    for i in range(4):
        bandHm[32 * i:32 * i + 32, i] = bandH
    return band.astype(ml_dtypes.bfloat16), bandHm.astype(ml_dtypes.bfloat16)


def kernel(x_prompt, x_sample, cache_kv_latent, cache_k_rope, state_pool, w_in, b_gate, q_norm_g, w_uq, kv_norm_g,
           w_uk, w_uv, w_attn_up, w_pool, pool_scale, w_o, ln1_g, ln1_b, w_gate, w_up, w_down, ln2_g, ln2_b):
    f = lambda a: np.ascontiguousarray(np.asarray(a), dtype=np.float32)
    x_prompt = f(x_prompt); x_sample = f(x_sample)
    B, SEQ, _ = x_prompt.shape
    NG = SEQ // 1024
    assert B == 2 and SEQ % 1024 == 0
    if NG not in _CACHE:
        _CACHE[NG] = build_program(NG)
    nc, _ = _CACHE[NG]

    cache_kv_latent = f(cache_kv_latent); cache_k_rope = f(cache_k_rope); state_pool = f(state_pool)
    w_uk_ = f(w_uk)[0]
    w_ukT = np.zeros((128, 8, 128), np.float32)
    for h in range(8):
        w_ukT[(h % 2) * 64:(h % 2) * 64 + 64, h, :] = w_uk_[:, h, :].T
    w_uv_ = f(w_uv)[0]
    w_uvp = np.zeros((128, 8, 128), np.float32)
    for h in range(8):
        w_uvp[:, h, (h % 2) * 64:(h % 2) * 64 + 64] = w_uv_[:, h, :]
    shared = {
        "x_all": x_prompt,
        "w_in": f(w_in)[0], "w_uq": f(w_uq)[0], "w_ukT": w_ukT, "w_uvp": w_uvp, "w_attn_up": f(w_attn_up)[0],
        "w_pool": f(w_pool)[0], "w_o": f(w_o)[0], "w_gate": f(w_gate)[0], "w_up": f(w_up)[0], "w_down": f(w_down)[0],
        "gq": f(q_norm_g), "gkv": f(kv_norm_g),
        "b_gateT": np.ascontiguousarray(f(b_gate)[0].reshape(16, 128).T),
        "pool_scaleT": np.ascontiguousarray(f(pool_scale)[0].reshape(8, 128).T),
        "lnv": np.ascontiguousarray(np.stack([f(ln1_g)[0], f(ln1_b)[0], f(ln2_g)[0], f(ln2_b)[0]])),
        "ident": np.eye(128, dtype=np.float32).astype(ml_dtypes.bfloat16),
    }
    cosK, sinK = _rope_tables(np.arange(SEQ))
    shared["tabK"] = np.ascontiguousarray(np.stack([cosK, sinK]).reshape(2, SEQ // 128, 128, 32).transpose(0, 2, 1, 3))
    cosS, sinS = _rope_tables(cache_kv_latent.shape[2] + np.arange(64))
    shared["tabS"] = np.ascontiguousarray(np.stack([cosS, sinS]))
    band, bandH = _bands()
    shared["bandH"] = bandH
    mq = np.zeros((128, 512), np.float32)
    mq[0, :] = NEG_BIG
    mq[1, :] = np.where((np.arange(512) % 128) < 64, NEG_BIG, 0.0)
    shared["maskQ"] = mq.astype(ml_dtypes.bfloat16)
    rm = np.zeros((128, 4), np.float32)
    for m_ in range(4):
        rm[32 * m_:32 * m_ + 32, m_] = 1.0
    shared["rowmask"] = rm
    band_generic = band.copy()
    band_generic[0] = band_generic[1]

    in_maps = []
    xp = x_prompt.reshape(2, NG, 8, 128, D_MODEL)
    for i in range(NCORES):
        m = dict(shared)
        m["x_own"] = np.ascontiguousarray(xp[:, :, i])
        halo = np.zeros((2, NG, 32, D_MODEL), np.float32)
        for s in range(NG):
            st = (8 * s + i) * 128
            if st >= 32:
                halo[:, s] = x_prompt[:, st - 32:st]
        m["x_halo"] = halo
        m["x_s"] = np.ascontiguousarray(x_sample[4 * i:4 * i + 4])
        m["ckv_cache"] = np.ascontiguousarray(cache_kv_latent[0, 4 * i:4 * i + 4])
        m["kr_cache"] = np.ascontiguousarray(cache_k_rope[0, 4 * i:4 * i + 4])
        m["pool_state"] = np.ascontiguousarray(state_pool[0, 4 * i:4 * i + 4])
        pos_own = ((8 * np.arange(NG)[:, None] + i) * 128 + np.arange(128)[None, :]).reshape(-1)
        cO, sO = _rope_tables(pos_own)
        m["tabO"] = np.ascontiguousarray(np.stack([cO, sO]).reshape(2, NG, 128, 32).transpose(0, 2, 1, 3))
        mk = np.zeros((128, 1024), np.float32)
        slot = np.arange(1024) // 128
        mk[0] = (slot > i).astype(np.float32)
        mk[1] = ((slot == i) & ((np.arange(1024) % 128) >= 64)).astype(np.float32)
        m["maskK"] = mk.astype(ml_dtypes.bfloat16)
        m["band"] = band if i == 0 else band_generic
        in_maps.append(m)

    res = run_bass_kernel_spmd(nc, in_maps, core_ids=list(range(NCORES)))
    R = res.results
    y_p = np.zeros((2, SEQ, D_MODEL), np.float32).reshape(2, NG, 8, 128, D_MODEL)
    ckv_p = np.zeros((2, NG, 8, 128, 128), np.float32)
    kr_p = np.zeros((2, NG, 8, 128, 32), np.float32)
    for i in range(NCORES):
        y_p[:, :, i] = R[i]["y_own"]
        ckv_p[:, :, i] = R[i]["ckv_own"]
        kr_p[:, :, i] = R[i]["kr_own"]
    y_s = np.concatenate([R[i]["y_s"] for i in range(NCORES)], axis=0)
    ckv_s = np.concatenate([R[i]["ckv_s"] for i in range(NCORES)], axis=0)
    kr_s = np.concatenate([R[i]["kr_s"] for i in range(NCORES)], axis=0)
    pool_s = np.concatenate([R[i]["pool_s"][:, -15:] for i in range(NCORES)], axis=0)
    pool_p = np.asarray(R[NCORES - 1]["pool_p"])[:, -15:]
    return (y_p.reshape(2, SEQ, D_MODEL).astype(np.float32),
            y_s.astype(np.float32),
            ckv_p.reshape(1, 2, SEQ, 128).astype(np.float32),
            kr_p.reshape(1, 2, SEQ, 32).astype(np.float32),
            pool_p.reshape(1, 2, 15, 512).astype(np.float32),
            ckv_s.reshape(1, 32, 64, 128).astype(np.float32),
            kr_s.reshape(1, 32, 64, 32).astype(np.float32),
            pool_s.reshape(1, 32, 15, 512).astype(np.float32))
```

```python
import math
import os
from contextlib import ExitStack
QSTOP = int(os.environ.get("QSTOP", "99"))
DSTOP = int(os.environ.get("DSTOP", "99"))

import numpy as np
import ml_dtypes

import concourse.bass as bass
import concourse.mybir as mybir
from concourse.bass_utils import run_bass_kernel_spmd

F32 = mybir.dt.float32
BF16 = mybir.dt.bfloat16
AF = mybir.ActivationFunctionType
ALU = mybir.AluOpType

D_MODEL = 1024
KC = 8
N_HEADS = 8
QK_NOPE = 64
QK_ROPE = 32
Q_LORA = 256
KV_LORA = 128
POOL_WIDTH = 512
D_FF = 2816
NFC = D_FF // 128
IN_TOTAL = 2976
ATTN_SCALE = 96 ** -0.5
ALPHA = 2 ** 0.25
RMS_EPS = 1e-6
LN_EPS = 1e-5
NEG_BIG = -30000.0
NCORES = 8
OW = 130


class _Op:
    __slots__ = ("stream", "fn", "deps", "dma", "signal", "pos", "dslot", "dval", "waits", "sigval")

    def __init__(self, stream, fn, deps, dma):
        self.stream = stream
        self.fn = fn
        self.deps = deps
        self.dma = dma
        self.signal = False
        self.pos = -1
        self.dslot = -1
        self.dval = 0
        self.waits = None
        self.sigval = 0


class Prog:
    STREAMS = ("pe", "act", "dve", "pool", "sp")
    R = 8

    def __init__(self, nc, es):
        self.nc = nc
        self.eng = {"pe": nc.tensor, "act": nc.scalar, "dve": nc.vector, "pool": nc.gpsimd, "sp": nc.sync}
        self.ops = []
        self.lastw = {}
        self.readers = {}
        self.sem = {s: es.enter_context(nc.semaphore("sem_" + s)) for s in self.STREAMS}
        self.dsem = {q: [es.enter_context(nc.semaphore("dq_%s_%d" % (q, i))) for i in range(self.R)]
                     for q in ("sp", "pool")}
        self.last_in_stream = {}
        self.dma_since_barrier = []

    def add(self, stream, fn, r=(), w=(), dma=False):
        idx = len(self.ops)
        deps = {}
        for k in r:
            p = self.lastw.get(k)
            if p is not None:
                deps[p] = True
        for k in w:
            p = self.lastw.get(k)
            if p is not None:
                deps[p] = True
            for q in self.readers.get(k, ()):
                if q not in deps:
                    deps[q] = False
        for k in r:
            self.readers.setdefault(k, []).append(idx)
        for k in w:
            self.lastw[k] = idx
            self.readers[k] = []
        self.ops.append(_Op(stream, fn, deps, dma))
        self.last_in_stream[stream] = idx
        if dma:
            self.dma_since_barrier.append(idx)
        return idx

    def barrier(self):
        deps = {i: True for i in self.last_in_stream.values()}
        for i in self.dma_since_barrier:
            deps[i] = True
        for s in self.STREAMS:
            self.ops.append(_Op(s, None, dict(deps), False))
        self.lastw = {}
        self.readers = {}
        self.dma_since_barrier = []
        self.last_in_stream = {}

    def finalize(self):
        ops = self.ops
        pos = {s: 0 for s in self.STREAMS}
        dcount = {"sp": 0, "pool": 0}
        dlist = {"sp": [], "pool": []}
        waited_pos = {s: {} for s in self.STREAMS}
        waited_dma = {s: {} for s in self.STREAMS}
        for idx, op in enumerate(ops):
            E = op.stream
            deps = op.deps
            if op.dma:
                j = dcount[E]
                if j >= self.R:
                    deps[dlist[E][j - self.R]] = True
                op.dslot = j % self.R
                op.dval = 16 * (j // self.R + 1)
                dcount[E] = j + 1
                dlist[E].append(idx)
            need = []
            for p in sorted(deps):
                P = ops[p]
                if P.dma:
                    key = (P.stream, P.dslot)
                    if waited_dma[E].get(key, 0) >= P.dval:
                        continue
                    waited_dma[E][key] = P.dval
                    need.append(("d", P.stream, P.dslot, P.dval))
                else:
                    if P.fn is None:
                        continue
                    Ep = P.stream
                    if Ep == E and E == "pe":
                        continue
                    if waited_pos[E].get(Ep, -1) >= P.pos:
                        continue
                    waited_pos[E][Ep] = P.pos
                    P.signal = True
                    need.append(("c", p))
            op.waits = need
            op.pos = pos[E]
            pos[E] += 1
        cnt = {s: 0 for s in self.STREAMS}
        for op in ops:
            if (not op.dma) and op.signal:
                cnt[op.stream] += 1
                op.sigval = cnt[op.stream]
        nwait = 0
        for op in ops:
            eng = self.eng[op.stream]
            for wt in op.waits:
                if wt[0] == "c":
                    P = ops[wt[1]]
                    eng.wait_ge(self.sem[P.stream], P.sigval)
                else:
                    eng.wait_ge(self.dsem[wt[1]][wt[2]], wt[3])
                nwait += 1
            if op.fn is None:
                continue
            ins = op.fn()
            if op.dma:
                ins.then_inc(self.dsem[op.stream][op.dslot], 16)
            elif op.signal:
                ins.then_inc(self.sem[op.stream], 1)
        self.stats = dict(cnt=dict(cnt), dma={q: len(v) for q, v in dlist.items()}, nops=len(ops), nwait=nwait)
        if os.environ.get("PROG_STATS"):
            print("PROG_STATS", self.stats, flush=True)
        return len(ops), nwait


class Arena:
    def __init__(self, nc, es, nbytes):
        self.t = es.enter_context(nc.sbuf_tensor("arena", [128, nbytes // 4], F32))
        self.size = nbytes
        self.off = 0

    def alloc(self, shape, dtype):
        n = 1
        for s in shape:
            n *= s
        nb = n * (2 if dtype == BF16 else 4)
        nb_al = (nb + 63) // 64 * 64
        assert self.off + nb_al <= self.size, ("arena overflow", self.off, nb_al, self.size)
        a = self.t[:, self.off // 4:(self.off + nb_al) // 4]
        self.off += nb_al
        if dtype == BF16:
            a = a.bitcast(BF16)
        a = a[:, 0:n]
        if len(shape) > 1:
            names = ["d%d" % i for i in range(len(shape))]
            pat = "p (" + " ".join(names) + ") -> p " + " ".join(names)
            a = a.rearrange(pat, **{nm: s for nm, s in zip(names[:-1], shape[:-1])})
        return a


def build_program(NG, STOP=99):
    SEQ = NG * 1024
    NKB = SEQ // 128
    NSTEP = NKB // 4
    nc = bass.Bass("TRN2", target_bir_lowering=False)

    def din(name, shape, dt=F32):
        return nc.dram_tensor(name, list(shape), dt, kind="ExternalInput").ap()

    def dout(name, shape, dt=F32):
        return nc.dram_tensor(name, list(shape), dt, kind="ExternalOutput").ap()

    def dint(name, shape, dt=BF16):
        return nc.dram_tensor(name, list(shape), dt, kind="Internal").ap()

    x_all = din("x_all", [2, SEQ, D_MODEL])
    x_own = din("x_own", [2, NG, 128, D_MODEL])
    x_halo = din("x_halo", [2, NG, 32, D_MODEL])
    x_s = din("x_s", [4, 64, D_MODEL])
    ckv_cache = din("ckv_cache", [4, 1024, 128])
    kr_cache = din("kr_cache", [4, 1024, 32])
    pool_state = din("pool_state", [4, 15, 512])
    w_in = din("w_in", [D_MODEL, IN_TOTAL])
    w_uq = din("w_uq", [Q_LORA, 768])
    w_ukT = din("w_ukT", [128, 8, 128])
    w_uvp = din("w_uvp", [128, 8, 128])
    w_attn_up = din("w_attn_up", [512, D_MODEL])
    w_pool = din("w_pool", [4, 128, 256])
    w_o = din("w_o", [D_MODEL, D_MODEL])
    w_gate = din("w_gate", [D_MODEL, D_FF])
    w_up = din("w_up", [D_MODEL, D_FF])
    w_down = din("w_down", [D_FF, D_MODEL])
    gq = din("gq", [1, Q_LORA])
    gkv = din("gkv", [1, KV_LORA])
    b_gateT = din("b_gateT", [128, 16])
    pool_scaleT = din("pool_scaleT", [128, 8])
    lnv = din("lnv", [4, D_MODEL])
    tabK = din("tabK", [2, 128, NKB, 32])
    tabO = din("tabO", [2, 128, NG, 32])
    tabS = din("tabS", [2, 64, 32])
    ident_d = din("ident", [128, 128], BF16)
    maskK_d = din("maskK", [128, 1024], BF16)
    maskQ_d = din("maskQ", [128, 512], BF16)
    rowmask_d = din("rowmask", [128, 4])
    band_d = din("band", [2, 128, 4, 128], BF16)
    bandH_d = din("bandH", [128, 4, 4, 128], BF16)

    y_own = dout("y_own", [2, NG, 128, D_MODEL])
    y_s = dout("y_s", [4, 64, D_MODEL])
    ckv_own = dout("ckv_own", [2, NG, 128, 128])
    kr_own = dout("kr_own", [2, NG, 128, 32])
    pool_p = dout("pool_p", [2, 64, 512])
    ckv_s = dout("ckv_s", [4, 64, 128])
    kr_s = dout("kr_s", [4, 64, 32])
    pool_s = dout("pool_s", [4, 32, 512])

    wb_in = dint("wb_in", [D_MODEL, IN_TOTAL])
    wb_attn_up = dint("wb_attn_up", [512, D_MODEL])
    wb_pool = dint("wb_pool", [4, 128, 256])
    wb_o = dint("wb_o", [D_MODEL, D_MODEL])
    wb_gate = dint("wb_gate", [D_MODEL, D_FF])
    wb_up = dint("wb_up", [D_MODEL, D_FF])
    wb_down = dint("wb_down", [D_FF, D_MODEL])

    es = ExitStack()
    with es:
        P = Prog(nc, es)
        ar = Arena(nc, es, 207 * 1024)
        ps = es.enter_context(nc.psum_tensor("ps", [128, 8, 512], F32))
        psf = ps[:, :, :]
        psb = psf.bitcast(BF16)

        def BK(j):
            return ("ps", j)

        ident = ar.alloc([128], BF16)
        w_in_a = ar.alloc([KC, 416], BF16)
        w_uq_sb = ar.alloc([2, 768], BF16)
        w_ukT_sb = ar.alloc([8, 128], BF16)
        gq_bc = ar.alloc([Q_LORA], F32)
        gkv_bc = ar.alloc([KV_LORA], F32)
        tabO_sb = ar.alloc([2, NG, 32], F32)
        tabS_sb = ar.alloc([2, 32], F32)
        maskK = ar.alloc([1024], BF16)
        maskQ = ar.alloc([512], BF16)
        mhalf = ar.alloc([8], F32)
        rowmask = ar.alloc([4], F32)
        small = ar.alloc([64], F32)
        junk = ar.alloc([1024], F32)
        b_gate_sb = ar.alloc([16], F32)
        pool_scale_sb = ar.alloc([8], F32)

        def dma(q, out, in_, r=(), w=()):
            eng = P.eng[q]
            return P.add(q, lambda: eng.dma_start(out=out, in_=in_), r=r, w=w, dma=True)

        RES = "res"
        dma("sp", ident, ident_d, w=[RES])
        dma("pool", w_in_a, w_in[:, 0:416].rearrange("(k p) n -> p k n", p=128), w=[RES])
        dma("pool", w_uq_sb, w_uq.rearrange("(k p) n -> p k n", p=128), w=[RES])
        dma("pool", w_ukT_sb, w_ukT, w=[RES])
        dma("sp", gq_bc, gq.to_broadcast([128, Q_LORA]), w=[RES])
        dma("sp", gkv_bc, gkv.to_broadcast([128, KV_LORA]), w=[RES])
        dma("sp", tabO_sb, tabO.rearrange("c p s j -> p c s j"), w=[RES])
        dma("sp", tabS_sb[0:64], tabS.rearrange("c p j -> p c j"), w=[RES])
        dma("sp", maskK, maskK_d, w=[RES])
        dma("sp", maskQ, maskQ_d, w=[RES])
        dma("sp", rowmask, rowmask_d, w=[RES])
        dma("sp", b_gate_sb, b_gateT, w=[RES])
        dma("sp", pool_scale_sb, pool_scaleT, w=[RES])
        P.add("dve", lambda: nc.vector.memset(mhalf, -0.5), w=[RES])

        def cast_w(dst, src, split):
            if split > 1:
                dst = dst.rearrange("r (a c) -> (r a) c", a=split)
                src = src.rearrange("r (a c) -> (r a) c", a=split)
            dma("pool", dst, src, w=["wscratch"])

        cast_list = [
            lambda: cast_w(wb_in, w_in, 2),
            lambda: cast_w(wb_attn_up, w_attn_up, 1),
            lambda: cast_w(wb_pool.rearrange("g a b -> (g a) b"), w_pool.rearrange("g a b -> (g a) b"), 1),
            lambda: cast_w(wb_o, w_o, 1),
            lambda: cast_w(wb_gate, w_gate, 2),
            lambda: cast_w(wb_up, w_up, 2),
            lambda: cast_w(wb_down, w_down, 1),
        ]

        stash_mark = ar.off

        def transposes(src_fn, n, nt, bank, dst, dst_key, src_keys, evac="dve"):
            def f():
                ins = None
                for i in range(n):
                    ins = nc.tensor.transpose(psb[:, bank, i * 128:(i + 1) * 128], src_fn(i), ident)
                return ins
            P.add("pe", f, r=list(src_keys) + [RES], w=[BK(bank)])
            src = psb[:, bank, 0:n * 128].rearrange("p (a t) -> p a t", a=n)[:, :, 0:nt]
            if evac == "act":
                P.add("act", lambda: nc.scalar.copy(out=dst, in_=src), r=[BK(bank)], w=[dst_key])
            else:
                P.add("dve", lambda: nc.vector.tensor_copy(out=dst, in_=src), r=[BK(bank)], w=[dst_key])

        def rstd_from_ss(ss, n_el, eps, ncol, nt, key):
            P.add("dve", lambda: nc.vector.tensor_scalar(out=ss, in0=ss, scalar1=1.0 / n_el, scalar2=eps,
                                                         op0=ALU.mult, op1=ALU.add), r=[key], w=[key])
            P.add("pool", lambda: nc.gpsimd.tensor_tensor(out=ss, in0=ss, in1=mhalf[0:nt, 0:ncol], op=ALU.pow),
                  r=[key, RES], w=[key])

        def rope_ops(zr, cos2, sin2, out, nt, r, w, tmpA, tmpB):
            P.add("dve", lambda: nc.vector.tensor_tensor(out=tmpA, in0=zr, in1=cos2, op=ALU.mult), r=r, w=["ropeA"])
            P.add("dve", lambda: nc.vector.tensor_tensor(out=tmpB[:, :, 0:16], in0=zr[:, :, 16:32], in1=sin2[:, :, 0:16],
                                                         op=ALU.mult), r=r, w=["ropeB0"])
            P.add("dve", lambda: nc.vector.tensor_tensor(out=tmpB[:, :, 16:32], in0=zr[:, :, 0:16], in1=sin2[:, :, 16:32],
                                                         op=ALU.mult), r=r, w=["ropeB1"])
            P.add("dve", lambda: nc.vector.tensor_tensor(out=out, in0=tmpA, in1=tmpB, op=ALU.add),
                  r=["ropeA", "ropeB0", "ropeB1"], w=w)

        att_ctr = {"t": 0}
        NSB = 3
        NPB = 4

        def attention(nq, groups, key_blocks, Pbuf, att_tm, rl, stash_dst, stash_key, qkeys, inter=None, inter_n=0):
            tiles = [(kb, g) for kb in key_blocks for g in groups]
            LAG = 2
            per_tile = 0
            if inter is not None:
                per_tile = min(3, max(1, -(-inter_n // max(1, len(tiles) - 4))))
            first_kb = key_blocks[0]
            last_kb = key_blocks[-1]
            started = set()
            info = []
            for i in range(len(tiles) + LAG):
                if i < len(tiles):
                    kb, g = tiles[i]
                    t = att_ctr["t"]
                    att_ctr["t"] += 1
                    sb = 3 + (t % NSB)
                    pb_i = t % NPB
                    nk = kb["nk"]
                    N = g["N"]
                    info.append((sb, pb_i))

                    def fS(kb=kb, g=g, sb=sb, nk=nk, N=N):
                        nc.tensor.matmul(psf[0:nk, sb, 0:N], lhsT=kb["ckvT"], rhs=g["lat"], start=True, stop=False)
                        last = kb["mask"] is None
                        ins = nc.tensor.matmul(psf[0:nk, sb, 0:N], lhsT=kb["krT"], rhs=g["rope"](kb["pb"]),
                                               start=False, stop=last)
                        if not last:
                            ins = nc.tensor.matmul(psf[0:nk, sb, 0:N], lhsT=kb["mask"], rhs=maskQ[:, 0:N],
                                                   start=False, stop=True)
                        return ins
                    P.add("pe", fS, r=list(kb["keys"]) + list(qkeys) + [RES], w=[BK(sb)])
                    P.add("act", lambda sb=sb, pb_i=pb_i, nk=nk, N=N: nc.scalar.activation(
                        out=Pbuf[0:nk, pb_i, 0:N], in_=psf[0:nk, sb, 0:N], func=AF.Exp, scale=ATTN_SCALE),
                        r=[BK(sb)], w=[("P", pb_i)])
                if i >= LAG:
                    kb, g = tiles[i - LAG]
                    sb, pb_i = info[i - LAG]
                    nk = kb["nk"]

                    def fPV(kb=kb, g=g, pb_i=pb_i, nk=nk):
                        ins = None
                        for (h, col) in g["heads"]:
                            bk = h // 3
                            st = (kb is first_kb) and (bk not in started)
                            if st:
                                started.add(bk)
                            c0 = (h % 3) * OW
                            ins = nc.tensor.matmul(psf[0:nq, bk, c0:c0 + 129], lhsT=Pbuf[:, pb_i, col:col + nq],
                                                   rhs=kb["V"], start=st, stop=(kb is last_kb),
                                                   skip_group_check=True)
                        return ins
                    P.add("pe", fPV, r=[("P", pb_i)] + list(kb["keys"]), w=[BK(0), BK(1), BK(2)])
                if inter is not None and i >= 2:
                    for _ in range(per_tile):
                        next(inter, None)
            for bk, (h0, nh) in enumerate([(0, 3), (3, 3), (6, 2)]):
                ov = psf[0:nq, bk, 0:nh * OW].rearrange("p (h c) -> p h c", c=OW)
                P.add("dve", lambda ov=ov, h0=h0, nh=nh: nc.vector.reciprocal(
                    out=rl[0:nq, h0:h0 + nh].unsqueeze(2), in_=ov[:, :, 128:129]), r=[BK(bk)], w=[("rl", bk)])
                P.add("dve", lambda ov=ov, h0=h0, nh=nh: nc.vector.tensor_tensor(
                    out=att_tm[0:nq, h0:h0 + nh, :], in0=ov[:, :, 0:128],
                    in1=rl[0:nq, h0:h0 + nh].unsqueeze(2).to_broadcast([nq, nh, 128]), op=ALU.mult),
                    r=[BK(bk), ("rl", bk)], w=[("att_tm", bk)])
            transposes(lambda h: att_tm[:, h, :], 8, nq, 7, stash_dst, stash_key,
                       [("att_tm", 0), ("att_tm", 1), ("att_tm", 2)])

        def q_stage(x_src, nt, cos2, sin2, ckv_dst, kr_dst, W, qb, sample_kv=None):
            xb = W["xb"]; xT = W["xT"]; QlatT = W["QlatT"][qb]; QropeT = W["QropeT"][qb]
            tag = ("q", qb)
            dma("pool", xb[0:nt], x_src, w=["q_xb"])
            transposes(lambda k: xb[:, k * 128:(k + 1) * 128], 8, nt, 7, xT[:, :, 0:nt], "q_xT", ["q_xb"])
            yield
            def fz():
                ins = None
                for k in range(KC):
                    ins = nc.tensor.matmul(psf[0:nt, 6, 0:416], lhsT=xT[:, k, 0:nt], rhs=w_in_a[:, k, :],
                                           start=(k == 0), stop=(k == KC - 1))
                return ins
            P.add("pe", fz, r=["q_xT", RES], w=[BK(6)])
            yield
            if QSTOP < 1:
                return
            ss = small[0:nt, 0:2]
            P.add("act", lambda: nc.scalar.activation(out=junk[0:nt, 0:256], in_=psf[0:nt, 6, 0:256], func=AF.Square,
                                                      accum_out=small[0:nt, 0:1]), r=[BK(6)], w=["q_ss0", "junk"])
            yield
            P.add("act", lambda: nc.scalar.activation(out=junk[0:nt, 0:128], in_=psf[0:nt, 6, 256:384], func=AF.Square,
                                                      accum_out=small[0:nt, 1:2]), r=[BK(6)], w=["q_ss1", "junk"])
            yield
            P.add("dve", lambda: nc.vector.tensor_scalar(out=small[0:nt, 0:1], in0=small[0:nt, 0:1], scalar1=1.0 / 256,
                                                         scalar2=RMS_EPS, op0=ALU.mult, op1=ALU.add),
                  r=["q_ss0"], w=["q_ss0"])
            yield
            P.add("dve", lambda: nc.vector.tensor_scalar(out=small[0:nt, 1:2], in0=small[0:nt, 1:2], scalar1=1.0 / 128,
                                                         scalar2=RMS_EPS, op0=ALU.mult, op1=ALU.add),
                  r=["q_ss1"], w=["q_ss1"])
            yield
            P.add("pool", lambda: nc.gpsimd.tensor_tensor(out=ss, in0=ss, in1=mhalf[0:nt, 0:2], op=ALU.pow),
                  r=["q_ss0", "q_ss1", RES], w=["q_ss0", "q_ss1"])
            yield
            if QSTOP < 2:
                return
            zqn = W["zqn"]
            P.add("dve", lambda: nc.vector.scalar_tensor_tensor(out=zqn[0:nt], in0=psf[0:nt, 6, 0:256], scalar=small[0:nt, 0:1],
                                                                in1=gq_bc[0:nt], op0=ALU.mult, op1=ALU.mult),
                  r=[BK(6), "q_ss0", RES], w=["q_zqn"])
            yield
            ckv32 = W["ckv32"]
            P.add("dve", lambda: nc.vector.scalar_tensor_tensor(out=ckv32[0:nt], in0=psf[0:nt, 6, 256:384], scalar=small[0:nt, 1:2],
                                                                in1=gkv_bc[0:nt], op0=ALU.mult, op1=ALU.mult),
                  r=[BK(6), "q_ss1", RES], w=["q_ckv32"])
            yield
            dma("sp", ckv_dst, ckv32[0:nt], r=["q_ckv32"])
            yield
            if QSTOP < 3:
                return
            kr32 = W["kr32"]
            rope_ops(psf[0:nt, 6, 384:416].unsqueeze(1), cos2.unsqueeze(1), sin2.unsqueeze(1), kr32[0:nt].unsqueeze(1), nt,
                     [BK(6), RES], ["q_kr32"], W["rA"][0:nt, 0:1, :], W["rB"][0:nt, 0:1, :])
            yield
            dma("sp", kr_dst, kr32[0:nt], r=["q_kr32"])
            yield
            if sample_kv is not None:
                sample_kv(ckv32, kr32)
            if QSTOP < 4:
                return
            zqnT = W["zqnT"]
            transposes(lambda k: zqn[:, k * 128:(k + 1) * 128], 2, nt, 7, zqnT[:, :, 0:nt], "q_zqnT", ["q_zqn"])
            yield
            qn = W["qn"]; qr = W["qr"]
            for half in range(2):
                def fq(half=half):
                    ins = None
                    for k in range(2):
                        ins = nc.tensor.matmul(psf[0:nt, 6, 0:384], lhsT=zqnT[:, k, 0:nt],
                                               rhs=w_uq_sb[:, k, half * 384:(half + 1) * 384], start=(k == 0), stop=(k == 1))
                    return ins
                P.add("pe", fq, r=["q_zqnT", RES], w=[BK(6)])
                yield
                qv = psf[0:nt, 6, 0:384].rearrange("p (h c) -> p h c", c=96)
                P.add("dve", lambda qv=qv, half=half: nc.vector.tensor_copy(out=qn[0:nt, half * 4:(half + 1) * 4, :], in_=qv[:, :, 0:64]),
                      r=[BK(6)], w=[("q_qn", half)])
                yield
                rope_ops(qv[:, :, 64:96], cos2.unsqueeze(1).to_broadcast([nt, 4, 32]), sin2.unsqueeze(1).to_broadcast([nt, 4, 32]),
                         W["qr32"][0:nt, half * 4:(half + 1) * 4, :], nt, [BK(6), RES], [("q_qr32", half)],
                         W["rA"][0:nt, 0:4, :], W["rB"][0:nt, 0:4, :])
            if QSTOP < 5:
                return
            P.add("dve", lambda: nc.vector.tensor_copy(out=qr[0:nt], in_=W["qr32"][0:nt].unsqueeze(2).to_broadcast([nt, 8, 4, 32])),
                  r=[("q_qr32", 0), ("q_qr32", 1)], w=["q_qr"])
            yield
            if QSTOP < 6:
                return
            qnT = W["qnT"]
            transposes(lambda j: qn[:, 2 * j:2 * j + 2, :].rearrange("p a b -> p (a b)"), 4, nt, 7, qnT[:, :, 0:nt], "q_qnT",
                       [("q_qn", 0), ("q_qn", 1)])
            yield
            if QSTOP < 7:
                return
            for half in range(2):
                def fl(half=half):
                    ins = None
                    for hh in range(4):
                        h = half * 4 + hh
                        pb = (h % 2) * 64
                        ins = nc.tensor.matmul(psf[:, 6, hh * nt:(hh + 1) * nt], lhsT=w_ukT_sb[:, h, :],
                                               rhs=qnT[:, h // 2, 0:nt], start=True, stop=True)
                    return ins
                P.add("pe", fl, r=["q_qnT", RES], w=[BK(6)])
                yield
                P.add("dve", lambda half=half: nc.vector.tensor_copy(out=QlatT[:, half * 4 * nt:(half + 1) * 4 * nt],
                                                                     in_=psf[:, 6, 0:4 * nt]),
                      r=[BK(6)], w=[(tag, "QlatT")])
            if QSTOP < 8:
                return
            def ftq():
                ins = None
                for h in range(8):
                    ins = nc.tensor.transpose(psb[:, 7, h * 128:(h + 1) * 128], qr[:, h, :, :].rearrange("p a b -> p (a b)"), ident)
                return ins
            P.add("pe", ftq, r=["q_qr", RES], w=[BK(7)])
            yield
            for m in range(4):
                P.add("dve", lambda m=m: nc.vector.tensor_scalar(
                    out=QropeT[:, m, 0:8 * nt].rearrange("p (h t) -> p h t", h=8),
                    in0=psb[:, 7, :].rearrange("p (h t) -> p h t", h=8)[:, :, 0:nt],
                    scalar1=rowmask[:, m:m + 1], scalar2=None, op0=ALU.mult),
                      r=[BK(7), RES], w=[(tag, "QropeT")])

        wctr = {"n": 0}

        def d_head(NB, nt, x_blk_src, xh_src, W, bank):
            xT = W["xT"]; xb = W["xb"]
            for i in range(NB):
                dma("pool", xb[0:nt, i % 2, :], x_blk_src(i), w=[("d_xb", i % 2)])
                transposes(lambda k, i=i: xb[:, i % 2, k * 128:(k + 1) * 128], 8, nt, bank, xT[:, :, i * nt:(i + 1) * nt],
                           ("d_xT", i), [("d_xb", i % 2)], evac="act")
            if xh_src is not None:
                xhb = W["xhb"]; xhT = W["xhT"]
                dma("pool", xhb, xh_src, w=["d_xhb"])
                transposes(lambda k: xhb[:, k * 128:(k + 1) * 128], 8, 128, bank, xhT, "d_xhT", ["d_xhb"], evac="act")

        def d_stage(NB, nt, x_blk_src, xh_src, attT, band_sel, y_dst, pool_dst, W, halo_direct=None, head_done=False,
                    next_head=None):
            T = NB * nt
            if DSTOP < -1:
                return
            xT = W["xT"]; xb = W["xb"]
            if not head_done:
                d_head(NB, nt, x_blk_src, xh_src, W, 7)
            xTk = [("d_xT", i) for i in range(NB)]
            if DSTOP < 0:
                return
            uh = W["uh"]
            wu = W["wA"][0]
            dma("sp", wu, wb_in[:, 416:928].rearrange("(k p) n -> p k n", p=128), w=[("d_wA", 0)])
            if xh_src is not None:
                xhT = W["xhT"]
                def fuh():
                    ins = None
                    for k in range(KC):
                        ins = nc.tensor.matmul(psf[:, 0, :], lhsT=xhT[:, k, :], rhs=wu[:, k, :], start=(k == 0), stop=(k == KC - 1))
                    return ins
                P.add("pe", fuh, r=["d_xhT", ("d_wA", 0)], w=[BK(0)])
                P.add("act", lambda: nc.scalar.copy(out=uh, in_=psf[:, 0, :]), r=[BK(0)], w=["d_uh"])
            else:
                halo_direct(uh)
            u_tm = W["u_tm"]; dT = W["dT"]; ulast = W["ulast"]
            for i in range(NB):
                bk = 1 + (i % 2)
                def fu(i=i, bk=bk):
                    ins = None
                    for k in range(KC):
                        ins = nc.tensor.matmul(psf[0:nt, bk, :], lhsT=xT[:, k, i * nt:(i + 1) * nt], rhs=wu[:, k, :],
                                               start=(k == 0), stop=(k == KC - 1))
                    return ins
                P.add("pe", fu, r=[("d_xT", i), ("d_wA", 0)], w=[BK(bk)])
                P.add("act", lambda i=i, bk=bk: nc.scalar.copy(out=u_tm[0:nt, i, :], in_=psf[0:nt, bk, :]), r=[BK(bk)],
                      w=[("d_u", i)])
                for (pi, dst) in pool_dst:
                    if pi == i and not os.environ.get("NOPOOLDST"):
                        lo = nt // 2
                        P.add("act", lambda bk=bk, lo=lo: nc.scalar.copy(out=ulast[0:nt, :], in_=psf[0:nt, bk, :]),
                              r=[BK(bk)], w=["d_ulast"])
                        if not os.environ.get("NOPOOLDMA"):
                            dma("sp", dst, ulast[lo:nt, :], r=["d_ulast"], w=["d_ulast_out"])
            if DSTOP < 1:
                return
            bandt = W["band"]; bandH = W["bandH"]
            for i in range(NB):
                bk = 3 + (i % 2)
                def fd(i=i, bk=bk):
                    ins = None
                    for g in range(4):
                        nc.tensor.matmul(psf[:, bk, g * nt:(g + 1) * nt], lhsT=u_tm[:, i, g * 128:(g + 1) * 128],
                                         rhs=bandt[:, band_sel(i), g, 0:nt], start=(g == 0), stop=False,
                                         skip_group_check=True)
                        ins = nc.tensor.matmul(psf[:, bk, g * nt:(g + 1) * nt], lhsT=uh[:, g * 128:(g + 1) * 128],
                                               rhs=bandH[:, i, g, 0:nt], start=False, stop=True,
                                               skip_group_check=True)
                    return ins
                P.add("pe", fd, r=[("d_u", i), "d_uh", "d_band"], w=[BK(bk)])
                P.add("dve", lambda i=i, bk=bk: nc.vector.tensor_copy(
                    out=dT[:, :, i * nt:(i + 1) * nt], in_=psf[:, bk, 0:4 * nt].rearrange("p (g t) -> p g t", g=4)),
                    r=[BK(bk)], w=[("d_dT", i)])
            dTk = [("d_dT", i) for i in range(NB)]
            if DSTOP < 2:
                return
            wuv = W["wuv"]; oT = W["oT"]
            for j in range(4):
                bk = 5 + (j % 2)
                def fo(j=j, bk=bk):
                    nc.tensor.matmul(psf[:, bk, 0:T], lhsT=wuv[:, 2 * j, :], rhs=attT[:, 2 * j, :], start=True, stop=False)
                    return nc.tensor.matmul(psf[:, bk, 0:T], lhsT=wuv[:, 2 * j + 1, :], rhs=attT[:, 2 * j + 1, :], start=False, stop=True)
                P.add("pe", fo, r=["d_attT", "d_wsm"], w=[BK(bk)])
                P.add("dve", lambda j=j, bk=bk: nc.vector.tensor_copy(out=oT[:, j, 0:T], in_=psf[:, bk, 0:T]), r=[BK(bk)],
                      w=[("d_oT", j)])
            oTk = [("d_oT", j) for j in range(4)]
            if DSTOP < 3:
                return
            wau = W["wau"]; wpl = W["wpl"]; mT = W["mT"]
            gsb = W["gsb"]; t1 = W["t1"]; t2 = W["t2"]
            for cc in range(8):
                par = cc % 2
                if cc % 4 == 0:
                    ia, ip = (1, 2) if cc == 0 else (0, 1)
                    wga = W["wA"][ia]
                    wgp = W["wA"][ip]
                    c0 = 928 + cc * 128
                    kga = ("d_wA", ia); kgp = ("d_wA", ip)
                    dma("sp", wga, wb_in[:, c0:c0 + 512].rearrange("(k p) n -> p k n", p=128), w=[kga])
                    dma("sp", wgp, wb_in[:, c0 + 1024:c0 + 1536].rearrange("(k p) n -> p k n", p=128), w=[kgp])
                co = (cc % 4) * 128
                for which, wt, wk, bk in ((0, wga, kga, 0 + par), (1, wgp, kgp, 2 + par)):
                    def fg(wt=wt, bk=bk, co=co):
                        ins = None
                        for k in range(KC):
                            ins = nc.tensor.matmul(psf[:, bk, 0:T], lhsT=wt[:, k, co:co + 128], rhs=xT[:, k, 0:T],
                                                   start=(k == 0), stop=(k == KC - 1))
                        return ins
                    P.add("pe", fg, r=xTk + [wk], w=[BK(bk)])
                    P.add("act", lambda which=which, bk=bk, cc=cc, par=par: nc.scalar.activation(
                        out=gsb[:, which, par, 0:T], in_=psf[:, bk, 0:T], func=AF.Sigmoid,
                        bias=b_gate_sb[:, which * 8 + cc:which * 8 + cc + 1], scale=1.0),
                        r=[BK(bk), RES], w=[("d_g", which, par)])
                bkA = 4 + par
                def fA(cc=cc, bkA=bkA):
                    ins = None
                    for j in range(4):
                        ins = nc.tensor.matmul(psf[:, bkA, 0:T], lhsT=wau[:, j, cc * 128:(cc + 1) * 128], rhs=oT[:, j, 0:T],
                                               start=(j == 0), stop=(j == 3))
                    return ins
                P.add("pe", fA, r=oTk + ["d_wsm"], w=[BK(bkA)])
                bkB = 6 + par
                g = cc // 2
                P.add("pe", lambda cc=cc, bkB=bkB, g=g: nc.tensor.matmul(
                    psf[:, bkB, 0:T], lhsT=wpl[:, g, (cc % 2) * 128:(cc % 2) * 128 + 128], rhs=dT[:, g, 0:T], start=True, stop=True),
                    r=dTk + ["d_wsm"], w=[BK(bkB)])
                P.add("dve", lambda bkA=bkA, par=par: nc.vector.tensor_tensor(out=t1[:, par, 0:T], in0=psf[:, bkA, 0:T],
                                                                              in1=gsb[:, 0, par, 0:T], op=ALU.mult),
                      r=[BK(bkA), ("d_g", 0, par)], w=[("d_t1", par)])
                P.add("dve", lambda bkB=bkB, par=par, cc=cc: nc.vector.scalar_tensor_tensor(
                    out=t2[:, par, 0:T], in0=psf[:, bkB, 0:T], scalar=pool_scale_sb[:, cc:cc + 1], in1=gsb[:, 1, par, 0:T],
                    op0=ALU.mult, op1=ALU.mult), r=[BK(bkB), ("d_g", 1, par), RES], w=[("d_t2", par)])
                P.add("dve", lambda par=par, cc=cc: nc.vector.tensor_tensor(out=mT[:, cc, 0:T], in0=t1[:, par, 0:T],
                                                                            in1=t2[:, par, 0:T], op=ALU.add),
                      r=[("d_t1", par), ("d_t2", par)], w=[("d_mT", cc)])
            mTk = [("d_mT", cc) for cc in range(8)]
            if DSTOP < 4:
                return
            wo = [W["wA"][2], W["wA"][0]]
            wok = [("d_wA", 2), ("d_wA", 0)]
            for half in range(2):
                dma("sp", wo[half], wb_o[:, half * 512:(half + 1) * 512].rearrange("(k p) n -> p k n", p=128), w=[wok[half]])
            h32 = W["h32"]; hb = W["hb"]; hT = W["hT"]; x32 = W["x32"]; lnc = W["lnc"]
            stats = W["stats"]; mv = W["mv"]

            def ln_a(i, b0, resid, resid_key, v, out_key):
                st = stats[0:nt, i]; m = mv[0:nt, i]
                P.add("dve", lambda: nc.vector.scalar_tensor_tensor(
                    out=v, in0=resid, scalar=float(ALPHA), in1=psf[0:nt, b0:b0 + 2, :].rearrange("p a b -> p (a b)"),
                    op0=ALU.mult, op1=ALU.add), r=[BK(b0), BK(b0 + 1), resid_key], w=[out_key])
                for c in range(2):
                    P.add("dve", lambda c=c: nc.vector.bn_stats(out=st[:, c, :], in_=v[:, c * 512:(c + 1) * 512]),
                          r=[out_key], w=[("d_stats", i, c)])
                P.add("dve", lambda: nc.vector.bn_aggr(out=m[:, 0:2], in_=st.rearrange("p a b -> p (a b)")),
                      r=[("d_stats", i, 0), ("d_stats", i, 1)], w=[("d_mv", i)])
                P.add("dve", lambda: nc.vector.tensor_scalar(out=m[:, 2:3], in0=m[:, 1:2], scalar1=LN_EPS, scalar2=None,
                                                             op0=ALU.add), r=[("d_mv", i)], w=[("d_mv2", i)])
                P.add("pool", lambda: nc.gpsimd.tensor_tensor(out=m[:, 3:4], in0=m[:, 2:3], in1=mhalf[0:nt, 0:1], op=ALU.pow),
                      r=[("d_mv2", i), RES], w=[("d_rstd", i)])

            def ln_b(i, gi, v, out_key):
                m = mv[0:nt, i]
                P.add("dve", lambda: nc.vector.tensor_scalar(out=m[:, 4:5], in0=m[:, 0:1], scalar1=m[:, 3:4], scalar2=-1.0,
                                                             op0=ALU.mult, op1=ALU.mult), r=[("d_mv", i), ("d_rstd", i)], w=[("d_nmr", i)])
                P.add("act", lambda: nc.scalar.activation(out=v, in_=v, func=AF.Identity, bias=m[:, 4:5], scale=m[:, 3:4]),
                      r=[out_key, ("d_nmr", i), ("d_rstd", i)], w=[out_key])
                P.add("dve", lambda: nc.vector.tensor_tensor(out=v, in0=v, in1=lnc[0:nt, gi, :], op=ALU.mult),
                      r=[out_key, "d_lnc"], w=[out_key])
                P.add("dve", lambda: nc.vector.tensor_tensor(out=v, in0=v, in1=lnc[0:nt, gi + 1, :], op=ALU.add),
                      r=[out_key, "d_lnc"], w=[out_key])

            def emit_fh(i):
                b0 = 2 * (i % 3)
                def fh(i=i, b0=b0):
                    ins = None
                    for half in range(2):
                        for cc in range(8):
                            ins = nc.tensor.matmul(psf[0:nt, b0 + half, :], lhsT=mT[:, cc, i * nt:(i + 1) * nt],
                                                   rhs=wo[half][:, cc, :], start=(cc == 0), stop=(cc == 7))
                    return ins
                P.add("pe", fh, r=mTk + wok, w=[BK(b0), BK(b0 + 1)])

            def emit_ln1a(i):
                b0 = 2 * (i % 3)
                dma("sp", x32[0:nt, i % 2, :], x_blk_src(i), w=[("d_x32", i % 2)])
                ln_a(i, b0, x32[0:nt, i % 2, :], ("d_x32", i % 2), h32[0:nt, i, :], ("d_h32", i))

            def emit_ln1b(i):
                ln_b(i, 0, h32[0:nt, i, :], ("d_h32", i))
                P.add("act", lambda i=i: nc.scalar.copy(out=hb[0:nt, i % 2, :], in_=h32[0:nt, i, :]), r=[("d_h32", i)],
                      w=[("d_hb", i % 2)])

            def emit_tr(i):
                transposes(lambda k, i=i: hb[:, i % 2, k * 128:(k + 1) * 128], 8, nt, 7, hT[:, :, i * nt:(i + 1) * nt],
                           ("d_hT", i), [("d_hb", i % 2)], evac="act")

            emit_fh(0)
            if NB > 1:
                emit_fh(1)
            for i in range(NB + 2):
                if i < NB:
                    emit_ln1a(i)
                if i + 2 < NB:
                    emit_fh(i + 2)
                if 1 <= i <= NB:
                    emit_ln1b(i - 1)
                if 2 <= i:
                    emit_tr(i - 2)
            hTk = [("d_hT", i) for i in range(NB)]
            P.barrier()
            if DSTOP < 5:
                return
            aT = W["aT"]; sl = W["sl"]
            for ft in range(D_FF // 256):
                wi = ft % 2
                wg = W["wG"][wi]; wu_ = W["wU"][wi]
                dma("sp", wg, wb_gate[:, ft * 256:(ft + 1) * 256].rearrange("(k p) n -> p k n", p=128), w=[("d_wG", wi)])
                dma("sp", wu_, wb_up[:, ft * 256:(ft + 1) * 256].rearrange("(k p) n -> p k n", p=128), w=[("d_wU", wi)])
                for sub in range(2):
                    fc = ft * 2 + sub
                    par = fc % 2
                    for which, wt, wk, bk in ((0, wg, ("d_wG", wi), 0 + par), (1, wu_, ("d_wU", wi), 2 + par)):
                        def fgu(wt=wt, bk=bk, sub=sub):
                            ins = None
                            for k in range(KC):
                                ins = nc.tensor.matmul(psf[:, bk, 0:T], lhsT=wt[:, k, sub * 128:(sub + 1) * 128], rhs=hT[:, k, 0:T],
                                                       start=(k == 0), stop=(k == KC - 1))
                            return ins
                        P.add("pe", fgu, r=hTk + [wk], w=[BK(bk)])
                    P.add("act", lambda par=par: nc.scalar.activation(out=sl[:, par, 0:T], in_=psf[:, 0 + par, 0:T], func=AF.Silu),
                          r=[BK(0 + par)], w=[("d_sl", par)])
                    P.add("dve", lambda par=par, fc=fc: nc.vector.tensor_tensor(out=aT[:, fc, 0:T], in0=psf[:, 2 + par, 0:T],
                                                                                in1=sl[:, par, 0:T], op=ALU.mult),
                          r=[BK(2 + par), ("d_sl", par)], w=[("d_aT", fc)])
            if DSTOP < 6:
                return
            for fc in range(NFC):
                wi = fc % 3
                wd = W["wD"][wi]
                dma("sp", wd, wb_down[fc * 128:(fc + 1) * 128, :], w=[("d_wD", wi)])
                def fdn(fc=fc, wd=wd):
                    ins = None
                    for i in range(NB):
                        for half in range(2):
                            ins = nc.tensor.matmul(psf[0:nt, 2 * i + half, :], lhsT=aT[:, fc, i * nt:(i + 1) * nt],
                                                   rhs=wd[:, half * 512:(half + 1) * 512], start=(fc == 0), stop=(fc == NFC - 1))
                    return ins
                P.add("pe", fdn, r=[("d_aT", fc), ("d_wD", wi)], w=[BK(j) for j in range(2 * NB)])
            if DSTOP < 7:
                return
            for i in range(NB):
                ln_a(i, 2 * i, h32[0:nt, i, :], ("d_h32", i), h32[0:nt, i, :], ("d_h32", i))
                if i == 0 and next_head is not None:
                    next_head(0)
            for i in range(NB):
                ln_b(i, 2, h32[0:nt, i, :], ("d_h32", i))
                dma("sp", y_dst(i), h32[0:nt, i, :], r=[("d_h32", i)], w=[("d_yout", i)])
            P.barrier()

        def d_alloc(NB, nt):
            T = NB * nt
            W = {}
            W["band"] = ar.alloc([2, 4, 128], BF16)
            W["bandH"] = ar.alloc([4, 4, 128], BF16)
            W["wuv"] = ar.alloc([8, 128], BF16)
            W["wau"] = ar.alloc([4, 1024], BF16)
            W["wpl"] = ar.alloc([4, 256], BF16)
            W["lnc"] = ar.alloc([4, 1024], F32)
            W["h32"] = ar.alloc([NB, 1024], F32)
            W["hT"] = ar.alloc([KC, T], BF16)
            W["x32"] = ar.alloc([2, 1024], F32)
            W["stats"] = ar.alloc([NB, 2, 6], F32)
            W["mv"] = ar.alloc([NB, 8], F32)
            W["xb"] = ar.alloc([2, 1024], BF16)
            W["xT"] = ar.alloc([KC, T], BF16)
            W["xhb"] = ar.alloc([1024], BF16)
            W["xhT"] = ar.alloc([KC, 128], BF16)
            mark = ar.off
            W["uh"] = ar.alloc([512], BF16)
            W["u_tm"] = ar.alloc([NB, 512], BF16)
            W["ulast"] = ar.alloc([512], F32)
            W["dT"] = ar.alloc([4, T], BF16)
            W["oT"] = ar.alloc([4, T], BF16)
            W["mT"] = ar.alloc([8, T], BF16)
            W["gsb"] = ar.alloc([2, 2, T], BF16)
            W["t1"] = ar.alloc([2, T], F32)
            W["t2"] = ar.alloc([2, T], F32)
            W["wA"] = [ar.alloc([KC, 512], BF16) for _ in range(3)]
            W["hb"] = ar.alloc([2, 1024], BF16)
            end1 = ar.off
            ar.off = mark
            W["aT"] = ar.alloc([NFC, T], BF16)
            W["sl"] = ar.alloc([2, T], F32)
            W["wG"] = [ar.alloc([KC, 256], BF16) for _ in range(2)]
            W["wU"] = [ar.alloc([KC, 256], BF16) for _ in range(2)]
            W["wD"] = [ar.alloc([1024], BF16) for _ in range(3)]
            ar.off = max(ar.off, end1)
            P.add("dve", lambda: nc.vector.memset(W["u_tm"], 0.0), w=[("d_u", i) for i in range(NB)])
            P.add("dve", lambda: nc.vector.memset(W["xb"], 0.0), w=[("d_xb", 0), ("d_xb", 1)])
            P.add("dve", lambda: nc.vector.memset(W["hb"], 0.0), w=[("d_hb", 0), ("d_hb", 1)])
            if DSTOP >= -2:
                dma("sp", W["band"], band_d.rearrange("f p g t -> p f g t"), w=["d_band"])
                dma("sp", W["bandH"], bandH_d, w=["d_band"])
            if DSTOP >= -3:
                dma("pool", W["wuv"], w_uvp, w=["d_wsm"])
            if DSTOP >= -4:
                dma("sp", W["wau"], wb_attn_up.rearrange("(j p) n -> p j n", p=128), r=["wscratch"], w=["d_wsm"])
                dma("sp", W["wpl"], wb_pool.rearrange("g p n -> p g n"), r=["wscratch"], w=["d_wsm"])
            if DSTOP >= -5:
                dma("sp", W["lnc"], lnv.rearrange("(o a) n -> o a n", o=1).to_broadcast([128, 4, 1024]), w=["d_lnc"])
            return W

        def qa_alloc():
            W = {}
            W["xb"] = ar.alloc([1024], BF16)
            W["xT"] = ar.alloc([KC, 128], BF16)
            W["zqn"] = ar.alloc([256], BF16)
            W["zqnT"] = ar.alloc([2, 128], BF16)
            W["ckv32"] = ar.alloc([128], F32)
            W["kr32"] = ar.alloc([32], F32)
            W["rA"] = ar.alloc([4, 32], F32)
            W["rB"] = ar.alloc([4, 32], F32)
            W["qn"] = ar.alloc([8, 64], BF16)
            W["qr32"] = ar.alloc([8, 32], F32)
            W["qr"] = ar.alloc([8, 4, 32], BF16)
            W["qnT"] = ar.alloc([4, 128], BF16)
            W["QlatT"] = [ar.alloc([1024], BF16) for _ in range(2)]
            W["QropeT"] = [ar.alloc([4, 1024], BF16) for _ in range(2)]
            W["Pbuf"] = ar.alloc([NPB, 512], BF16)
            W["att_tm"] = ar.alloc([8, 128], BF16)
            W["rl"] = ar.alloc([8], F32)
            P.add("dve", lambda: nc.vector.memset(W["Pbuf"], 0.0), w=[("P", i) for i in range(NPB)])
            P.add("dve", lambda: nc.vector.memset(W["att_tm"], 0.0), w=[("att_tm", i) for i in range(3)])
            P.add("dve", lambda: nc.vector.memset(W["qr"], 0.0), w=["q_qr"])
            P.add("dve", lambda: nc.vector.memset(W["qn"], 0.0), w=[("q_qn", 0), ("q_qn", 1)])
            P.add("dve", lambda: nc.vector.memset(W["zqn"], 0.0), w=["q_zqn"])
            return W

        attT_stash = ar.alloc([8, NG * 128], BF16)
        phase_mark = ar.off
        P.barrier()

        def prompt_batch(b):
            ar.off = phase_mark
            ckvT = ar.alloc([SEQ], BF16)
            Vx = ar.alloc([NKB, OW], BF16)
            krT = ar.alloc([NKB // 4 * 128], BF16)
            kv_mark = ar.off
            NXB = 3
            xkb = [ar.alloc([4, 1024], BF16) for _ in range(NXB)]
            xkT = [ar.alloc([KC, 128], BF16) for _ in range(2)]
            tabk = [ar.alloc([2, 4, 32], F32) for _ in range(2)]
            sskb = [ar.alloc([4], F32) for _ in range(2)]
            rAk = ar.alloc([4, 32], F32)
            rBk = ar.alloc([4, 32], F32)
            krtm = ar.alloc([4, 32], BF16)
            P.add("dve", lambda: nc.vector.memset(Vx[:, :, 128:130], 1.0), w=["Vones"])

            def k_load(t):
                xk = xkb[t % NXB]
                dma("pool", xk, x_all[b, t * 512:(t + 1) * 512, :].rearrange("(j p) d -> p j d", p=128), w=[("k_x", t % NXB)])

            def k_head(t):
                bi = t % 2
                xk = xkb[t % NXB]
                ssk = sskb[bi]
                dma("sp", tabk[bi], tabK[:, :, 4 * t:4 * t + 4, :].rearrange("c p j e -> p c j e"), w=[("k_tab", bi)])
                zb0 = 2 * bi

                def emit_T(j):
                    xi = (4 * t + j) % 2
                    transposes(lambda k, xk=xk, j=j: xk[:, j, k * 128:(k + 1) * 128], 8, 128, 4 + xi, xkT[xi], ("k_xT", xi),
                               [("k_x", t % NXB)], evac="act")

                def emit_M(j):
                    xi = (4 * t + j) % 2
                    zbk = zb0 + j % 2
                    zc = (j // 2) * 256
                    def fzk(xi=xi, zbk=zbk, zc=zc):
                        ins = None
                        for k in range(KC):
                            ins = nc.tensor.matmul(psf[:, zbk, zc:zc + 160], lhsT=xkT[xi][:, k, :], rhs=w_in_a[:, k, 256:416],
                                                   start=(k == 0), stop=(k == KC - 1), skip_group_check=True)
                        return ins
                    P.add("pe", fzk, r=[("k_xT", xi), RES], w=[("k_z", bi, j), BK(zbk)])
                    jc = (bi * 4 + j) * 128
                    P.add("act", lambda zbk=zbk, zc=zc, j=j, jc=jc, ssk=ssk: nc.scalar.activation(
                        out=junk[:, jc:jc + 128], in_=psf[:, zbk, zc:zc + 128], func=AF.Square, accum_out=ssk[:, j:j + 1]),
                        r=[("k_z", bi, j), BK(zbk)], w=[("k_ss", bi, j), ("junk", bi, j)])

                emit_T(0)
                emit_T(1)
                emit_M(0)
                emit_T(2)
                emit_M(1)
                emit_T(3)
                emit_M(2)
                emit_M(3)

            def k_tail(t):
                bi = t % 2
                ssk = sskb[bi]
                zb0 = 2 * bi
                ssk_keys = [("k_ss", bi, j) for j in range(4)]
                P.add("dve", lambda: nc.vector.tensor_scalar(out=ssk, in0=ssk, scalar1=1.0 / 128, scalar2=RMS_EPS,
                                                             op0=ALU.mult, op1=ALU.add), r=ssk_keys, w=ssk_keys)
                P.add("pool", lambda: nc.gpsimd.tensor_tensor(out=ssk, in0=ssk, in1=mhalf[:, 0:4], op=ALU.pow),
                      r=ssk_keys + [RES], w=ssk_keys)
                for j in range(4):
                    kb = 4 * t + j
                    zbk = zb0 + j % 2
                    zc = (j // 2) * 256
                    P.add("dve", lambda zbk=zbk, zc=zc, kb=kb, j=j: nc.vector.scalar_tensor_tensor(
                        out=Vx[:, kb, 0:128], in0=psf[:, zbk, zc:zc + 128], scalar=ssk[:, j:j + 1], in1=gkv_bc,
                        op0=ALU.mult, op1=ALU.mult), r=[("k_z", bi, j), BK(zbk), ("k_ss", bi, j), RES], w=[("V", kb)])
                for a in range(2):
                    zr = psf[:, zb0:zb0 + 2, a * 256 + 128:a * 256 + 160]
                    rope_ops(zr, tabk[bi][:, 0, 2 * a:2 * a + 2, :], tabk[bi][:, 1, 2 * a:2 * a + 2, :], krtm[:, 2 * a:2 * a + 2, :], 128,
                             [("k_z", bi, 2 * a), ("k_z", bi, 2 * a + 1), BK(zb0), BK(zb0 + 1), ("k_tab", bi)],
                             [("k_krtm", a)], rAk[:, 2 * a:2 * a + 2, :], rBk[:, 2 * a:2 * a + 2, :])
                def ftr():
                    ins = None
                    for j in range(4):
                        ins = nc.tensor.transpose(psb[:, 6, j * 128:(j + 1) * 128], Vx[:, 4 * t + j, 0:128], ident)
                    return ins
                P.add("pe", ftr, r=[("V", 4 * t + j) for j in range(4)] + [RES], w=[BK(6)])
                P.add("dve", lambda: nc.vector.tensor_copy(out=ckvT[:, t * 512:(t + 1) * 512], in_=psb[:, 6, 0:512]),
                      r=[BK(6)], w=[("ckvT", t)])
                P.add("pe", lambda: nc.tensor.transpose(psb[:, 7, 0:128], krtm.rearrange("p a b -> p (a b)"), ident),
                      r=[("k_krtm", 0), ("k_krtm", 1), RES], w=[BK(7)])
                P.add("dve", lambda: nc.vector.tensor_copy(out=krT[:, t * 128:(t + 1) * 128], in_=psb[:, 7, 0:128]),
                      r=[BK(7)], w=[("krT", t)])

            k_load(0)
            if NSTEP > 1:
                k_load(1)
            for t in range(NSTEP + 1):
                if t + 2 < NSTEP:
                    k_load(t + 2)
                if t < NSTEP:
                    k_head(t)
                if t >= 1:
                    k_tail(t - 1)
            P.barrier()
            casts = list(cast_list) if b == 0 else []
            if STOP < 2:
                for c in casts:
                    c()
                return
            ar.off = kv_mark
            W = qa_alloc()
            def mk_q(s):
                return q_stage(x_own[b, s], 128, tabO_sb[:, 0, s, :], tabO_sb[:, 1, s, :], ckv_own[b, s], kr_own[b, s], W, s % 2)
            order = list(range(NG - 1, -1, -1))
            for _ in mk_q(order[0]):
                pass
            for oi, s in enumerate(order):
                qb = s % 2
                groups = []
                for g in range(2):
                    groups.append(dict(lat=W["QlatT"][qb][:, g * 512:(g + 1) * 512],
                                       rope=(lambda pb, g=g, qb=qb: W["QropeT"][qb][:, pb // 32, g * 512:(g + 1) * 512]),
                                       heads=[(g * 4 + hl, hl * 128) for hl in range(4)], N=512))
                kbs = []
                for kb in range(8 * s + 8):
                    pb = 32 * (kb % 4)
                    c0 = (kb // 4) * 128
                    ing = kb >= 8 * s
                    kbs.append(dict(ckvT=ckvT[:, kb * 128:(kb + 1) * 128], krT=krT[:, c0:c0 + 128], pb=pb,
                                    V=Vx[:, kb, 0:129], nk=128,
                                    mask=(maskK[:, (kb - 8 * s) * 128:(kb - 8 * s + 1) * 128] if ing else None), keys=[]))
                nxt = mk_q(order[oi + 1]) if (oi + 1 < NG and not os.environ.get("NOINTER")) else None
                if QSTOP >= 9:
                    attention(128, groups, kbs, W["Pbuf"], W["att_tm"], W["rl"],
                              attT_stash[:, :, s * 128:(s + 1) * 128], ("stash", s), [(("q", qb), "QlatT"), (("q", qb), "QropeT")],
                              inter=nxt, inter_n=24)
                if casts:
                    casts.pop(0)()
                if nxt is not None:
                    for _ in nxt:
                        pass
                elif oi + 1 < NG:
                    for _ in mk_q(order[oi + 1]):
                        pass
            for c in casts:
                c()
            P.barrier()
            if STOP < 3:
                return
            ar.off = phase_mark
            NB = min(4, NG)
            Wd = d_alloc(NB, 128)
            ngr = NG // NB
            def xsrc(gi):
                return lambda i: x_own[b, gi * NB + i]
            def xhsrc(gi):
                return x_halo[b, gi * NB:(gi + 1) * NB].rearrange("s r d -> (s r) d")
            for gi in range(ngr):
                pool_dst = [(NB - 1, pool_p[b])] if gi == ngr - 1 else []
                nh = None
                if gi + 1 < ngr and not os.environ.get("NOPREHEAD"):
                    nh = (lambda bank, gi=gi: d_head(NB, 128, xsrc(gi + 1), xhsrc(gi + 1), Wd, bank))
                d_stage(NB, 128, xsrc(gi), xhsrc(gi),
                        attT_stash[:, :, gi * NB * 128:(gi + 1) * NB * 128],
                        (lambda i, gi=gi: 0 if (gi == 0 and i == 0) else 1),
                        lambda i, gi=gi: y_own[b, gi * NB + i], pool_dst, Wd,
                        head_done=(gi > 0 and not os.environ.get("NOPREHEAD")), next_head=nh)
            P.barrier()

        for b in range(2):
            if STOP >= 1:
                prompt_batch(b)

        def sample_phase():
            ar.off = phase_mark
            ckvT = ar.alloc([1088], BF16)
            Vx = ar.alloc([9, OW], BF16)
            krT = ar.alloc([3 * 128], BF16)
            krc = ar.alloc([8, 32], BF16)
            ckvn = ar.alloc([128], BF16)
            krn = ar.alloc([128], BF16)
            W = qa_alloc()
            P.add("dve", lambda: nc.vector.memset(Vx[:, :, 128:130], 1.0), w=["Vones"])
            P.add("dve", lambda: nc.vector.memset(krT[:, 256:384], 0.0), w=["s_krTn"])
            P.add("dve", lambda: nc.vector.memset(krn, 0.0), w=["s_krn"])
            P.add("dve", lambda: nc.vector.memset(Vx[64:128, 8, :], 0.0), w=["s_Vn", "Vones"])
            for e in range(4):
                dma("pool", Vx[:, 0:8, 0:128], ckv_cache[e].rearrange("(k p) r -> p k r", p=128), w=["s_V"])
                dma("pool", krc, kr_cache[e].rearrange("(k p) r -> p k r", p=128), w=["s_krc"])
                transposes(lambda k: Vx[:, k, 0:128], 8, 128, 7, ckvT[:, 0:1024].rearrange("p (k t) -> p k t", k=8), "s_ckvT", ["s_V"])
                transposes(lambda k: krc[:, 4 * k:4 * k + 4, :].rearrange("p a b -> p (a b)"), 2, 128, 7,
                           krT[:, 0:256].rearrange("p (k t) -> p k t", k=2), "s_krT", ["s_krc"])

                def sample_kv(ckv32, kr32):
                    P.add("dve", lambda: nc.vector.tensor_copy(out=Vx[0:64, 8, 0:128], in_=ckv32[0:64]), r=["q_ckv32"], w=["s_Vn"])
                    P.add("dve", lambda: nc.vector.tensor_copy(out=krn[0:64, 0:32], in_=kr32[0:64]), r=["q_kr32"], w=["s_krn"])
                    transposes(lambda k: Vx[:, 8, 0:128], 1, 64, 7, ckvT[:, 1024:1088].unsqueeze(1), "s_ckvTn", ["s_Vn"])
                    P.add("pe", lambda: nc.tensor.transpose(psb[:, 7, 0:128], krn, ident), r=["s_krn", RES],
                          w=[BK(7)])
                    P.add("dve", lambda: nc.vector.tensor_copy(out=krT[0:32, 256:320], in_=psb[0:32, 7, 0:64]), r=[BK(7)], w=["s_krTn"])

                for _ in q_stage(x_s[e], 64, tabS_sb[0:64, 0, :], tabS_sb[0:64, 1, :], ckv_s[e], kr_s[e], W, 0, sample_kv=sample_kv):
                    pass
                groups = [dict(lat=W["QlatT"][0][:, 0:512], rope=(lambda pb: W["QropeT"][0][:, pb // 32, 0:512]),
                               heads=[(h, h * 64) for h in range(8)], N=512)]
                kbs = []
                for kb in range(8):
                    pb = 32 * (kb % 4)
                    c0 = (kb // 4) * 128
                    kbs.append(dict(ckvT=ckvT[:, kb * 128:(kb + 1) * 128], krT=krT[:, c0:c0 + 128], pb=pb,
                                    V=Vx[:, kb, 0:129], nk=128, mask=None, keys=["s_ckvT", "s_krT", "s_V"]))
                kbs.append(dict(ckvT=ckvT[:, 1024:1088], krT=krT[:, 256:320], pb=0, V=Vx[:, 8, 0:129], nk=64, mask=None,
                                keys=["s_ckvTn", "s_krTn", "s_Vn"]))
                attention(64, groups, kbs, W["Pbuf"], W["att_tm"], W["rl"],
                          attT_stash[:, :, e * 64:(e + 1) * 64], ("stash", e), [(("q", 0), "QlatT"), (("q", 0), "QropeT")])
            P.barrier()
            ar.off = phase_mark
            Wd = d_alloc(4, 64)

            def halo_direct(uh):
                P.add("dve", lambda: nc.vector.memset(uh, 0.0), w=["d_uh0"])
                for e in range(4):
                    dma("pool", uh[32 * e + 17:32 * e + 32, :], pool_state[e], r=["d_uh0"], w=["d_uh"])

            d_stage(4, 64, lambda i: x_s[i], None, attT_stash[:, :, 0:256], lambda i: 1, lambda i: y_s[i],
                    [(i, pool_s[i]) for i in range(4)], Wd, halo_direct=halo_direct)
            P.barrier()
        if STOP >= 4:
            sample_phase()
        nops, nwait = P.finalize()
    return nc, (nops, nwait)


_CACHE = {}


def _rope_tables(pos):
    half = QK_ROPE // 2
    inv = (np.float32(1.0) / (np.float32(10000.0) ** (np.arange(half, dtype=np.float32) / np.float32(half)))).astype(np.float32)
    ang = (pos.astype(np.float32)[:, None] * inv[None, :]).astype(np.float32)
    c = np.cos(ang).astype(np.float32)
    s = np.sin(ang).astype(np.float32)
    cos2 = np.concatenate([c, c], axis=-1)
    sin2 = np.concatenate([-s, s], axis=-1)
    return cos2, sin2


def _bands():
    wins = (2, 4, 8, 16)
    band = np.zeros((2, 128, 4, 128), np.float32)
    bandH = np.zeros((32, 4, 128), np.float32)
    tp = np.arange(128)[:, None]
    t = np.arange(128)[None, :]
    r = np.arange(32)[:, None] - 32
    for g, w in enumerate(wins):
        inwin = ((tp <= t) & (tp >= t - w + 1)).astype(np.float32)
        eye = (tp == t).astype(np.float32)
        band[1, :, g, :] = inwin / w - eye
        cnt = np.minimum(t + 1, w).astype(np.float32)
        band[0, :, g, :] = inwin / cnt - eye
        bandH[:, g, :] = ((r >= t - w + 1).astype(np.float32)) / w
    bandHm = np.zeros((128, 4, 4, 128), np.float32)
    for i in range(4):
        bandHm[32 * i:32 * i + 32, i] = bandH
    return band.astype(ml_dtypes.bfloat16), bandHm.astype(ml_dtypes.bfloat16)


def kernel(x_prompt, x_sample, cache_kv_latent, cache_k_rope, state_pool, w_in, b_gate, q_norm_g, w_uq, kv_norm_g,
           w_uk, w_uv, w_attn_up, w_pool, pool_scale, w_o, ln1_g, ln1_b, w_gate, w_up, w_down, ln2_g, ln2_b):
    f = lambda a: np.ascontiguousarray(np.asarray(a), dtype=np.float32)
    x_prompt = f(x_prompt); x_sample = f(x_sample)
    B, SEQ, _ = x_prompt.shape
    NG = SEQ // 1024
    assert B == 2 and SEQ % 1024 == 0
    if NG not in _CACHE:
        _CACHE[NG] = build_program(NG)
    nc, _ = _CACHE[NG]

    cache_kv_latent = f(cache_kv_latent); cache_k_rope = f(cache_k_rope); state_pool = f(state_pool)
    w_uk_ = f(w_uk)[0]
    w_ukT = np.zeros((128, 8, 128), np.float32)
    for h in range(8):
        w_ukT[(h % 2) * 64:(h % 2) * 64 + 64, h, :] = w_uk_[:, h, :].T
    w_uv_ = f(w_uv)[0]
    w_uvp = np.zeros((128, 8, 128), np.float32)
    for h in range(8):
        w_uvp[:, h, (h % 2) * 64:(h % 2) * 64 + 64] = w_uv_[:, h, :]
    shared = {
        "x_all": x_prompt,
        "w_in": f(w_in)[0], "w_uq": f(w_uq)[0], "w_ukT": w_ukT, "w_uvp": w_uvp, "w_attn_up": f(w_attn_up)[0],
        "w_pool": f(w_pool)[0], "w_o": f(w_o)[0], "w_gate": f(w_gate)[0], "w_up": f(w_up)[0], "w_down": f(w_down)[0],
        "gq": f(q_norm_g), "gkv": f(kv_norm_g),
        "b_gateT": np.ascontiguousarray(f(b_gate)[0].reshape(16, 128).T),
        "pool_scaleT": np.ascontiguousarray(f(pool_scale)[0].reshape(8, 128).T),
        "lnv": np.ascontiguousarray(np.stack([f(ln1_g)[0], f(ln1_b)[0], f(ln2_g)[0], f(ln2_b)[0]])),
        "ident": np.eye(128, dtype=np.float32).astype(ml_dtypes.bfloat16),
    }
    cosK, sinK = _rope_tables(np.arange(SEQ))
    shared["tabK"] = np.ascontiguousarray(np.stack([cosK, sinK]).reshape(2, SEQ // 128, 128, 32).transpose(0, 2, 1, 3))
    cosS, sinS = _rope_tables(cache_kv_latent.shape[2] + np.arange(64))
    shared["tabS"] = np.ascontiguousarray(np.stack([cosS, sinS]))
    band, bandH = _bands()
    shared["bandH"] = bandH
    mq = np.zeros((128, 512), np.float32)
    mq[0, :] = NEG_BIG
    mq[1, :] = np.where((np.arange(512) % 128) < 64, NEG_BIG, 0.0)
    shared["maskQ"] = mq.astype(ml_dtypes.bfloat16)
    rm = np.zeros((128, 4), np.float32)
    for m_ in range(4):
        rm[32 * m_:32 * m_ + 32, m_] = 1.0
    shared["rowmask"] = rm
    band_generic = band.copy()
    band_generic[0] = band_generic[1]

    in_maps = []
    xp = x_prompt.reshape(2, NG, 8, 128, D_MODEL)
    for i in range(NCORES):
        m = dict(shared)
        m["x_own"] = np.ascontiguousarray(xp[:, :, i])
        halo = np.zeros((2, NG, 32, D_MODEL), np.float32)
        for s in range(NG):
            st = (8 * s + i) * 128
            if st >= 32:
                halo[:, s] = x_prompt[:, st - 32:st]
        m["x_halo"] = halo
        m["x_s"] = np.ascontiguousarray(x_sample[4 * i:4 * i + 4])
        m["ckv_cache"] = np.ascontiguousarray(cache_kv_latent[0, 4 * i:4 * i + 4])
        m["kr_cache"] = np.ascontiguousarray(cache_k_rope[0, 4 * i:4 * i + 4])
        m["pool_state"] = np.ascontiguousarray(state_pool[0, 4 * i:4 * i + 4])
        pos_own = ((8 * np.arange(NG)[:, None] + i) * 128 + np.arange(128)[None, :]).reshape(-1)
        cO, sO = _rope_tables(pos_own)
        m["tabO"] = np.ascontiguousarray(np.stack([cO, sO]).reshape(2, NG, 128, 32).transpose(0, 2, 1, 3))
        mk = np.zeros((128, 1024), np.float32)
        slot = np.arange(1024) // 128
        mk[0] = (slot > i).astype(np.float32)
        mk[1] = ((slot == i) & ((np.arange(1024) % 128) >= 64)).astype(np.float32)
        m["maskK"] = mk.astype(ml_dtypes.bfloat16)
        m["band"] = band if i == 0 else band_generic
        in_maps.append(m)

    res = run_bass_kernel_spmd(nc, in_maps, core_ids=list(range(NCORES)))
    R = res.results
    y_p = np.zeros((2, SEQ, D_MODEL), np.float32).reshape(2, NG, 8, 128, D_MODEL)
    ckv_p = np.zeros((2, NG, 8, 128, 128), np.float32)
    kr_p = np.zeros((2, NG, 8, 128, 32), np.float32)
    for i in range(NCORES):
        y_p[:, :, i] = R[i]["y_own"]
        ckv_p[:, :, i] = R[i]["ckv_own"]
        kr_p[:, :, i] = R[i]["kr_own"]
    y_s = np.concatenate([R[i]["y_s"] for i in range(NCORES)], axis=0)
    ckv_s = np.concatenate([R[i]["ckv_s"] for i in range(NCORES)], axis=0)
    kr_s = np.concatenate([R[i]["kr_s"] for i in range(NCORES)], axis=0)
    pool_s = np.concatenate([R[i]["pool_s"][:, -15:] for i in range(NCORES)], axis=0)
    pool_p = np.asarray(R[NCORES - 1]["pool_p"])[:, -15:]
    return (y_p.reshape(2, SEQ, D_MODEL).astype(np.float32),
            y_s.astype(np.float32),
            ckv_p.reshape(1, 2, SEQ, 128).astype(np.float32),
            kr_p.reshape(1, 2, SEQ, 32).astype(np.float32),
            pool_p.reshape(1, 2, 15, 512).astype(np.float32),
            ckv_s.reshape(1, 32, 64, 128).astype(np.float32),
            kr_s.reshape(1, 32, 64, 32).astype(np.float32),
            pool_s.reshape(1, 32, 15, 512).astype(np.float32))
```

```python
import math
import os
from contextlib import ExitStack
QSTOP = int(os.environ.get("QSTOP", "99"))
DSTOP = int(os.environ.get("DSTOP", "99"))

import numpy as np
import ml_dtypes

import concourse.bass as bass
import concourse.mybir as mybir
from concourse.bass_utils import run_bass_kernel_spmd

F32 = mybir.dt.float32
BF16 = mybir.dt.bfloat16
AF = mybir.ActivationFunctionType
ALU = mybir.AluOpType

D_MODEL = 1024
KC = 8
N_HEADS = 8
QK_NOPE = 64
QK_ROPE = 32
Q_LORA = 256
KV_LORA = 128
POOL_WIDTH = 512
D_FF = 2816
NFC = D_FF // 128
IN_TOTAL = 2976
ATTN_SCALE = 96 ** -0.5
ALPHA = 2 ** 0.25
RMS_EPS = 1e-6
LN_EPS = 1e-5
NEG_BIG = -30000.0
NCORES = 8
OW = 130


class _Op:
    __slots__ = ("stream", "fn", "deps", "dma", "signal", "pos", "dslot", "dval", "waits", "sigval")

    def __init__(self, stream, fn, deps, dma):
        self.stream = stream
        self.fn = fn
        self.deps = deps
        self.dma = dma
        self.signal = False
        self.pos = -1
        self.dslot = -1
        self.dval = 0
        self.waits = None
        self.sigval = 0


class Prog:
    STREAMS = ("pe", "act", "dve", "pool", "sp")
    R = 8

    def __init__(self, nc, es):
        self.nc = nc
        self.eng = {"pe": nc.tensor, "act": nc.scalar, "dve": nc.vector, "pool": nc.gpsimd, "sp": nc.sync}
        self.ops = []
        self.lastw = {}
        self.readers = {}
        self.sem = {s: es.enter_context(nc.semaphore("sem_" + s)) for s in self.STREAMS}
        self.dsem = {q: [es.enter_context(nc.semaphore("dq_%s_%d" % (q, i))) for i in range(self.R)]
                     for q in ("sp", "pool")}
        self.last_in_stream = {}
        self.dma_since_barrier = []

    def add(self, stream, fn, r=(), w=(), dma=False):
        idx = len(self.ops)
        deps = {}
        for k in r:
            p = self.lastw.get(k)
            if p is not None:
                deps[p] = True
        for k in w:
            p = self.lastw.get(k)
            if p is not None:
                deps[p] = True
            for q in self.readers.get(k, ()):
                if q not in deps:
                    deps[q] = False
        for k in r:
            self.readers.setdefault(k, []).append(idx)
        for k in w:
            self.lastw[k] = idx
            self.readers[k] = []
        self.ops.append(_Op(stream, fn, deps, dma))
        self.last_in_stream[stream] = idx
        if dma:
            self.dma_since_barrier.append(idx)
        return idx

    def barrier(self):
        deps = {i: True for i in self.last_in_stream.values()}
        for i in self.dma_since_barrier:
            deps[i] = True
        for s in self.STREAMS:
            self.ops.append(_Op(s, None, dict(deps), False))
        self.lastw = {}
        self.readers = {}
        self.dma_since_barrier = []
        self.last_in_stream = {}

    def finalize(self):
        ops = self.ops
        pos = {s: 0 for s in self.STREAMS}
        dcount = {"sp": 0, "pool": 0}
        dlist = {"sp": [], "pool": []}
        waited_pos = {s: {} for s in self.STREAMS}
        waited_dma = {s: {} for s in self.STREAMS}
        for idx, op in enumerate(ops):
            E = op.stream
            deps = op.deps
            if op.dma:
                j = dcount[E]
                if j >= self.R:
                    deps[dlist[E][j - self.R]] = True
                op.dslot = j % self.R
                op.dval = 16 * (j // self.R + 1)
                dcount[E] = j + 1
                dlist[E].append(idx)
            need = []
            for p in sorted(deps):
                P = ops[p]
                if P.dma:
                    key = (P.stream, P.dslot)
                    if waited_dma[E].get(key, 0) >= P.dval:
                        continue
                    waited_dma[E][key] = P.dval
                    need.append(("d", P.stream, P.dslot, P.dval))
                else:
                    if P.fn is None:
                        continue
                    Ep = P.stream
                    if Ep == E and E == "pe":
                        continue
                    if waited_pos[E].get(Ep, -1) >= P.pos:
                        continue
                    waited_pos[E][Ep] = P.pos
                    P.signal = True
                    need.append(("c", p))
            op.waits = need
            op.pos = pos[E]
            pos[E] += 1
        cnt = {s: 0 for s in self.STREAMS}
        for op in ops:
            if (not op.dma) and op.signal:
                cnt[op.stream] += 1
                op.sigval = cnt[op.stream]
        nwait = 0
        for op in ops:
            eng = self.eng[op.stream]
            for wt in op.waits:
                if wt[0] == "c":
                    P = ops[wt[1]]
                    eng.wait_ge(self.sem[P.stream], P.sigval)
                else:
                    eng.wait_ge(self.dsem[wt[1]][wt[2]], wt[3])
                nwait += 1
            if op.fn is None:
                continue
            ins = op.fn()
            if op.dma:
                ins.then_inc(self.dsem[op.stream][op.dslot], 16)
            elif op.signal:
                ins.then_inc(self.sem[op.stream], 1)
        self.stats = dict(cnt=dict(cnt), dma={q: len(v) for q, v in dlist.items()}, nops=len(ops), nwait=nwait)
        if os.environ.get("PROG_STATS"):
            print("PROG_STATS", self.stats, flush=True)
        return len(ops), nwait


class Arena:
    def __init__(self, nc, es, nbytes):
        self.t = es.enter_context(nc.sbuf_tensor("arena", [128, nbytes // 4], F32))
        self.size = nbytes
        self.off = 0

    def alloc(self, shape, dtype):
        n = 1
        for s in shape:
            n *= s
        nb = n * (2 if dtype == BF16 else 4)
        nb_al = (nb + 63) // 64 * 64
        assert self.off + nb_al <= self.size, ("arena overflow", self.off, nb_al, self.size)
        a = self.t[:, self.off // 4:(self.off + nb_al) // 4]
        self.off += nb_al
        if dtype == BF16:
            a = a.bitcast(BF16)
        a = a[:, 0:n]
        if len(shape) > 1:
            names = ["d%d" % i for i in range(len(shape))]
            pat = "p (" + " ".join(names) + ") -> p " + " ".join(names)
            a = a.rearrange(pat, **{nm: s for nm, s in zip(names[:-1], shape[:-1])})
        return a


def build_program(NG, STOP=99):
    SEQ = NG * 1024
    NKB = SEQ // 128
    NSTEP = NKB // 4
    nc = bass.Bass("TRN2", target_bir_lowering=False)

    def din(name, shape, dt=F32):
        return nc.dram_tensor(name, list(shape), dt, kind="ExternalInput").ap()

    def dout(name, shape, dt=F32):
        return nc.dram_tensor(name, list(shape), dt, kind="ExternalOutput").ap()

    def dint(name, shape, dt=BF16):
        return nc.dram_tensor(name, list(shape), dt, kind="Internal").ap()

    x_all = din("x_all", [2, SEQ, D_MODEL])
    x_own = din("x_own", [2, NG, 128, D_MODEL])
    x_halo = din("x_halo", [2, NG, 32, D_MODEL])
    x_s = din("x_s", [4, 64, D_MODEL])
    ckv_cache = din("ckv_cache", [4, 1024, 128])
    kr_cache = din("kr_cache", [4, 1024, 32])
    pool_state = din("pool_state", [4, 15, 512])
    w_in = din("w_in", [D_MODEL, IN_TOTAL])
    w_uq = din("w_uq", [Q_LORA, 768])
    w_ukT = din("w_ukT", [128, 8, 128])
    w_uvp = din("w_uvp", [128, 8, 128])
    w_attn_up = din("w_attn_up", [512, D_MODEL])
    w_pool = din("w_pool", [4, 128, 256])
    w_o = din("w_o", [D_MODEL, D_MODEL])
    w_gate = din("w_gate", [D_MODEL, D_FF])
    w_up = din("w_up", [D_MODEL, D_FF])
    w_down = din("w_down", [D_FF, D_MODEL])
    gq = din("gq", [1, Q_LORA])
    gkv = din("gkv", [1, KV_LORA])
    b_gateT = din("b_gateT", [128, 16])
    pool_scaleT = din("pool_scaleT", [128, 8])
    lnv = din("lnv", [4, D_MODEL])
    tabK = din("tabK", [2, 128, NKB, 32])
    tabO = din("tabO", [2, 128, NG, 32])
    tabS = din("tabS", [2, 64, 32])
    ident_d = din("ident", [128, 128], BF16)
    maskK_d = din("maskK", [128, 1024], BF16)
    maskQ_d = din("maskQ", [128, 512], BF16)
    rowmask_d = din("rowmask", [128, 4])
    band_d = din("band", [2, 128, 4, 128], BF16)
    bandH_d = din("bandH", [128, 4, 4, 128], BF16)

    y_own = dout("y_own", [2, NG, 128, D_MODEL])
    y_s = dout("y_s", [4, 64, D_MODEL])
    ckv_own = dout("ckv_own", [2, NG, 128, 128])
    kr_own = dout("kr_own", [2, NG, 128, 32])
    pool_p = dout("pool_p", [2, 64, 512])
    ckv_s = dout("ckv_s", [4, 64, 128])
    kr_s = dout("kr_s", [4, 64, 32])
    pool_s = dout("pool_s", [4, 32, 512])

    wb_in = dint("wb_in", [D_MODEL, IN_TOTAL])
    wb_attn_up = dint("wb_attn_up", [512, D_MODEL])
    wb_pool = dint("wb_pool", [4, 128, 256])
    wb_o = dint("wb_o", [D_MODEL, D_MODEL])
    wb_gate = dint("wb_gate", [D_MODEL, D_FF])
    wb_up = dint("wb_up", [D_MODEL, D_FF])
    wb_down = dint("wb_down", [D_FF, D_MODEL])

    es = ExitStack()
    with es:
        P = Prog(nc, es)
        ar = Arena(nc, es, 207 * 1024)
        ps = es.enter_context(nc.psum_tensor("ps", [128, 8, 512], F32))
        psf = ps[:, :, :]
        psb = psf.bitcast(BF16)

        def BK(j):
            return ("ps", j)

        ident = ar.alloc([128], BF16)
        w_in_a = ar.alloc([KC, 416], BF16)
        w_uq_sb = ar.alloc([2, 768], BF16)
        w_ukT_sb = ar.alloc([8, 128], BF16)
        gq_bc = ar.alloc([Q_LORA], F32)
        gkv_bc = ar.alloc([KV_LORA], F32)
        tabO_sb = ar.alloc([2, NG, 32], F32)
        tabS_sb = ar.alloc([2, 32], F32)
        maskK = ar.alloc([1024], BF16)
        maskQ = ar.alloc([512], BF16)
        mhalf = ar.alloc([8], F32)
        rowmask = ar.alloc([4], F32)
        small = ar.alloc([64], F32)
        junk = ar.alloc([1024], F32)
        b_gate_sb = ar.alloc([16], F32)
        pool_scale_sb = ar.alloc([8], F32)

        def dma(q, out, in_, r=(), w=()):
            eng = P.eng[q]
            return P.add(q, lambda: eng.dma_start(out=out, in_=in_), r=r, w=w, dma=True)

        RES = "res"
        dma("sp", ident, ident_d, w=[RES])
        dma("pool", w_in_a, w_in[:, 0:416].rearrange("(k p) n -> p k n", p=128), w=[RES])
        dma("pool", w_uq_sb, w_uq.rearrange("(k p) n -> p k n", p=128), w=[RES])
        dma("pool", w_ukT_sb, w_ukT, w=[RES])
        dma("sp", gq_bc, gq.to_broadcast([128, Q_LORA]), w=[RES])
        dma("sp", gkv_bc, gkv.to_broadcast([128, KV_LORA]), w=[RES])
        dma("sp", tabO_sb, tabO.rearrange("c p s j -> p c s j"), w=[RES])
        dma("sp", tabS_sb[0:64], tabS.rearrange("c p j -> p c j"), w=[RES])
        dma("sp", maskK, maskK_d, w=[RES])
        dma("sp", maskQ, maskQ_d, w=[RES])
        dma("sp", rowmask, rowmask_d, w=[RES])
        dma("sp", b_gate_sb, b_gateT, w=[RES])
        dma("sp", pool_scale_sb, pool_scaleT, w=[RES])
        P.add("dve", lambda: nc.vector.memset(mhalf, -0.5), w=[RES])

        def cast_w(dst, src, split):
            if split > 1:
                dst = dst.rearrange("r (a c) -> (r a) c", a=split)
                src = src.rearrange("r (a c) -> (r a) c", a=split)
            dma("pool", dst, src, w=["wscratch"])

        cast_list = [
            lambda: cast_w(wb_in, w_in, 2),
            lambda: cast_w(wb_attn_up, w_attn_up, 1),
            lambda: cast_w(wb_pool.rearrange("g a b -> (g a) b"), w_pool.rearrange("g a b -> (g a) b"), 1),
            lambda: cast_w(wb_o, w_o, 1),
            lambda: cast_w(wb_gate, w_gate, 2),
            lambda: cast_w(wb_up, w_up, 2),
            lambda: cast_w(wb_down, w_down, 1),
        ]

        stash_mark = ar.off

        def transposes(src_fn, n, nt, bank, dst, dst_key, src_keys, evac="dve"):
            def f():
                ins = None
                for i in range(n):
                    ins = nc.tensor.transpose(psb[:, bank, i * 128:(i + 1) * 128], src_fn(i), ident)
                return ins
            P.add("pe", f, r=list(src_keys) + [RES], w=[BK(bank)])
            src = psb[:, bank, 0:n * 128].rearrange("p (a t) -> p a t", a=n)[:, :, 0:nt]
            if evac == "act":
                P.add("act", lambda: nc.scalar.copy(out=dst, in_=src), r=[BK(bank)], w=[dst_key])
            else:
                P.add("dve", lambda: nc.vector.tensor_copy(out=dst, in_=src), r=[BK(bank)], w=[dst_key])

        def rstd_from_ss(ss, n_el, eps, ncol, nt, key):
            P.add("dve", lambda: nc.vector.tensor_scalar(out=ss, in0=ss, scalar1=1.0 / n_el, scalar2=eps,
                                                         op0=ALU.mult, op1=ALU.add), r=[key], w=[key])
            P.add("pool", lambda: nc.gpsimd.tensor_tensor(out=ss, in0=ss, in1=mhalf[0:nt, 0:ncol], op=ALU.pow),
                  r=[key, RES], w=[key])

        def rope_ops(zr, cos2, sin2, out, nt, r, w, tmpA, tmpB):
            P.add("dve", lambda: nc.vector.tensor_tensor(out=tmpA, in0=zr, in1=cos2, op=ALU.mult), r=r, w=["ropeA"])
            P.add("dve", lambda: nc.vector.tensor_tensor(out=tmpB[:, :, 0:16], in0=zr[:, :, 16:32], in1=sin2[:, :, 0:16],
                                                         op=ALU.mult), r=r, w=["ropeB0"])
            P.add("dve", lambda: nc.vector.tensor_tensor(out=tmpB[:, :, 16:32], in0=zr[:, :, 0:16], in1=sin2[:, :, 16:32],
                                                         op=ALU.mult), r=r, w=["ropeB1"])
            P.add("dve", lambda: nc.vector.tensor_tensor(out=out, in0=tmpA, in1=tmpB, op=ALU.add),
                  r=["ropeA", "ropeB0", "ropeB1"], w=w)

        att_ctr = {"t": 0}
        NSB = 3
        NPB = 4

        def attention(nq, groups, key_blocks, Pbuf, att_tm, rl, stash_dst, stash_key, qkeys, inter=None, inter_n=0):
            tiles = [(kb, g) for kb in key_blocks for g in groups]
            LAG = 2
            per_tile = 0
            if inter is not None:
                per_tile = min(3, max(1, -(-inter_n // max(1, len(tiles) - 4))))
            first_kb = key_blocks[0]
            last_kb = key_blocks[-1]
            started = set()
            info = []
            for i in range(len(tiles) + LAG):
                if i < len(tiles):
                    kb, g = tiles[i]
                    t = att_ctr["t"]
                    att_ctr["t"] += 1
                    sb = 3 + (t % NSB)
                    pb_i = t % NPB
                    nk = kb["nk"]
                    N = g["N"]
                    info.append((sb, pb_i))

                    def fS(kb=kb, g=g, sb=sb, nk=nk, N=N):
                        nc.tensor.matmul(psf[0:nk, sb, 0:N], lhsT=kb["ckvT"], rhs=g["lat"], start=True, stop=False)
                        last = kb["mask"] is None
                        ins = nc.tensor.matmul(psf[0:nk, sb, 0:N], lhsT=kb["krT"], rhs=g["rope"](kb["pb"]),
                                               start=False, stop=last)
                        if not last:
                            ins = nc.tensor.matmul(psf[0:nk, sb, 0:N], lhsT=kb["mask"], rhs=maskQ[:, 0:N],
                                                   start=False, stop=True)
                        return ins
                    P.add("pe", fS, r=list(kb["keys"]) + list(qkeys) + [RES], w=[BK(sb)])
                    P.add("act", lambda sb=sb, pb_i=pb_i, nk=nk, N=N: nc.scalar.activation(
                        out=Pbuf[0:nk, pb_i, 0:N], in_=psf[0:nk, sb, 0:N], func=AF.Exp, scale=ATTN_SCALE),
                        r=[BK(sb)], w=[("P", pb_i)])
                if i >= LAG:
                    kb, g = tiles[i - LAG]
                    sb, pb_i = info[i - LAG]
                    nk = kb["nk"]

                    def fPV(kb=kb, g=g, pb_i=pb_i, nk=nk):
                        ins = None
                        for (h, col) in g["heads"]:
                            bk = h // 3
                            st = (kb is first_kb) and (bk not in started)
                            if st:
                                started.add(bk)
                            c0 = (h % 3) * OW
                            ins = nc.tensor.matmul(psf[0:nq, bk, c0:c0 + 129], lhsT=Pbuf[:, pb_i, col:col + nq],
                                                   rhs=kb["V"], start=st, stop=(kb is last_kb),
                                                   skip_group_check=True)
                        return ins
                    P.add("pe", fPV, r=[("P", pb_i)] + list(kb["keys"]), w=[BK(0), BK(1), BK(2)])
                if inter is not None and i >= 2:
                    for _ in range(per_tile):
                        next(inter, None)
            for bk, (h0, nh) in enumerate([(0, 3), (3, 3), (6, 2)]):
                ov = psf[0:nq, bk, 0:nh * OW].rearrange("p (h c) -> p h c", c=OW)
                P.add("dve", lambda ov=ov, h0=h0, nh=nh: nc.vector.reciprocal(
                    out=rl[0:nq, h0:h0 + nh].unsqueeze(2), in_=ov[:, :, 128:129]), r=[BK(bk)], w=[("rl", bk)])
                P.add("dve", lambda ov=ov, h0=h0, nh=nh: nc.vector.tensor_tensor(
                    out=att_tm[0:nq, h0:h0 + nh, :], in0=ov[:, :, 0:128],
                    in1=rl[0:nq, h0:h0 + nh].unsqueeze(2).to_broadcast([nq, nh, 128]), op=ALU.mult),
                    r=[BK(bk), ("rl", bk)], w=[("att_tm", bk)])
            transposes(lambda h: att_tm[:, h, :], 8, nq, 7, stash_dst, stash_key,
                       [("att_tm", 0), ("att_tm", 1), ("att_tm", 2)])

        def q_stage(x_src, nt, cos2, sin2, ckv_dst, kr_dst, W, qb, sample_kv=None):
            xb = W["xb"]; xT = W["xT"]; QlatT = W["QlatT"][qb]; QropeT = W["QropeT"][qb]
            tag = ("q", qb)
            dma("pool", xb[0:nt], x_src, w=["q_xb"])
            transposes(lambda k: xb[:, k * 128:(k + 1) * 128], 8, nt, 7, xT[:, :, 0:nt], "q_xT", ["q_xb"])
            yield
            def fz():
                ins = None
                for k in range(KC):
                    ins = nc.tensor.matmul(psf[0:nt, 6, 0:416], lhsT=xT[:, k, 0:nt], rhs=w_in_a[:, k, :],
                                           start=(k == 0), stop=(k == KC - 1))
                return ins
            P.add("pe", fz, r=["q_xT", RES], w=[BK(6)])
            yield
            if QSTOP < 1:
                return
            ss = small[0:nt, 0:2]
            P.add("act", lambda: nc.scalar.activation(out=junk[0:nt, 0:256], in_=psf[0:nt, 6, 0:256], func=AF.Square,
                                                      accum_out=small[0:nt, 0:1]), r=[BK(6)], w=["q_ss0", "junk"])
            yield
            P.add("act", lambda: nc.scalar.activation(out=junk[0:nt, 0:128], in_=psf[0:nt, 6, 256:384], func=AF.Square,
                                                      accum_out=small[0:nt, 1:2]), r=[BK(6)], w=["q_ss1", "junk"])
            yield
            P.add("dve", lambda: nc.vector.tensor_scalar(out=small[0:nt, 0:1], in0=small[0:nt, 0:1], scalar1=1.0 / 256,
                                                         scalar2=RMS_EPS, op0=ALU.mult, op1=ALU.add),
                  r=["q_ss0"], w=["q_ss0"])
            yield
            P.add("dve", lambda: nc.vector.tensor_scalar(out=small[0:nt, 1:2], in0=small[0:nt, 1:2], scalar1=1.0 / 128,
                                                         scalar2=RMS_EPS, op0=ALU.mult, op1=ALU.add),
                  r=["q_ss1"], w=["q_ss1"])
            yield
            P.add("pool", lambda: nc.gpsimd.tensor_tensor(out=ss, in0=ss, in1=mhalf[0:nt, 0:2], op=ALU.pow),
                  r=["q_ss0", "q_ss1", RES], w=["q_ss0", "q_ss1"])
            yield
            if QSTOP < 2:
                return
            zqn = W["zqn"]
            P.add("dve", lambda: nc.vector.scalar_tensor_tensor(out=zqn[0:nt], in0=psf[0:nt, 6, 0:256], scalar=small[0:nt, 0:1],
                                                                in1=gq_bc[0:nt], op0=ALU.mult, op1=ALU.mult),
                  r=[BK(6), "q_ss0", RES], w=["q_zqn"])
            yield
            ckv32 = W["ckv32"]
            P.add("dve", lambda: nc.vector.scalar_tensor_tensor(out=ckv32[0:nt], in0=psf[0:nt, 6, 256:384], scalar=small[0:nt, 1:2],
                                                                in1=gkv_bc[0:nt], op0=ALU.mult, op1=ALU.mult),
                  r=[BK(6), "q_ss1", RES], w=["q_ckv32"])
            yield
            dma("sp", ckv_dst, ckv32[0:nt], r=["q_ckv32"])
            yield
            if QSTOP < 3:
                return
            kr32 = W["kr32"]
            rope_ops(psf[0:nt, 6, 384:416].unsqueeze(1), cos2.unsqueeze(1), sin2.unsqueeze(1), kr32[0:nt].unsqueeze(1), nt,
                     [BK(6), RES], ["q_kr32"], W["rA"][0:nt, 0:1, :], W["rB"][0:nt, 0:1, :])
            yield
            dma("sp", kr_dst, kr32[0:nt], r=["q_kr32"])
            yield
            if sample_kv is not None:
                sample_kv(ckv32, kr32)
            if QSTOP < 4:
                return
            zqnT = W["zqnT"]
            transposes(lambda k: zqn[:, k * 128:(k + 1) * 128], 2, nt, 7, zqnT[:, :, 0:nt], "q_zqnT", ["q_zqn"])
            yield
            qn = W["qn"]; qr = W["qr"]
            for half in range(2):
                def fq(half=half):
                    ins = None
                    for k in range(2):
                        ins = nc.tensor.matmul(psf[0:nt, 6, 0:384], lhsT=zqnT[:, k, 0:nt],
                                               rhs=w_uq_sb[:, k, half * 384:(half + 1) * 384], start=(k == 0), stop=(k == 1))
                    return ins
                P.add("pe", fq, r=["q_zqnT", RES], w=[BK(6)])
                yield
                qv = psf[0:nt, 6, 0:384].rearrange("p (h c) -> p h c", c=96)
                P.add("dve", lambda qv=qv, half=half: nc.vector.tensor_copy(out=qn[0:nt, half * 4:(half + 1) * 4, :], in_=qv[:, :, 0:64]),
                      r=[BK(6)], w=[("q_qn", half)])
                yield
                rope_ops(qv[:, :, 64:96], cos2.unsqueeze(1).to_broadcast([nt, 4, 32]), sin2.unsqueeze(1).to_broadcast([nt, 4, 32]),
                         W["qr32"][0:nt, half * 4:(half + 1) * 4, :], nt, [BK(6), RES], [("q_qr32", half)],
                         W["rA"][0:nt, 0:4, :], W["rB"][0:nt, 0:4, :])
            if QSTOP < 5:
                return
            P.add("dve", lambda: nc.vector.tensor_copy(out=qr[0:nt], in_=W["qr32"][0:nt].unsqueeze(2).to_broadcast([nt, 8, 4, 32])),
                  r=[("q_qr32", 0), ("q_qr32", 1)], w=["q_qr"])
            yield
            if QSTOP < 6:
                return
            qnT = W["qnT"]
            transposes(lambda j: qn[:, 2 * j:2 * j + 2, :].rearrange("p a b -> p (a b)"), 4, nt, 7, qnT[:, :, 0:nt], "q_qnT",
                       [("q_qn", 0), ("q_qn", 1)])
            yield
            if QSTOP < 7:
                return
            for half in range(2):
                def fl(half=half):
                    ins = None
                    for hh in range(4):
                        h = half * 4 + hh
                        pb = (h % 2) * 64
                        ins = nc.tensor.matmul(psf[:, 6, hh * nt:(hh + 1) * nt], lhsT=w_ukT_sb[:, h, :],
                                               rhs=qnT[:, h // 2, 0:nt], start=True, stop=True)
                    return ins
                P.add("pe", fl, r=["q_qnT", RES], w=[BK(6)])
                yield
                P.add("dve", lambda half=half: nc.vector.tensor_copy(out=QlatT[:, half * 4 * nt:(half + 1) * 4 * nt],
                                                                     in_=psf[:, 6, 0:4 * nt]),
                      r=[BK(6)], w=[(tag, "QlatT")])
            if QSTOP < 8:
                return
            def ftq():
                ins = None
                for h in range(8):
                    ins = nc.tensor.transpose(psb[:, 7, h * 128:(h + 1) * 128], qr[:, h, :, :].rearrange("p a b -> p (a b)"), ident)
                return ins
            P.add("pe", ftq, r=["q_qr", RES], w=[BK(7)])
            yield
            for m in range(4):
                P.add("dve", lambda m=m: nc.vector.tensor_scalar(
                    out=QropeT[:, m, 0:8 * nt].rearrange("p (h t) -> p h t", h=8),
                    in0=psb[:, 7, :].rearrange("p (h t) -> p h t", h=8)[:, :, 0:nt],
                    scalar1=rowmask[:, m:m + 1], scalar2=None, op0=ALU.mult),
                      r=[BK(7), RES], w=[(tag, "QropeT")])

        wctr = {"n": 0}

        def d_head(NB, nt, x_blk_src, xh_src, W, bank, wu_prefetch=False):
            xT = W["xT"]; xb = W["xb"]
            if wu_prefetch:
                dma("sp", W["wA"][0], wb_in[:, 416:928].rearrange("(k p) n -> p k n", p=128),
                    w=[("d_wA", 0), ("d_wU", 0), ("d_wU", 1), ("d_wD", 0)])
            for i in range(NB):
                dma("pool", xb[0:nt, i % 2, :], x_blk_src(i), w=[("d_xb", i % 2)])
                transposes(lambda k, i=i: xb[:, i % 2, k * 128:(k + 1) * 128], 8, nt, bank, xT[:, :, i * nt:(i + 1) * nt],
                           ("d_xT", i), [("d_xb", i % 2)], evac="act")
            if xh_src is not None:
                xhb = W["xhb"]; xhT = W["xhT"]
                dma("pool", xhb, xh_src, w=["d_xhb"])
                transposes(lambda k: xhb[:, k * 128:(k + 1) * 128], 8, 128, bank, xhT, "d_xhT", ["d_xhb"], evac="act")

        def d_stage(NB, nt, x_blk_src, xh_src, attT, band_sel, y_dst, pool_dst, W, halo_direct=None, head_done=False,
                    next_head=None):
            T = NB * nt
            if DSTOP < -1:
                return
            xT = W["xT"]; xb = W["xb"]
            if not head_done:
                d_head(NB, nt, x_blk_src, xh_src, W, 7)
            xTk = [("d_xT", i) for i in range(NB)]
            if DSTOP < 0:
                return
            uh = W["uh"]
            wu = W["wA"][0]
            if not head_done:
                dma("sp", wu, wb_in[:, 416:928].rearrange("(k p) n -> p k n", p=128), w=[("d_wA", 0)])
            if xh_src is not None:
                xhT = W["xhT"]
                def fuh():
                    ins = None
                    for k in range(KC):
                        ins = nc.tensor.matmul(psf[:, 0, :], lhsT=xhT[:, k, :], rhs=wu[:, k, :], start=(k == 0), stop=(k == KC - 1))
                    return ins
                P.add("pe", fuh, r=["d_xhT", ("d_wA", 0)], w=[BK(0)])
                P.add("act", lambda: nc.scalar.copy(out=uh, in_=psf[:, 0, :]), r=[BK(0)], w=["d_uh"])
            else:
                halo_direct(uh)
            u_tm = W["u_tm"]; dT = W["dT"]; ulast = W["ulast"]
            for i in range(NB):
                bk = 1 + (i % 2)
                def fu(i=i, bk=bk):
                    ins = None
                    for k in range(KC):
                        ins = nc.tensor.matmul(psf[0:nt, bk, :], lhsT=xT[:, k, i * nt:(i + 1) * nt], rhs=wu[:, k, :],
                                               start=(k == 0), stop=(k == KC - 1))
                    return ins
                P.add("pe", fu, r=[("d_xT", i), ("d_wA", 0)], w=[BK(bk)])
                P.add("act", lambda i=i, bk=bk: nc.scalar.copy(out=u_tm[0:nt, i, :], in_=psf[0:nt, bk, :]), r=[BK(bk)],
                      w=[("d_u", i)])
                for (pi, dst) in pool_dst:
                    if pi == i and not os.environ.get("NOPOOLDST"):
                        lo = nt // 2
                        P.add("act", lambda bk=bk, lo=lo: nc.scalar.copy(out=ulast[0:nt, :], in_=psf[0:nt, bk, :]),
                              r=[BK(bk)], w=["d_ulast"])
                        if not os.environ.get("NOPOOLDMA"):
                            dma("sp", dst, ulast[lo:nt, :], r=["d_ulast"], w=["d_ulast_out"])
            if DSTOP < 1:
                return
            bandt = W["band"]; bandH = W["bandH"]
            for i in range(NB):
                bk = 3 + (i % 2)
                def fd(i=i, bk=bk):
                    ins = None
                    for g in range(4):
                        nc.tensor.matmul(psf[:, bk, g * nt:(g + 1) * nt], lhsT=u_tm[:, i, g * 128:(g + 1) * 128],
                                         rhs=bandt[:, band_sel(i), g, 0:nt], start=(g == 0), stop=False,
                                         skip_group_check=True)
                        ins = nc.tensor.matmul(psf[:, bk, g * nt:(g + 1) * nt], lhsT=uh[:, g * 128:(g + 1) * 128],
                                               rhs=bandH[:, i, g, 0:nt], start=False, stop=True,
                                               skip_group_check=True)
                    return ins
                P.add("pe", fd, r=[("d_u", i), "d_uh", "d_band"], w=[BK(bk)])
                P.add("dve", lambda i=i, bk=bk: nc.vector.tensor_copy(
                    out=dT[:, :, i * nt:(i + 1) * nt], in_=psf[:, bk, 0:4 * nt].rearrange("p (g t) -> p g t", g=4)),
                    r=[BK(bk)], w=[("d_dT", i)])
            dTk = [("d_dT", i) for i in range(NB)]
            if DSTOP < 2:
                return
            wuv = W["wuv"]; oT = W["oT"]
            for j in range(4):
                bk = 5 + (j % 2)
                def fo(j=j, bk=bk):
                    nc.tensor.matmul(psf[:, bk, 0:T], lhsT=wuv[:, 2 * j, :], rhs=attT[:, 2 * j, :], start=True, stop=False)
                    return nc.tensor.matmul(psf[:, bk, 0:T], lhsT=wuv[:, 2 * j + 1, :], rhs=attT[:, 2 * j + 1, :], start=False, stop=True)
                P.add("pe", fo, r=["d_attT", "d_wsm"], w=[BK(bk)])
                P.add("dve", lambda j=j, bk=bk: nc.vector.tensor_copy(out=oT[:, j, 0:T], in_=psf[:, bk, 0:T]), r=[BK(bk)],
                      w=[("d_oT", j)])
            oTk = [("d_oT", j) for j in range(4)]
            if DSTOP < 3:
                return
            wau = W["wau"]; wpl = W["wpl"]; mT = W["mT"]
            gsb = W["gsb"]; t1 = W["t1"]; t2 = W["t2"]
            for cc in range(8):
                par = cc % 2
                if cc % 4 == 0:
                    ia, ip = (1, 2) if cc == 0 else (0, 1)
                    wga = W["wA"][ia]
                    wgp = W["wA"][ip]
                    c0 = 928 + cc * 128
                    kga = ("d_wA", ia); kgp = ("d_wA", ip)
                    dma("sp", wga, wb_in[:, c0:c0 + 512].rearrange("(k p) n -> p k n", p=128), w=[kga])
                    dma("sp", wgp, wb_in[:, c0 + 1024:c0 + 1536].rearrange("(k p) n -> p k n", p=128), w=[kgp])
                co = (cc % 4) * 128
                for which, wt, wk, bk in ((0, wga, kga, 0 + par), (1, wgp, kgp, 2 + par)):
                    def fg(wt=wt, bk=bk, co=co):
                        ins = None
                        for k in range(KC):
                            ins = nc.tensor.matmul(psf[:, bk, 0:T], lhsT=wt[:, k, co:co + 128], rhs=xT[:, k, 0:T],
                                                   start=(k == 0), stop=(k == KC - 1))
                        return ins
                    P.add("pe", fg, r=xTk + [wk], w=[BK(bk)])
                    P.add("act", lambda which=which, bk=bk, cc=cc, par=par: nc.scalar.activation(
                        out=gsb[:, which, par, 0:T], in_=psf[:, bk, 0:T], func=AF.Sigmoid,
                        bias=b_gate_sb[:, which * 8 + cc:which * 8 + cc + 1], scale=1.0),
                        r=[BK(bk), RES], w=[("d_g", which, par)])
                bkA = 4 + par
                def fA(cc=cc, bkA=bkA):
                    ins = None
                    for j in range(4):
                        ins = nc.tensor.matmul(psf[:, bkA, 0:T], lhsT=wau[:, j, cc * 128:(cc + 1) * 128], rhs=oT[:, j, 0:T],
                                               start=(j == 0), stop=(j == 3))
                    return ins
                P.add("pe", fA, r=oTk + ["d_wsm"], w=[BK(bkA)])
                bkB = 6 + par
                g = cc // 2
                P.add("pe", lambda cc=cc, bkB=bkB, g=g: nc.tensor.matmul(
                    psf[:, bkB, 0:T], lhsT=wpl[:, g, (cc % 2) * 128:(cc % 2) * 128 + 128], rhs=dT[:, g, 0:T], start=True, stop=True),
                    r=dTk + ["d_wsm"], w=[BK(bkB)])
                P.add("dve", lambda bkA=bkA, par=par: nc.vector.tensor_tensor(out=t1[:, par, 0:T], in0=psf[:, bkA, 0:T],
                                                                              in1=gsb[:, 0, par, 0:T], op=ALU.mult),
                      r=[BK(bkA), ("d_g", 0, par)], w=[("d_t1", par)])
                P.add("dve", lambda bkB=bkB, par=par, cc=cc: nc.vector.scalar_tensor_tensor(
                    out=t2[:, par, 0:T], in0=psf[:, bkB, 0:T], scalar=pool_scale_sb[:, cc:cc + 1], in1=gsb[:, 1, par, 0:T],
                    op0=ALU.mult, op1=ALU.mult), r=[BK(bkB), ("d_g", 1, par), RES], w=[("d_t2", par)])
                P.add("dve", lambda par=par, cc=cc: nc.vector.tensor_tensor(out=mT[:, cc, 0:T], in0=t1[:, par, 0:T],
                                                                            in1=t2[:, par, 0:T], op=ALU.add),
                      r=[("d_t1", par), ("d_t2", par)], w=[("d_mT", cc)])
            mTk = [("d_mT", cc) for cc in range(8)]
            if DSTOP < 4:
                return
            wo = [W["wA"][2], W["wA"][0]]
            wok = [("d_wA", 2), ("d_wA", 0)]
            for half in range(2):
                dma("sp", wo[half], wb_o[:, half * 512:(half + 1) * 512].rearrange("(k p) n -> p k n", p=128), w=[wok[half]])
            h32 = W["h32"]; hb = W["hb"]; hT = W["hT"]; x32 = W["x32"]; lnc = W["lnc"]
            stats = W["stats"]; mv = W["mv"]

            def ln_a(i, b0, resid, resid_key, v, out_key):
                st = stats[0:nt, i]; m = mv[0:nt, i]
                P.add("dve", lambda: nc.vector.scalar_tensor_tensor(
                    out=v, in0=resid, scalar=float(ALPHA), in1=psf[0:nt, b0:b0 + 2, :].rearrange("p a b -> p (a b)"),
                    op0=ALU.mult, op1=ALU.add), r=[BK(b0), BK(b0 + 1), resid_key], w=[out_key])
                for c in range(2):
                    P.add("dve", lambda c=c: nc.vector.bn_stats(out=st[:, c, :], in_=v[:, c * 512:(c + 1) * 512]),
                          r=[out_key], w=[("d_stats", i, c)])
                P.add("dve", lambda: nc.vector.bn_aggr(out=m[:, 0:2], in_=st.rearrange("p a b -> p (a b)")),
                      r=[("d_stats", i, 0), ("d_stats", i, 1)], w=[("d_mv", i)])
                P.add("dve", lambda: nc.vector.tensor_scalar(out=m[:, 2:3], in0=m[:, 1:2], scalar1=LN_EPS, scalar2=None,
                                                             op0=ALU.add), r=[("d_mv", i)], w=[("d_mv2", i)])
                P.add("pool", lambda: nc.gpsimd.tensor_tensor(out=m[:, 3:4], in0=m[:, 2:3], in1=mhalf[0:nt, 0:1], op=ALU.pow),
                      r=[("d_mv2", i), RES], w=[("d_rstd", i)])

            def ln_b(i, gi, v, out_key):
                m = mv[0:nt, i]
                P.add("dve", lambda: nc.vector.tensor_scalar(out=m[:, 4:5], in0=m[:, 0:1], scalar1=m[:, 3:4], scalar2=-1.0,
                                                             op0=ALU.mult, op1=ALU.mult), r=[("d_mv", i), ("d_rstd", i)], w=[("d_nmr", i)])
                P.add("act", lambda: nc.scalar.activation(out=v, in_=v, func=AF.Identity, bias=m[:, 4:5], scale=m[:, 3:4]),
                      r=[out_key, ("d_nmr", i), ("d_rstd", i)], w=[out_key])
                P.add("dve", lambda: nc.vector.tensor_tensor(out=v, in0=v, in1=lnc[0:nt, gi, :], op=ALU.mult),
                      r=[out_key, "d_lnc"], w=[out_key])
                P.add("dve", lambda: nc.vector.tensor_tensor(out=v, in0=v, in1=lnc[0:nt, gi + 1, :], op=ALU.add),
                      r=[out_key, "d_lnc"], w=[out_key])

            def emit_fh(i):
                b0 = 2 * (i % 3)
                def fh(i=i, b0=b0):
                    ins = None
                    for half in range(2):
                        for cc in range(8):
                            ins = nc.tensor.matmul(psf[0:nt, b0 + half, :], lhsT=mT[:, cc, i * nt:(i + 1) * nt],
                                                   rhs=wo[half][:, cc, :], start=(cc == 0), stop=(cc == 7))
                    return ins
                P.add("pe", fh, r=mTk + wok, w=[BK(b0), BK(b0 + 1)])

            def emit_ln1a(i):
                b0 = 2 * (i % 3)
                dma("sp", x32[0:nt, i % 2, :], x_blk_src(i), w=[("d_x32", i % 2)])
                ln_a(i, b0, x32[0:nt, i % 2, :], ("d_x32", i % 2), h32[0:nt, i, :], ("d_h32", i))

            def emit_ln1b(i):
                ln_b(i, 0, h32[0:nt, i, :], ("d_h32", i))
                P.add("act", lambda i=i: nc.scalar.copy(out=hb[0:nt, i % 2, :], in_=h32[0:nt, i, :]), r=[("d_h32", i)],
                      w=[("d_hb", i % 2)])

            def emit_tr(i):
                transposes(lambda k, i=i: hb[:, i % 2, k * 128:(k + 1) * 128], 8, nt, 7, hT[:, :, i * nt:(i + 1) * nt],
                           ("d_hT", i), [("d_hb", i % 2)], evac="act")

            emit_fh(0)
            if NB > 1:
                emit_fh(1)
            for i in range(NB + 2):
                if i < NB:
                    emit_ln1a(i)
                if i + 2 < NB:
                    emit_fh(i + 2)
                if 1 <= i <= NB:
                    emit_ln1b(i - 1)
                if 2 <= i:
                    emit_tr(i - 2)
            hTk = [("d_hT", i) for i in range(NB)]
            P.barrier()
            if DSTOP < 5:
                return
            aT = W["aT"]; sl = W["sl"]
            for ft in range(D_FF // 256):
                wi = ft % 2
                wg = W["wG"][wi]; wu_ = W["wU"][wi]
                dma("sp", wg, wb_gate[:, ft * 256:(ft + 1) * 256].rearrange("(k p) n -> p k n", p=128), w=[("d_wG", wi)])
                dma("sp", wu_, wb_up[:, ft * 256:(ft + 1) * 256].rearrange("(k p) n -> p k n", p=128), w=[("d_wU", wi)])
                for sub in range(2):
                    fc = ft * 2 + sub
                    par = fc % 2
                    for which, wt, wk, bk in ((0, wg, ("d_wG", wi), 0 + par), (1, wu_, ("d_wU", wi), 2 + par)):
                        def fgu(wt=wt, bk=bk, sub=sub):
                            ins = None
                            for k in range(KC):
                                ins = nc.tensor.matmul(psf[:, bk, 0:T], lhsT=wt[:, k, sub * 128:(sub + 1) * 128], rhs=hT[:, k, 0:T],
                                                       start=(k == 0), stop=(k == KC - 1))
                            return ins
                        P.add("pe", fgu, r=hTk + [wk], w=[BK(bk)])
                    P.add("act", lambda par=par: nc.scalar.activation(out=sl[:, par, 0:T], in_=psf[:, 0 + par, 0:T], func=AF.Silu),
                          r=[BK(0 + par)], w=[("d_sl", par)])
                    P.add("dve", lambda par=par, fc=fc: nc.vector.tensor_tensor(out=aT[:, fc, 0:T], in0=psf[:, 2 + par, 0:T],
                                                                                in1=sl[:, par, 0:T], op=ALU.mult),
                          r=[BK(2 + par), ("d_sl", par)], w=[("d_aT", fc)])
            if DSTOP < 6:
                return
            for fc in range(NFC):
                wi = fc % 3
                wd = W["wD"][wi]
                dma("sp", wd, wb_down[fc * 128:(fc + 1) * 128, :], w=[("d_wD", wi)])
                def fdn(fc=fc, wd=wd):
                    ins = None
                    for i in range(NB):
                        for half in range(2):
                            ins = nc.tensor.matmul(psf[0:nt, 2 * i + half, :], lhsT=aT[:, fc, i * nt:(i + 1) * nt],
                                                   rhs=wd[:, half * 512:(half + 1) * 512], start=(fc == 0), stop=(fc == NFC - 1))
                    return ins
                P.add("pe", fdn, r=[("d_aT", fc), ("d_wD", wi)], w=[BK(j) for j in range(2 * NB)])
            if DSTOP < 7:
                return
            for i in range(NB):
                ln_a(i, 2 * i, h32[0:nt, i, :], ("d_h32", i), h32[0:nt, i, :], ("d_h32", i))
                if i == 0 and next_head is not None:
                    next_head(0)
            for i in range(NB):
                ln_b(i, 2, h32[0:nt, i, :], ("d_h32", i))
                dma("sp", y_dst(i), h32[0:nt, i, :], r=[("d_h32", i)], w=[("d_yout", i)])
            P.barrier()

        def d_alloc(NB, nt):
            T = NB * nt
            W = {}
            W["band"] = ar.alloc([2, 4, 128], BF16)
            W["bandH"] = ar.alloc([4, 4, 128], BF16)
            W["wuv"] = ar.alloc([8, 128], BF16)
            W["wau"] = ar.alloc([4, 1024], BF16)
            W["wpl"] = ar.alloc([4, 256], BF16)
            W["lnc"] = ar.alloc([4, 1024], F32)
            W["h32"] = ar.alloc([NB, 1024], F32)
            W["hT"] = ar.alloc([KC, T], BF16)
            W["x32"] = ar.alloc([2, 1024], F32)
            W["stats"] = ar.alloc([NB, 2, 6], F32)
            W["mv"] = ar.alloc([NB, 8], F32)
            W["xb"] = ar.alloc([2, 1024], BF16)
            W["xT"] = ar.alloc([KC, T], BF16)
            W["xhb"] = ar.alloc([1024], BF16)
            W["xhT"] = ar.alloc([KC, 128], BF16)
            mark = ar.off
            W["uh"] = ar.alloc([512], BF16)
            W["u_tm"] = ar.alloc([NB, 512], BF16)
            W["ulast"] = ar.alloc([512], F32)
            W["dT"] = ar.alloc([4, T], BF16)
            W["oT"] = ar.alloc([4, T], BF16)
            W["mT"] = ar.alloc([8, T], BF16)
            W["gsb"] = ar.alloc([2, 2, T], BF16)
            W["t1"] = ar.alloc([2, T], F32)
            W["t2"] = ar.alloc([2, T], F32)
            W["wA"] = [ar.alloc([KC, 512], BF16) for _ in range(3)]
            W["hb"] = ar.alloc([2, 1024], BF16)
            end1 = ar.off
            ar.off = mark
            W["aT"] = ar.alloc([NFC, T], BF16)
            W["sl"] = ar.alloc([2, T], F32)
            W["wG"] = [ar.alloc([KC, 256], BF16) for _ in range(2)]
            W["wU"] = [ar.alloc([KC, 256], BF16) for _ in range(2)]
            W["wD"] = [ar.alloc([1024], BF16) for _ in range(3)]
            ar.off = max(ar.off, end1)
            P.add("dve", lambda: nc.vector.memset(W["u_tm"], 0.0), w=[("d_u", i) for i in range(NB)])
            P.add("dve", lambda: nc.vector.memset(W["xb"], 0.0), w=[("d_xb", 0), ("d_xb", 1)])
            P.add("dve", lambda: nc.vector.memset(W["hb"], 0.0), w=[("d_hb", 0), ("d_hb", 1)])
            if DSTOP >= -2:
                dma("sp", W["band"], band_d.rearrange("f p g t -> p f g t"), w=["d_band"])
                dma("sp", W["bandH"], bandH_d, w=["d_band"])
            if DSTOP >= -3:
                dma("pool", W["wuv"], w_uvp, w=["d_wsm"])
            if DSTOP >= -4:
                dma("sp", W["wau"], wb_attn_up.rearrange("(j p) n -> p j n", p=128), r=["wscratch"], w=["d_wsm"])
                dma("sp", W["wpl"], wb_pool.rearrange("g p n -> p g n"), r=["wscratch"], w=["d_wsm"])
            if DSTOP >= -5:
                dma("sp", W["lnc"], lnv.rearrange("(o a) n -> o a n", o=1).to_broadcast([128, 4, 1024]), w=["d_lnc"])
            return W

        def qa_alloc():
            W = {}
            W["xb"] = ar.alloc([1024], BF16)
            W["xT"] = ar.alloc([KC, 128], BF16)
            W["zqn"] = ar.alloc([256], BF16)
            W["zqnT"] = ar.alloc([2, 128], BF16)
            W["ckv32"] = ar.alloc([128], F32)
            W["kr32"] = ar.alloc([32], F32)
            W["rA"] = ar.alloc([4, 32], F32)
            W["rB"] = ar.alloc([4, 32], F32)
            W["qn"] = ar.alloc([8, 64], BF16)
            W["qr32"] = ar.alloc([8, 32], F32)
            W["qr"] = ar.alloc([8, 4, 32], BF16)
            W["qnT"] = ar.alloc([4, 128], BF16)
            W["QlatT"] = [ar.alloc([1024], BF16) for _ in range(2)]
            W["QropeT"] = [ar.alloc([4, 1024], BF16) for _ in range(2)]
            W["Pbuf"] = ar.alloc([NPB, 512], BF16)
            W["att_tm"] = ar.alloc([8, 128], BF16)
            W["rl"] = ar.alloc([8], F32)
            P.add("dve", lambda: nc.vector.memset(W["Pbuf"], 0.0), w=[("P", i) for i in range(NPB)])
            P.add("dve", lambda: nc.vector.memset(W["att_tm"], 0.0), w=[("att_tm", i) for i in range(3)])
            P.add("dve", lambda: nc.vector.memset(W["qr"], 0.0), w=["q_qr"])
            P.add("dve", lambda: nc.vector.memset(W["qn"], 0.0), w=[("q_qn", 0), ("q_qn", 1)])
            P.add("dve", lambda: nc.vector.memset(W["zqn"], 0.0), w=["q_zqn"])
            return W

        attT_stash = ar.alloc([8, NG * 128], BF16)
        phase_mark = ar.off
        P.barrier()

        def prompt_batch(b):
            ar.off = phase_mark
            ckvT = ar.alloc([SEQ], BF16)
            Vx = ar.alloc([NKB, OW], BF16)
            krT = ar.alloc([NKB // 4 * 128], BF16)
            kv_mark = ar.off
            NXB = 3
            xkb = [ar.alloc([4, 1024], BF16) for _ in range(NXB)]
            xkT = [ar.alloc([KC, 128], BF16) for _ in range(2)]
            tabk = [ar.alloc([2, 4, 32], F32) for _ in range(2)]
            sskb = [ar.alloc([4], F32) for _ in range(2)]
            rAk = ar.alloc([4, 32], F32)
            rBk = ar.alloc([4, 32], F32)
            krtm = ar.alloc([4, 32], BF16)
            P.add("dve", lambda: nc.vector.memset(Vx[:, :, 128:130], 1.0), w=["Vones"])

            def k_load(t):
                xk = xkb[t % NXB]
                dma("pool", xk, x_all[b, t * 512:(t + 1) * 512, :].rearrange("(j p) d -> p j d", p=128), w=[("k_x", t % NXB)])

            def k_head(t):
                bi = t % 2
                xk = xkb[t % NXB]
                ssk = sskb[bi]
                dma("sp", tabk[bi], tabK[:, :, 4 * t:4 * t + 4, :].rearrange("c p j e -> p c j e"), w=[("k_tab", bi)])
                zb0 = 2 * bi

                def emit_T(j):
                    xi = (4 * t + j) % 2
                    transposes(lambda k, xk=xk, j=j: xk[:, j, k * 128:(k + 1) * 128], 8, 128, 4 + xi, xkT[xi], ("k_xT", xi),
                               [("k_x", t % NXB)], evac="act")

                def emit_M(j):
                    xi = (4 * t + j) % 2
                    zbk = zb0 + j % 2
                    zc = (j // 2) * 256
                    def fzk(xi=xi, zbk=zbk, zc=zc):
                        ins = None
                        for k in range(KC):
                            ins = nc.tensor.matmul(psf[:, zbk, zc:zc + 160], lhsT=xkT[xi][:, k, :], rhs=w_in_a[:, k, 256:416],
                                                   start=(k == 0), stop=(k == KC - 1), skip_group_check=True)
                        return ins
                    P.add("pe", fzk, r=[("k_xT", xi), RES], w=[("k_z", bi, j), BK(zbk)])
                    jc = (bi * 4 + j) * 128
                    P.add("act", lambda zbk=zbk, zc=zc, j=j, jc=jc, ssk=ssk: nc.scalar.activation(
                        out=junk[:, jc:jc + 128], in_=psf[:, zbk, zc:zc + 128], func=AF.Square, accum_out=ssk[:, j:j + 1]),
                        r=[("k_z", bi, j), BK(zbk)], w=[("k_ss", bi, j), ("junk", bi, j)])

                emit_T(0)
                emit_T(1)
                emit_M(0)
                emit_T(2)
                emit_M(1)
                emit_T(3)
                emit_M(2)
                emit_M(3)

            def k_tail(t):
                bi = t % 2
                ssk = sskb[bi]
                zb0 = 2 * bi
                ssk_keys = [("k_ss", bi, j) for j in range(4)]
                P.add("dve", lambda: nc.vector.tensor_scalar(out=ssk, in0=ssk, scalar1=1.0 / 128, scalar2=RMS_EPS,
                                                             op0=ALU.mult, op1=ALU.add), r=ssk_keys, w=ssk_keys)
                P.add("pool", lambda: nc.gpsimd.tensor_tensor(out=ssk, in0=ssk, in1=mhalf[:, 0:4], op=ALU.pow),
                      r=ssk_keys + [RES], w=ssk_keys)
                for j in range(4):
                    kb = 4 * t + j
                    zbk = zb0 + j % 2
                    zc = (j // 2) * 256
                    P.add("dve", lambda zbk=zbk, zc=zc, kb=kb, j=j: nc.vector.scalar_tensor_tensor(
                        out=Vx[:, kb, 0:128], in0=psf[:, zbk, zc:zc + 128], scalar=ssk[:, j:j + 1], in1=gkv_bc,
                        op0=ALU.mult, op1=ALU.mult), r=[("k_z", bi, j), BK(zbk), ("k_ss", bi, j), RES], w=[("V", kb)])
                for a in range(2):
                    zr = psf[:, zb0:zb0 + 2, a * 256 + 128:a * 256 + 160]
                    rope_ops(zr, tabk[bi][:, 0, 2 * a:2 * a + 2, :], tabk[bi][:, 1, 2 * a:2 * a + 2, :], krtm[:, 2 * a:2 * a + 2, :], 128,
                             [("k_z", bi, 2 * a), ("k_z", bi, 2 * a + 1), BK(zb0), BK(zb0 + 1), ("k_tab", bi)],
                             [("k_krtm", a)], rAk[:, 2 * a:2 * a + 2, :], rBk[:, 2 * a:2 * a + 2, :])
                def ftr():
                    ins = None
                    for j in range(4):
                        ins = nc.tensor.transpose(psb[:, 6, j * 128:(j + 1) * 128], Vx[:, 4 * t + j, 0:128], ident)
                    return ins
                P.add("pe", ftr, r=[("V", 4 * t + j) for j in range(4)] + [RES], w=[BK(6)])
                P.add("dve", lambda: nc.vector.tensor_copy(out=ckvT[:, t * 512:(t + 1) * 512], in_=psb[:, 6, 0:512]),
                      r=[BK(6)], w=[("ckvT", t)])
                P.add("pe", lambda: nc.tensor.transpose(psb[:, 7, 0:128], krtm.rearrange("p a b -> p (a b)"), ident),
                      r=[("k_krtm", 0), ("k_krtm", 1), RES], w=[BK(7)])
                P.add("dve", lambda: nc.vector.tensor_copy(out=krT[:, t * 128:(t + 1) * 128], in_=psb[:, 7, 0:128]),
                      r=[BK(7)], w=[("krT", t)])

            k_load(0)
            if NSTEP > 1:
                k_load(1)
            for t in range(NSTEP + 1):
                if t + 2 < NSTEP:
                    k_load(t + 2)
                if t < NSTEP:
                    k_head(t)
                if t >= 1:
                    k_tail(t - 1)
            P.barrier()
            casts = list(cast_list) if b == 0 else []
            if STOP < 2:
                for c in casts:
                    c()
                return
            ar.off = kv_mark
            W = qa_alloc()
            def mk_q(s):
                return q_stage(x_own[b, s], 128, tabO_sb[:, 0, s, :], tabO_sb[:, 1, s, :], ckv_own[b, s], kr_own[b, s], W, s % 2)
            order = list(range(NG - 1, -1, -1))
            for _ in mk_q(order[0]):
                pass
            for oi, s in enumerate(order):
                qb = s % 2
                groups = []
                for g in range(2):
                    groups.append(dict(lat=W["QlatT"][qb][:, g * 512:(g + 1) * 512],
                                       rope=(lambda pb, g=g, qb=qb: W["QropeT"][qb][:, pb // 32, g * 512:(g + 1) * 512]),
                                       heads=[(g * 4 + hl, hl * 128) for hl in range(4)], N=512))
                kbs = []
                for kb in range(8 * s + 8):
                    pb = 32 * (kb % 4)
                    c0 = (kb // 4) * 128
                    ing = kb >= 8 * s
                    kbs.append(dict(ckvT=ckvT[:, kb * 128:(kb + 1) * 128], krT=krT[:, c0:c0 + 128], pb=pb,
                                    V=Vx[:, kb, 0:129], nk=128,
                                    mask=(maskK[:, (kb - 8 * s) * 128:(kb - 8 * s + 1) * 128] if ing else None), keys=[]))
                nxt = mk_q(order[oi + 1]) if (oi + 1 < NG and not os.environ.get("NOINTER")) else None
                if QSTOP >= 9:
                    attention(128, groups, kbs, W["Pbuf"], W["att_tm"], W["rl"],
                              attT_stash[:, :, s * 128:(s + 1) * 128], ("stash", s), [(("q", qb), "QlatT"), (("q", qb), "QropeT")],
                              inter=nxt, inter_n=24)
                if casts:
                    casts.pop(0)()
                if nxt is not None:
                    for _ in nxt:
                        pass
                elif oi + 1 < NG:
                    for _ in mk_q(order[oi + 1]):
                        pass
            for c in casts:
                c()
            P.barrier()
            if STOP < 3:
                return
            ar.off = phase_mark
            NB = min(4, NG)
            Wd = d_alloc(NB, 128)
            ngr = NG // NB
            def xsrc(gi):
                return lambda i: x_own[b, gi * NB + i]
            def xhsrc(gi):
                return x_halo[b, gi * NB:(gi + 1) * NB].rearrange("s r d -> (s r) d")
            for gi in range(ngr):
                pool_dst = [(NB - 1, pool_p[b])] if gi == ngr - 1 else []
                nh = None
                if gi + 1 < ngr and not os.environ.get("NOPREHEAD"):
                    nh = (lambda bank, gi=gi: d_head(NB, 128, xsrc(gi + 1), xhsrc(gi + 1), Wd, bank, wu_prefetch=True))
                d_stage(NB, 128, xsrc(gi), xhsrc(gi),
                        attT_stash[:, :, gi * NB * 128:(gi + 1) * NB * 128],
                        (lambda i, gi=gi: 0 if (gi == 0 and i == 0) else 1),
                        lambda i, gi=gi: y_own[b, gi * NB + i], pool_dst, Wd,
                        head_done=(gi > 0 and not os.environ.get("NOPREHEAD")), next_head=nh)
            P.barrier()

        for b in range(2):
            if STOP >= 1:
                prompt_batch(b)

        def sample_phase():
            ar.off = phase_mark
            ckvT = ar.alloc([1088], BF16)
            Vx = ar.alloc([9, OW], BF16)
            krT = ar.alloc([3 * 128], BF16)
            krc = ar.alloc([8, 32], BF16)
            ckvn = ar.alloc([128], BF16)
            krn = ar.alloc([128], BF16)
            W = qa_alloc()
            P.add("dve", lambda: nc.vector.memset(Vx[:, :, 128:130], 1.0), w=["Vones"])
            P.add("dve", lambda: nc.vector.memset(krT[:, 256:384], 0.0), w=["s_krTn"])
            P.add("dve", lambda: nc.vector.memset(krn, 0.0), w=["s_krn"])
            P.add("dve", lambda: nc.vector.memset(Vx[64:128, 8, :], 0.0), w=["s_Vn", "Vones"])
            for e in range(4):
                dma("pool", Vx[:, 0:8, 0:128], ckv_cache[e].rearrange("(k p) r -> p k r", p=128), w=["s_V"])
                dma("pool", krc, kr_cache[e].rearrange("(k p) r -> p k r", p=128), w=["s_krc"])
                transposes(lambda k: Vx[:, k, 0:128], 8, 128, 7, ckvT[:, 0:1024].rearrange("p (k t) -> p k t", k=8), "s_ckvT", ["s_V"])
                transposes(lambda k: krc[:, 4 * k:4 * k + 4, :].rearrange("p a b -> p (a b)"), 2, 128, 7,
                           krT[:, 0:256].rearrange("p (k t) -> p k t", k=2), "s_krT", ["s_krc"])

                def sample_kv(ckv32, kr32):
                    P.add("dve", lambda: nc.vector.tensor_copy(out=Vx[0:64, 8, 0:128], in_=ckv32[0:64]), r=["q_ckv32"], w=["s_Vn"])
                    P.add("dve", lambda: nc.vector.tensor_copy(out=krn[0:64, 0:32], in_=kr32[0:64]), r=["q_kr32"], w=["s_krn"])
                    transposes(lambda k: Vx[:, 8, 0:128], 1, 64, 7, ckvT[:, 1024:1088].unsqueeze(1), "s_ckvTn", ["s_Vn"])
                    P.add("pe", lambda: nc.tensor.transpose(psb[:, 7, 0:128], krn, ident), r=["s_krn", RES],
                          w=[BK(7)])
                    P.add("dve", lambda: nc.vector.tensor_copy(out=krT[0:32, 256:320], in_=psb[0:32, 7, 0:64]), r=[BK(7)], w=["s_krTn"])

                for _ in q_stage(x_s[e], 64, tabS_sb[0:64, 0, :], tabS_sb[0:64, 1, :], ckv_s[e], kr_s[e], W, 0, sample_kv=sample_kv):
                    pass
                groups = [dict(lat=W["QlatT"][0][:, 0:512], rope=(lambda pb: W["QropeT"][0][:, pb // 32, 0:512]),
                               heads=[(h, h * 64) for h in range(8)], N=512)]
                kbs = []
                for kb in range(8):
                    pb = 32 * (kb % 4)
                    c0 = (kb // 4) * 128
                    kbs.append(dict(ckvT=ckvT[:, kb * 128:(kb + 1) * 128], krT=krT[:, c0:c0 + 128], pb=pb,
                                    V=Vx[:, kb, 0:129], nk=128, mask=None, keys=["s_ckvT", "s_krT", "s_V"]))
                kbs.append(dict(ckvT=ckvT[:, 1024:1088], krT=krT[:, 256:320], pb=0, V=Vx[:, 8, 0:129], nk=64, mask=None,
                                keys=["s_ckvTn", "s_krTn", "s_Vn"]))
                attention(64, groups, kbs, W["Pbuf"], W["att_tm"], W["rl"],
                          attT_stash[:, :, e * 64:(e + 1) * 64], ("stash", e), [(("q", 0), "QlatT"), (("q", 0), "QropeT")])
            P.barrier()
            ar.off = phase_mark
            Wd = d_alloc(4, 64)

            def halo_direct(uh):
                P.add("dve", lambda: nc.vector.memset(uh, 0.0), w=["d_uh0"])
                for e in range(4):
                    dma("pool", uh[32 * e + 17:32 * e + 32, :], pool_state[e], r=["d_uh0"], w=["d_uh"])

            d_stage(4, 64, lambda i: x_s[i], None, attT_stash[:, :, 0:256], lambda i: 1, lambda i: y_s[i],
                    [(i, pool_s[i]) for i in range(4)], Wd, halo_direct=halo_direct)
            P.barrier()
        if STOP >= 4:
            sample_phase()
        nops, nwait = P.finalize()
    return nc, (nops, nwait)


_CACHE = {}


def _rope_tables(pos):
    half = QK_ROPE // 2
    inv = (np.float32(1.0) / (np.float32(10000.0) ** (np.arange(half, dtype=np.float32) / np.float32(half)))).astype(np.float32)
    ang = (pos.astype(np.float32)[:, None] * inv[None, :]).astype(np.float32)
    c = np.cos(ang).astype(np.float32)
    s = np.sin(ang).astype(np.float32)
    cos2 = np.concatenate([c, c], axis=-1)
    sin2 = np.concatenate([-s, s], axis=-1)
    return cos2, sin2


def _bands():
    wins = (2, 4, 8, 16)
    band = np.zeros((2, 128, 4, 128), np.float32)
    bandH = np.zeros((32, 4, 128), np.float32)
    tp = np.arange(128)[:, None]
    t = np.arange(128)[None, :]
    r = np.arange(32)[:, None] - 32
    for g, w in enumerate(wins):
        inwin = ((tp <= t) & (tp >= t - w + 1)).astype(np.float32)
        eye = (tp == t).astype(np.float32)
        band[1, :, g, :] = inwin / w - eye
        cnt = np.minimum(t + 1, w).astype(np.float32)
        band[0, :, g, :] = inwin / cnt - eye
        bandH[:, g, :] = ((r >= t - w + 1).astype(np.float32)) / w
    bandHm = np.zeros((128, 4, 4, 128), np.float32)
    for i in range(4):
        bandHm[32 * i:32 * i + 32, i] = bandH
    return band.astype(ml_dtypes.bfloat16), bandHm.astype(ml_dtypes.bfloat16)


def kernel(x_prompt, x_sample, cache_kv_latent, cache_k_rope, state_pool, w_in, b_gate, q_norm_g, w_uq, kv_norm_g,
           w_uk, w_uv, w_attn_up, w_pool, pool_scale, w_o, ln1_g, ln1_b, w_gate, w_up, w_down, ln2_g, ln2_b):
    f = lambda a: np.ascontiguousarray(np.asarray(a), dtype=np.float32)
    x_prompt = f(x_prompt); x_sample = f(x_sample)
    B, SEQ, _ = x_prompt.shape
    NG = SEQ // 1024
    assert B == 2 and SEQ % 1024 == 0
    if NG not in _CACHE:
        _CACHE[NG] = build_program(NG)
    nc, _ = _CACHE[NG]

    cache_kv_latent = f(cache_kv_latent); cache_k_rope = f(cache_k_rope); state_pool = f(state_pool)
    w_uk_ = f(w_uk)[0]
    w_ukT = np.zeros((128, 8, 128), np.float32)
    for h in range(8):
        w_ukT[(h % 2) * 64:(h % 2) * 64 + 64, h, :] = w_uk_[:, h, :].T
    w_uv_ = f(w_uv)[0]
    w_uvp = np.zeros((128, 8, 128), np.float32)
    for h in range(8):
        w_uvp[:, h, (h % 2) * 64:(h % 2) * 64 + 64] = w_uv_[:, h, :]
    shared = {
        "x_all": x_prompt,
        "w_in": f(w_in)[0], "w_uq": f(w_uq)[0], "w_ukT": w_ukT, "w_uvp": w_uvp, "w_attn_up": f(w_attn_up)[0],
        "w_pool": f(w_pool)[0], "w_o": f(w_o)[0], "w_gate": f(w_gate)[0], "w_up": f(w_up)[0], "w_down": f(w_down)[0],
        "gq": f(q_norm_g), "gkv": f(kv_norm_g),
        "b_gateT": np.ascontiguousarray(f(b_gate)[0].reshape(16, 128).T),
        "pool_scaleT": np.ascontiguousarray(f(pool_scale)[0].reshape(8, 128).T),
        "lnv": np.ascontiguousarray(np.stack([f(ln1_g)[0], f(ln1_b)[0], f(ln2_g)[0], f(ln2_b)[0]])),
        "ident": np.eye(128, dtype=np.float32).astype(ml_dtypes.bfloat16),
    }
    cosK, sinK = _rope_tables(np.arange(SEQ))
    shared["tabK"] = np.ascontiguousarray(np.stack([cosK, sinK]).reshape(2, SEQ // 128, 128, 32).transpose(0, 2, 1, 3))
    cosS, sinS = _rope_tables(cache_kv_latent.shape[2] + np.arange(64))
    shared["tabS"] = np.ascontiguousarray(np.stack([cosS, sinS]))
    band, bandH = _bands()
    shared["bandH"] = bandH
    mq = np.zeros((128, 512), np.float32)
    mq[0, :] = NEG_BIG
    mq[1, :] = np.where((np.arange(512) % 128) < 64, NEG_BIG, 0.0)
    shared["maskQ"] = mq.astype(ml_dtypes.bfloat16)
    rm = np.zeros((128, 4), np.float32)
    for m_ in range(4):
        rm[32 * m_:32 * m_ + 32, m_] = 1.0
    shared["rowmask"] = rm
    band_generic = band.copy()
    band_generic[0] = band_generic[1]

    in_maps = []
    xp = x_prompt.reshape(2, NG, 8, 128, D_MODEL)
    for i in range(NCORES):
        m = dict(shared)
        m["x_own"] = np.ascontiguousarray(xp[:, :, i])
        halo = np.zeros((2, NG, 32, D_MODEL), np.float32)
        for s in range(NG):
            st = (8 * s + i) * 128
            if st >= 32:
                halo[:, s] = x_prompt[:, st - 32:st]
        m["x_halo"] = halo
        m["x_s"] = np.ascontiguousarray(x_sample[4 * i:4 * i + 4])
        m["ckv_cache"] = np.ascontiguousarray(cache_kv_latent[0, 4 * i:4 * i + 4])
        m["kr_cache"] = np.ascontiguousarray(cache_k_rope[0, 4 * i:4 * i + 4])
        m["pool_state"] = np.ascontiguousarray(state_pool[0, 4 * i:4 * i + 4])
        pos_own = ((8 * np.arange(NG)[:, None] + i) * 128 + np.arange(128)[None, :]).reshape(-1)
        cO, sO = _rope_tables(pos_own)
        m["tabO"] = np.ascontiguousarray(np.stack([cO, sO]).reshape(2, NG, 128, 32).transpose(0, 2, 1, 3))
        mk = np.zeros((128, 1024), np.float32)
        slot = np.arange(1024) // 128
        mk[0] = (slot > i).astype(np.float32)
        mk[1] = ((slot == i) & ((np.arange(1024) % 128) >= 64)).astype(np.float32)
        m["maskK"] = mk.astype(ml_dtypes.bfloat16)
        m["band"] = band if i == 0 else band_generic
        in_maps.append(m)

    res = run_bass_kernel_spmd(nc, in_maps, core_ids=list(range(NCORES)))
    R = res.results
    y_p = np.zeros((2, SEQ, D_MODEL), np.float32).reshape(2, NG, 8, 128, D_MODEL)
    ckv_p = np.zeros((2, NG, 8, 128, 128), np.float32)
    kr_p = np.zeros((2, NG, 8, 128, 32), np.float32)
    for i in range(NCORES):
        y_p[:, :, i] = R[i]["y_own"]
        ckv_p[:, :, i] = R[i]["ckv_own"]
        kr_p[:, :, i] = R[i]["kr_own"]
    y_s = np.concatenate([R[i]["y_s"] for i in range(NCORES)], axis=0)
    ckv_s = np.concatenate([R[i]["ckv_s"] for i in range(NCORES)], axis=0)
    kr_s = np.concatenate([R[i]["kr_s"] for i in range(NCORES)], axis=0)
    pool_s = np.concatenate([R[i]["pool_s"][:, -15:] for i in range(NCORES)], axis=0)
    pool_p = np.asarray(R[NCORES - 1]["pool_p"])[:, -15:]
    return (y_p.reshape(2, SEQ, D_MODEL).astype(np.float32),
            y_s.astype(np.float32),
            ckv_p.reshape(1, 2, SEQ, 128).astype(np.float32),
            kr_p.reshape(1, 2, SEQ, 32).astype(np.float32),
            pool_p.reshape(1, 2, 15, 512).astype(np.float32),
            ckv_s.reshape(1, 32, 64, 128).astype(np.float32),
            kr_s.reshape(1, 32, 64, 32).astype(np.float32),
            pool_s.reshape(1, 32, 15, 512).astype(np.float32))
```
